# Optimizing a Trainium2 kernel written in Bass

```python
import math
import jax, jax.numpy as jnp
from jax import lax
import numpy as np

D_MODEL = 2048
BATCH = 8
SEQ = 4096
DEPTH = 1
DEC_BATCH = 32
DEC_SEQ = 16
PAST_LEN = 4096

CHUNK = 64
Q_BLOCK = 128
A_HEADS = D_MODEL // 256
A_HEAD_DIM = 64
A_V_DIM = 2 * A_HEAD_DIM
A_WIDTH = A_HEADS * A_V_DIM
A_QK_WIDTH = A_HEADS * 2 * A_HEAD_DIM
ROT_DIM = A_HEAD_DIM // 4
ROPE_THETA = 500000.0
B_HEADS = D_MODEL // 256
B_KEY_DIM = 128
B_VAL_DIM = 128
B_KEY_WIDTH = B_HEADS * B_KEY_DIM
B_WIDTH = B_HEADS * B_VAL_DIM
N_MEM = 256
C_HEADS = 4
C_HEAD_DIM = D_MODEL // 8
C_WIDTH = C_HEADS * C_HEAD_DIM
DN_ALPHA = (2 * DEPTH) ** 0.25
DN_BETA = (8 * DEPTH) ** -0.25
NORM_EPS = 1e-5
IN_WIDTHS = (A_QK_WIDTH, A_QK_WIDTH, A_WIDTH, A_WIDTH,
             B_KEY_WIDTH, B_KEY_WIDTH, B_WIDTH, B_WIDTH, B_WIDTH,
             C_WIDTH, C_WIDTH,
             D_MODEL, D_MODEL, D_MODEL)
N_IN = sum(IN_WIDTHS)

kernel_name = 'stream_diffattn_hgrn2_mem_step'


def _rms_norm(x, g):
    xf = x.astype(jnp.float32)
    return xf * lax.rsqrt(jnp.mean(xf * xf, axis=-1, keepdims=True) + NORM_EPS) * g.astype(jnp.float32)


def _layer_norm(x, g, b):
    xf = x.astype(jnp.float32)
    mu = jnp.mean(xf, axis=-1, keepdims=True)
    var = jnp.mean(jnp.square(xf - mu), axis=-1, keepdims=True)
    return (xf - mu) * lax.rsqrt(var + NORM_EPS) * g.astype(jnp.float32) + b.astype(jnp.float32)


def _rope_partial(x, pos):
    inv_freq = jnp.power(ROPE_THETA, -jnp.arange(0, ROT_DIM, 2, dtype=jnp.float32) / ROT_DIM)
    ang = pos.astype(jnp.float32)[:, None] * inv_freq[None, :]
    cos = jnp.cos(ang)[:, None, None, :]
    sin = jnp.sin(ang)[:, None, None, :]
    xr = x[..., :ROT_DIM].astype(jnp.float32)
    x1, x2 = xr[..., :ROT_DIM // 2], xr[..., ROT_DIM // 2:]
    rot = jnp.concatenate([x1 * cos - x2 * sin, x2 * cos + x1 * sin], axis=-1)
    return jnp.concatenate([rot.astype(x.dtype), x[..., ROT_DIM:]], axis=-1)


def _diff_weights(s, lam):
    p = jax.nn.softmax(s, axis=-1)
    return p[:, :, 0] - lam * p[:, :, 1]


def _diff_attn_prompt(q, k, v, lam):
    n_b, seq = q.shape[:2]
    n_blk = seq // Q_BLOCK
    q_blocks = jnp.moveaxis(q.reshape(n_b, n_blk, Q_BLOCK, A_HEADS, 2, A_HEAD_DIM), 1, 0)
    key_chunk = jnp.arange(seq) // CHUNK
    vf = v.astype(jnp.float32)
    scale = A_HEAD_DIM ** -0.5

    def one_block(args):
        qb, start = args
        q_chunk = (start + jnp.arange(Q_BLOCK)) // CHUNK
        s = jnp.einsum('bqhmd,bkhmd->bhmqk', qb, k).astype(jnp.float32) * scale
        allowed = key_chunk[None, :] <= q_chunk[:, None]
        s = jnp.where(allowed, s, -jnp.inf)
        w = _diff_weights(s, lam)
        return jnp.einsum('bhqk,bkhe->bqhe', w, vf)

    o = lax.map(one_block, (q_blocks, jnp.arange(n_blk) * Q_BLOCK))
    return jnp.moveaxis(o, 0, 1).reshape(n_b, seq, A_HEADS, A_V_DIM)


def _diff_attn_sample(q, k_all, v_all, lam):
    s = jnp.einsum('bqhmd,bkhmd->bhmqk', q, k_all).astype(jnp.float32) * (A_HEAD_DIM ** -0.5)
    w = _diff_weights(s, lam)
    return jnp.einsum('bhqk,bkhe->bqhe', w, v_all.astype(jnp.float32))


def _hgrn2_chunkwise(q, k, v, g, s0, blk):
    n_b, t = q.shape[:2]
    n_c = t // blk

    def to_chunks(a):
        return a.reshape(n_b, n_c, blk, B_HEADS, a.shape[-1]).transpose(1, 0, 3, 2, 4)

    qc, kc, vc, gc = to_chunks(q), to_chunks(k), to_chunks(v), to_chunks(g)
    b = jnp.cumsum(gc, axis=3)
    mid = (blk - 1) // 2
    b_mid = b[:, :, :, mid:mid + 1, :]
    b_last = b[:, :, :, -1:, :]
    scores = jnp.einsum('cnhtd,cnhsd->cnhts', qc * jnp.exp(b - b_mid), kc * jnp.exp(b_mid - b))
    causal = jnp.tril(jnp.ones((blk, blk), dtype=bool))
    o_intra = jnp.einsum('cnhts,cnhse->cnhte', jnp.where(causal, scores, 0.0), vc)
    q_from_state = qc * jnp.exp(b)
    k_to_state = kc * jnp.exp(b_last - b)
    chunk_decay = jnp.exp(b_last[:, :, :, 0, :])

    def step(state, inp):
        q_i, k_i, v_i, d_i = inp
        o_i = jnp.einsum('nhtd,nhde->nhte', q_i, state)
        state = state * d_i[..., None] + jnp.einsum('nhtd,nhte->nhde', k_i, v_i)
        return state, o_i

    s_fin, o_inter = lax.scan(step, s0, (q_from_state, k_to_state, vc, chunk_decay))
    o = (o_intra + o_inter).transpose(1, 0, 3, 2, 4).reshape(n_b, t, B_HEADS, v.shape[-1])
    return o, s_fin


def _mem_attn(q, mem_k, mem_v):
    s = jnp.einsum('bthd,bmhd->bhtm', q, mem_k).astype(jnp.float32) * (C_HEAD_DIM ** -0.5)
    p = jax.nn.softmax(s, axis=-1)
    return jnp.einsum('bhtm,bmhd->bthd', p, mem_v.astype(jnp.float32))


def _layer(x, pos, past_k, past_v, hgrn_s0, mem_k, mem_v, rec_block, layer_idx, lower_bound, params):
    (w_in, lq1, lk1, lq2, lk2, sub_norm, hgrn_gain, w_a, w_b, w_c, w_o, ln_g, ln_b) = params
    f32 = jnp.float32
    n_b, t, _ = x.shape
    split_at = [int(c) for c in np.cumsum(IN_WIDTHS)[:-1]]
    (qa, ka, va, za, qb, fb, ib, ogb, zb, qc, zc,
     gate_a, gate_b, gate_c) = jnp.split(x @ w_in, split_at, axis=-1)

    qa = _rope_partial(qa.reshape(n_b, t, A_HEADS, 2, A_HEAD_DIM), pos)
    ka = _rope_partial(ka.reshape(n_b, t, A_HEADS, 2, A_HEAD_DIM), pos)
    va = va.reshape(n_b, t, A_HEADS, A_V_DIM)
    lam_init = 0.8 - 0.6 * math.exp(-0.3 * layer_idx)
    lam = (jnp.exp(jnp.sum(lq1.astype(f32) * lk1.astype(f32)))
           - jnp.exp(jnp.sum(lq2.astype(f32) * lk2.astype(f32))) + lam_init)
    if past_k is None:
        oa = _diff_attn_prompt(qa, ka, va, lam)
    else:
        oa = _diff_attn_sample(qa, jnp.concatenate([past_k, ka], axis=1),
                               jnp.concatenate([past_v, va], axis=1), lam)
    ya = (_rms_norm(oa, sub_norm) * (1.0 - lam_init)).reshape(n_b, t, A_WIDTH) * jax.nn.silu(za.astype(f32))

    fgate = (lower_bound + (1.0 - lower_bound) * jax.nn.sigmoid(fb.astype(f32))).reshape(n_b, t, B_HEADS, B_KEY_DIM)
    qh = jax.nn.silu(qb.astype(f32)).reshape(n_b, t, B_HEADS, B_KEY_DIM)
    vh = ib.astype(f32).reshape(n_b, t, B_HEADS, B_VAL_DIM)
    ob, s_fin = _hgrn2_chunkwise(qh, 1.0 - fgate, vh, jnp.log(fgate), hgrn_s0.astype(f32), rec_block)
    ob = _rms_norm(ob, hgrn_gain.reshape(B_HEADS, B_VAL_DIM)) * jax.nn.sigmoid(
        ogb.astype(f32).reshape(n_b, t, B_HEADS, B_VAL_DIM))
    yb = ob.reshape(n_b, t, B_WIDTH) * jax.nn.silu(zb.astype(f32))

    oc = _mem_attn(qc.reshape(n_b, t, C_HEADS, C_HEAD_DIM), mem_k, mem_v)
    yc = oc.reshape(n_b, t, C_WIDTH) * jax.nn.silu(zc.astype(f32))

    merged = (jax.nn.sigmoid(gate_a.astype(f32)) * (ya @ w_a.astype(f32))
              + jax.nn.sigmoid(gate_b.astype(f32)) * (yb @ w_b.astype(f32))
              + jax.nn.sigmoid(gate_c.astype(f32)) * (yc @ w_c.astype(f32)))
    sub = merged @ w_o.astype(f32)
    y = _layer_norm(DN_ALPHA * x.astype(f32) + sub, ln_g, ln_b).astype(x.dtype)
    return y, ka, va, s_fin.astype(x.dtype)


def setup_inputs(seed: int = 0) -> dict:
    key = jax.random.key(seed)
    ks = jax.random.split(key, 26)

    def nrm(k, shape, s):
        return jax.random.normal(k, shape, jnp.float32) * s

    return {
        'x_prompt': nrm(ks[0], (BATCH, SEQ, D_MODEL), 1.0),
        'x_sample': nrm(ks[1], (DEC_BATCH, DEC_SEQ, D_MODEL), 1.0),
        'cache_attn_k': nrm(ks[2], (DEPTH, DEC_BATCH, PAST_LEN, A_HEADS, 2, A_HEAD_DIM), 1.0),
        'cache_attn_v': nrm(ks[3], (DEPTH, DEC_BATCH, PAST_LEN, A_HEADS, A_V_DIM), 1.0),
        'state_hgrn': nrm(ks[4], (DEPTH, DEC_BATCH, B_HEADS, B_KEY_DIM, B_VAL_DIM), 0.3),
        'cache_mem_k': nrm(ks[5], (DEPTH, DEC_BATCH, N_MEM, C_HEADS, C_HEAD_DIM), 1.0),
        'cache_mem_v': nrm(ks[6], (DEPTH, DEC_BATCH, N_MEM, C_HEADS, C_HEAD_DIM), 1.0),
        'mem_prompt': nrm(ks[7], (BATCH, N_MEM, D_MODEL), 1.0),
        'w_in': nrm(ks[8], (DEPTH, D_MODEL, N_IN), D_MODEL ** -0.5),
        'lambda_q1': nrm(ks[9], (DEPTH, A_HEAD_DIM), 0.1),
        'lambda_k1': nrm(ks[10], (DEPTH, A_HEAD_DIM), 0.1),
        'lambda_q2': nrm(ks[11], (DEPTH, A_HEAD_DIM), 0.1),
        'lambda_k2': nrm(ks[12], (DEPTH, A_HEAD_DIM), 0.1),
        'attn_sub_norm': 1.0 + nrm(ks[13], (DEPTH, A_V_DIM), 0.02),
        'hgrn_lb_logits': nrm(ks[14], (DEPTH + 1, B_KEY_WIDTH), 0.5),
        'hgrn_norm': 1.0 + nrm(ks[15], (DEPTH, B_WIDTH), 0.02),
        'w_mem_k': nrm(ks[16], (DEPTH, D_MODEL, C_WIDTH), D_MODEL ** -0.5),
        'w_mem_v': nrm(ks[17], (DEPTH, D_MODEL, C_WIDTH), D_MODEL ** -0.5),
        'w_branch_a': nrm(ks[18], (DEPTH, A_WIDTH, D_MODEL), A_WIDTH ** -0.5 * DN_BETA),
        'w_branch_b': nrm(ks[19], (DEPTH, B_WIDTH, D_MODEL), B_WIDTH ** -0.5 * DN_BETA),
        'w_branch_c': nrm(ks[20], (DEPTH, C_WIDTH, D_MODEL), C_WIDTH ** -0.5 * DN_BETA),
        'w_out': nrm(ks[21], (DEPTH, D_MODEL, D_MODEL), D_MODEL ** -0.5 * DN_BETA),
        'ln_gamma': 1.0 + nrm(ks[22], (DEPTH, D_MODEL), 0.02),
        'ln_beta': nrm(ks[23], (DEPTH, D_MODEL), 0.02),
    }


def reference(x_prompt, x_sample, cache_attn_k, cache_attn_v, state_hgrn, cache_mem_k, cache_mem_v,
              mem_prompt, w_in, lambda_q1, lambda_k1, lambda_q2, lambda_k2, attn_sub_norm,
              hgrn_lb_logits, hgrn_norm, w_mem_k, w_mem_v, w_branch_a, w_branch_b, w_branch_c,
              w_out, ln_gamma, ln_beta):
    bp, seq = x_prompt.shape[:2]
    t_new = x_sample.shape[1]
    past = cache_attn_k.shape[2]
    pos_prompt = jnp.arange(seq)
    pos_sample = past + jnp.arange(t_new)
    lower_bounds = jnp.cumsum(jax.nn.softmax(hgrn_lb_logits.astype(jnp.float32), axis=0), axis=0)

    h_p, h_s = x_prompt, x_sample
    kp_l, vp_l, sp_l, mkp_l, mvp_l, ks_l, vs_l, ss_l = [], [], [], [], [], [], [], []
    for l in range(DEPTH):
        params = (w_in[l], lambda_q1[l], lambda_k1[l], lambda_q2[l], lambda_k2[l], attn_sub_norm[l],
                  hgrn_norm[l], w_branch_a[l], w_branch_b[l], w_branch_c[l], w_out[l],
                  ln_gamma[l], ln_beta[l])
        mk_p = (mem_prompt @ w_mem_k[l]).reshape(bp, N_MEM, C_HEADS, C_HEAD_DIM)
        mv_p = (mem_prompt @ w_mem_v[l]).reshape(bp, N_MEM, C_HEADS, C_HEAD_DIM)
        s0_p = jnp.zeros((bp, B_HEADS, B_KEY_DIM, B_VAL_DIM), jnp.float32)
        h_p, k_p, v_p, s_p = _layer(h_p, pos_prompt, None, None, s0_p, mk_p, mv_p, CHUNK, l,
                                    lower_bounds[l], params)
        h_s, k_s, v_s, s_s = _layer(h_s, pos_sample, cache_attn_k[l], cache_attn_v[l], state_hgrn[l],
                                    cache_mem_k[l], cache_mem_v[l], t_new, l, lower_bounds[l], params)
        kp_l.append(k_p); vp_l.append(v_p); sp_l.append(s_p); mkp_l.append(mk_p); mvp_l.append(mv_p)
        ks_l.append(k_s); vs_l.append(v_s); ss_l.append(s_s)

    k_prompt = jnp.stack(kp_l)
    v_prompt = jnp.stack(vp_l)
    hgrn_prompt = jnp.stack(sp_l)
    mem_k_prompt = jnp.stack(mkp_l)
    mem_v_prompt = jnp.stack(mvp_l)
    k_sample = jnp.stack(ks_l)
    v_sample = jnp.stack(vs_l)
    hgrn_sample = jnp.stack(ss_l)
    return (h_p, h_s, k_prompt, v_prompt, hgrn_prompt, mem_k_prompt, mem_v_prompt,
            k_sample, v_sample, hgrn_sample)
```

```python
import numpy as np
from contextlib import ExitStack
import concourse.bass as bass
import concourse.mybir as mybir
from concourse.bass_utils import run_bass_kernel_spmd

F32 = mybir.dt.float32
BF16 = mybir.dt.bfloat16
AF = mybir.ActivationFunctionType
ALU = mybir.AluOpType

D = 2048
SEQ = 4096
NIN = 17408
T = 512
NT = SEQ // T
ALPHA = 2.0 ** 0.25
EPS = 1e-5
LAM_INIT = 0.8 - 0.6 * 1.0
NU_BIG = 22
U_MERGE = 22
U_OUT = 38
U_MK = 42
U_MV = 44
NUNITS = 46
RUN_SAMPLE = True
JIT_TILE0 = True
USZ = 9216
BIG_ORDER = [0, 1, 2, 3, 4, 5, 6, 7, 8, 10, 12, 14, 16, 9, 11, 13, 15, 17, 18, 19, 20, 21]


class Sem:
    def __init__(self, h):
        self.h = h
        self.n = 0


class Buf:
    def __init__(self, name, t=None):
        self.name = name
        self.t = t
        self.w = None
        self.r = {}
        self.sem = None


class Eng:
    def __init__(self, name, eng, sem):
        self.name = name
        self.eng = eng
        self.sem = sem
        self.seen = {}


class K:
    def __init__(self, nc, es):
        self.nc = nc
        self.es = es
        self.nsem = 0
        self.pe = Eng("pe", nc.tensor, self.new_sem("pe"))
        self.act = Eng("act", nc.scalar, self.new_sem("act"))
        self.dve = Eng("dve", nc.vector, self.new_sem("dve"))
        self.pool = Eng("pool", nc.gpsimd, self.new_sem("pool"))
        self.sp = Eng("sp", nc.sync, self.new_sem("sp"))
        self.engines = [self.pe, self.act, self.dve, self.pool, self.sp]
        self.dma_sems = []
        self.out_evs = {}
        self.rr = 0
        self.cast_rr = 0
        self.local_bufs = []
        self.sem_pool = []

    def new_sem(self, name):
        self.nsem += 1
        return Sem(self.es.enter_context(self.nc.semaphore("s%d_%s" % (self.nsem, name))))

    def sb(self, name, shape, dt, es=None):
        self.nbuf = getattr(self, "nbuf", 0) + 1
        name = "%s_%d" % (name, self.nbuf)
        t = (es or self.es).enter_context(self.nc.sbuf_tensor(name, list(shape), dt))
        b = Buf(name, t)
        if es is not None:
            self.local_bufs.append(b)
        return b

    def wait(self, E, ev):
        sem, val = ev
        if E is self.pe and sem is E.sem:
            return
        if E.seen.get(sem, 0) >= val:
            return
        E.eng.wait_ge(sem.h, val)
        E.seen[sem] = val

    def _deps(self, E, rd, wr):
        for b in rd:
            if b.w is not None:
                for ev in (b.w if isinstance(b.w, list) else [b.w]):
                    self.wait(E, ev)
        for b in wr:
            if b.w is not None:
                for ev in (b.w if isinstance(b.w, list) else [b.w]):
                    self.wait(E, ev)
            for sem, val in b.r.items():
                self.wait(E, (sem, val))

    def _record(self, ev, rd, wr):
        sem, val = ev
        for b in rd:
            if b.r.get(sem, 0) < val:
                b.r[sem] = val
        for b in wr:
            b.w = ev
            b.r = {}

    def op(self, E, fn, rd=(), wr=(), sig=True):
        self._deps(E, rd, wr)
        inst = fn()
        if sig:
            E.sem.n += 1
            inst.then_inc(E.sem.h, 1)
            ev = (E.sem, E.sem.n)
        else:
            ev = (E.sem, E.sem.n + 1)
        self._record(ev, rd, wr)
        return ev

    def dma(self, Q, out, in_, rd=(), wr=(), sbuf=None, is_out=False):
        self._deps(Q, rd, wr)
        inst = Q.eng.dma_start(out=out, in_=in_)
        if sbuf.sem is None:
            if self.sem_pool:
                sbuf.sem = self.sem_pool.pop()
            else:
                sbuf.sem = self.new_sem("d%d" % len(self.dma_sems))
                self.dma_sems.append(sbuf.sem)
        sbuf.sem.n += 16
        inst.then_inc(sbuf.sem.h, 16)
        ev = (sbuf.sem, sbuf.sem.n)
        self._record(ev, rd, wr)
        if is_out:
            self.out_evs[sbuf.sem] = sbuf.sem.n
        return ev

    def barrier(self):
        self._barrier()
        for b in self.local_bufs:
            if b.sem is not None:
                self.sem_pool.append(b.sem)
                b.sem = None
        self.local_bufs = []

    def _barrier(self):
        evs = [(e.sem, e.sem.n) for e in self.engines if e.sem.n > 0]
        evs += [(s, s.n) for s in self.dma_sems if s.n > 0]
        for E in self.engines:
            for ev in evs:
                if ev[0] is E.sem:
                    continue
                self.wait(E, ev)

    def finish(self):
        for sem, val in self.out_evs.items():
            self.wait(self.sp, (sem, val))
        for e in self.engines:
            if e is not self.sp and e.sem.n > 0:
                self.wait(self.sp, (e.sem, e.sem.n))

    def mm(self, out_ap, lhsT, rhs, start, stop, rd, wr, sig):
        nc = self.nc
        return self.op(self.pe, lambda: nc.tensor.matmul(out_ap, lhsT=lhsT, rhs=rhs, start=start, stop=stop),
                       rd=rd, wr=wr, sig=sig)

    def tr(self, out_ap, in_ap, ident_ap, rd, wr, sig=True):
        nc = self.nc
        return self.op(self.pe, lambda: nc.tensor.transpose(out_ap, in_ap, ident_ap), rd=rd, wr=wr, sig=sig)

    def actf(self, out_ap, in_ap, func, rd, wr, scale=None, bias=None, accum=None):
        nc = self.nc
        kw = {}
        if scale is not None:
            kw["scale"] = scale
        if bias is not None:
            kw["bias"] = bias
        if accum is not None:
            kw["accum_out"] = accum
        return self.op(self.act, lambda: nc.scalar.activation(out=out_ap, in_=in_ap, func=func, **kw), rd=rd, wr=wr)

    def tt(self, E, out_ap, in0, in1, op, rd, wr):
        return self.op(E, lambda: E.eng.tensor_tensor(out=out_ap, in0=in0, in1=in1, op=op), rd=rd, wr=wr)

    def ts(self, E, out_ap, in0, s1, s2, op0, op1, rd, wr):
        if op1 is None:
            return self.op(E, lambda: E.eng.tensor_scalar(out=out_ap, in0=in0, scalar1=s1, scalar2=None, op0=op0),
                           rd=rd, wr=wr)
        return self.op(E, lambda: E.eng.tensor_scalar(out=out_ap, in0=in0, scalar1=s1, scalar2=s2, op0=op0, op1=op1),
                       rd=rd, wr=wr)

    def stt(self, out_ap, in0, scalar, in1, op0, op1, rd, wr):
        nc = self.nc
        return self.op(self.dve, lambda: nc.vector.scalar_tensor_tensor(out=out_ap, in0=in0, scalar=scalar, in1=in1,
                                                                       op0=op0, op1=op1), rd=rd, wr=wr)

    def copy(self, E, out_ap, in_ap, rd, wr):
        if E is self.act:
            return self.op(E, lambda: self.nc.scalar.copy(out=out_ap, in_=in_ap), rd=rd, wr=wr)
        return self.op(E, lambda: E.eng.tensor_copy(out=out_ap, in_=in_ap), rd=rd, wr=wr)

    def memset(self, E, ap, val, wr):
        return self.op(E, lambda: E.eng.memset(ap, val), rd=(), wr=wr)


def build_program():
    nc = bass.Bass("TRN2", target_bir_lowering=False)
    es = ExitStack()
    k = K(nc, es)

    def din(name, shape):
        return nc.dram_tensor(name, list(shape), F32, kind="ExternalInput").ap()

    def dout(name, shape):
        return nc.dram_tensor(name, list(shape), F32, kind="ExternalOutput").ap()

    xp = din("xp", [SEQ, D])
    xs = din("xs", [64, D])
    ck = din("ck", [4, SEQ, 1024])
    cv = din("cv", [4, SEQ, 1024])
    s0 = din("s0", [32, 128, 128])
    cmk = din("cmk", [4, 256, 1024])
    cmv = din("cmv", [4, 256, 1024])
    mem = din("mem", [256, D])
    w_in = din("w_in", [D, NIN])
    w_mk = din("w_mk", [D, 1024])
    w_mv = din("w_mv", [D, 1024])
    w_a = din("w_a", [1024, D])
    w_b = din("w_b", [1024, D])
    w_c = din("w_c", [1024, D])
    w_o = din("w_o", [D, D])
    lam4 = din("lam4", [4, 64])
    subn = din("subn", [1, 128])
    lbl = din("lbl", [2, 1024])
    hgn = din("hgn", [1, 1024])
    lng = din("lng", [1, D])
    lnb = din("lnb", [1, D])
    c_ident = din("c_ident", [128, 128])
    c_rope = din("c_rope", [128, 33 * 16])
    c_tri = din("c_tri", [128, 64])
    c_rmask = din("c_rmask", [128, 512])
    c_rmask16 = din("c_rmask16", [128, 64])
    c_tri16 = din("c_tri16", [64, 64])
    c_rowmask = din("c_rowmask", [64, 4])

    y_p = dout("y_p", [SEQ, D])
    y_s = dout("y_s", [64, D])
    k_p = dout("k_p", [SEQ, 1024])
    v_p = dout("v_p", [SEQ, 1024])
    hg_p = dout("hg_p", [8, 128, 128])
    mk_p = dout("mk_p", [256, 1024])
    mv_p = dout("mv_p", [256, 1024])
    k_s = dout("k_s", [64, 1024])
    v_s = dout("v_s", [64, 1024])
    hg_s = dout("hg_s", [32, 128, 128])

    WS = nc.dram_tensor("ws_scratch", [NUNITS, 128, USZ], BF16, kind="Internal").ap()
    KTP = nc.dram_tensor("ktp_scratch", [8, 128, SEQ], BF16, kind="Internal").ap()
    VP = nc.dram_tensor("vp_scratch", [8, 128, 32 * 130], BF16, kind="Internal").ap()
    KTS = nc.dram_tensor("kts_scratch", [32, 128, SEQ + 16], BF16, kind="Internal").ap()
    VS = nc.dram_tensor("vs_scratch", [32, 128, 33 * 130], BF16, kind="Internal").ap()
    ws_b = [Buf("ws%d" % u) for u in range(NUNITS)]
    ktp_b = [Buf("ktp%d" % h) for h in range(8)]
    vp_b = [Buf("vp%d" % h) for h in range(8)]
    kts_b = [Buf("kts%d" % i) for i in range(32)]
    vs_b = [Buf("vs%d" % i) for i in range(32)]

    ident = k.sb("ident", [128, 128], F32)
    rope = k.sb("rope", [128, 33, 16], F32)
    tri = k.sb("tri", [128, 64], F32)
    rmask = k.sb("rmask", [128, 512], F32)
    rmask16 = k.sb("rmask16", [128, 64], F32)
    tri16 = k.sb("tri16", [64, 64], F32)
    rowmask = k.sb("rowmask", [64, 4], F32)
    gain = k.sb("gain", [128, 1024], F32)
    sn = k.sb("sn", [128, 128], F32)
    ones_bf = k.sb("ones_bf", [128, 128], BF16)
    lamw = k.sb("lamw", [128, 4, 64], F32)
    lamv = k.sb("lamv", [128, 8], F32)
    lbt = k.sb("lbt", [128, 2, 8], F32)
    lbv = k.sb("lbv", [128, 2, 8], F32)
    m05 = k.sb("m05", [128, 1], F32)
    W = [k.sb("wslot%d" % i, [128, USZ], BF16) for i in range(2)]
    xT = k.sb("xT", [128, 16, T], BF16)
    yaT = k.sb("yaT", [128, 8, T], BF16)
    ybT = k.sb("ybT", [128, 8, T], BF16)
    ycT = k.sb("ycT", [128, 8, T], BF16)
    qTz = [None, None]
    zaTh = [None]
    Sst = [k.sb("S%d" % h, [128, 128], F32) for h in range(8)]
    Sbf = [[k.sb("Sbf%d_%d" % (h, p), [128, 128], BF16) for p in range(2)] for h in range(8)]
    mkT = k.sb("mkT", [128, 8, 256], BF16)
    mvb = k.sb("mvb", [128, 2, 1024], BF16)

    PS = es.enter_context(nc.psum_tensor("psum_all", [128, 4096], F32))
    PS3 = PS[:, :].rearrange("p (b n) -> p b n", n=512)
    banks = [Buf("bank%d" % i, PS[:, i * 512:(i + 1) * 512]) for i in range(8)]
    st = {"brot": list(range(8)), "bi": 0, "wi": 0, "si": 0}

    def getbank():
        b = banks[st["brot"][st["bi"] % len(st["brot"])]]
        st["bi"] += 1
        return b

    def rr_eng(engs):
        k.rr += 1
        return engs[k.rr % len(engs)]

    SP, PE, ACT, DVE, POOL = k.sp, k.pe, k.act, k.dve, k.pool

    def load_const(buf, src, shape_ap=None):
        k.dma(SP, buf.t[:] if shape_ap is None else shape_ap, src, rd=(), wr=(buf,), sbuf=buf)

    load_const(ident, c_ident)
    k.dma(SP, rope.t[:].rearrange("p a b -> p (a b)"), c_rope, wr=(rope,), sbuf=rope)
    load_const(tri, c_tri)
    load_const(rmask, c_rmask)
    load_const(rmask16, c_rmask16)
    load_const(tri16, c_tri16)
    load_const(rowmask, c_rowmask)
    k.dma(SP, gain.t[:], hgn[0:1, :].to_broadcast([128, 1024]), wr=(gain,), sbuf=gain)
    k.dma(SP, sn.t[:], subn[0:1, :].to_broadcast([128, 128]), wr=(sn,), sbuf=sn)
    for i in range(4):
        k.dma(SP, lamw.t[:, i, :], lam4[i:i + 1, :].to_broadcast([128, 64]), wr=(lamw,), sbuf=lamw)
    with nc.allow_non_contiguous_dma(reason="tiny lb logits transpose load"):
        for l in range(2):
            k.dma(SP, lbt.t[:, l, :], lbl[l:l + 1, :].rearrange("o (h d) -> d (o h)", d=128), wr=(lbt,), sbuf=lbt)
    k.memset(DVE, ones_bf.t[:], 1.0, wr=(ones_bf,))
    k.memset(DVE, m05.t[:], -0.5, wr=(m05,))
    k.ts(DVE, sn.t[:], sn.t[:], 1.0 - LAM_INIT, None, ALU.mult, None, rd=(sn,), wr=(sn,))
    with ExitStack() as les:
        lt = k.sb("lam_tmp", [128, 64], F32, les)
        for j in range(2):
            k.tt(DVE, lt.t[:], lamw.t[:, 2 * j, :], lamw.t[:, 2 * j + 1, :], ALU.mult, rd=(lamw,), wr=(lt,))
            k.op(DVE, lambda j=j: nc.vector.reduce_sum(out=lamv.t[:, j:j + 1], in_=lt.t[:],
                                                       axis=mybir.AxisListType.X), rd=(lt,), wr=(lamv,))
        k.actf(lamv.t[:, 2:4], lamv.t[:, 0:2], AF.Exp, rd=(lamv,), wr=(lamv,))
        k.tt(DVE, lamv.t[:, 4:5], lamv.t[:, 3:4], lamv.t[:, 2:3], ALU.subtract, rd=(lamv,), wr=(lamv,))
        k.ts(DVE, lamv.t[:, 4:5], lamv.t[:, 4:5], -LAM_INIT, None, ALU.add, None, rd=(lamv,), wr=(lamv,))
        k.tt(DVE, lbv.t[:, 0, :], lbt.t[:, 0, :], lbt.t[:, 1, :], ALU.subtract, rd=(lbt,), wr=(lbv,))
        k.actf(lbv.t[:, 0, :], lbv.t[:, 0, :], AF.Sigmoid, rd=(lbv,), wr=(lbv,))
        k.ts(DVE, lbv.t[:, 1, :], lbv.t[:, 0, :], -1.0, 1.0, ALU.mult, ALU.add, rd=(lbv,), wr=(lbv,))
        k.barrier()
    neg_lam = lamv.t[:, 4:5]

    w_in_v = w_in.rearrange("(kc p) n -> p kc n", p=128)
    w_o_v = w_o.rearrange("(kc p) n -> p kc n", p=128)
    w_mk_v = w_mk.rearrange("(kc p) n -> p kc n", p=128)
    w_mv_v = w_mv.rearrange("(kc p) n -> p kc n", p=128)
    w_abc_v = [w.rearrange("(kc p) n -> p kc n", p=128) for w in (w_a, w_b, w_c)]

    def unit_pieces(u):
        if u < NU_BIG or u >= U_OUT:
            if u < NU_BIG:
                src, c0 = w_in_v, 512 * BIG_ORDER[u]
            elif u < U_MK:
                src, c0 = w_o_v, 512 * (u - U_OUT)
            elif u < U_MV:
                src, c0 = w_mk_v, 512 * (u - U_MK)
            else:
                src, c0 = w_mv_v, 512 * (u - U_MV)
            return [(1024 * q, [(src[:, 2 * q:2 * q + 2, c0:c0 + 512], 2, 512)]) for q in range(8)]
        oc = u - U_MERGE
        g = [w_in_v[:, :, 11264 + 2048 * b + 128 * oc: 11264 + 2048 * b + 128 * oc + 128] for b in range(3)]
        wv = [w_abc_v[b][:, :, 128 * oc:128 * oc + 128] for b in range(3)]
        pcs = []
        for b in range(3):
            for q in range(2):
                pcs.append((2048 * b + 1024 * q, [(g[b][:, 8 * q:8 * q + 8, :], 8, 128)]))
        for b in range(3):
            pcs.append((6144 + 1024 * b, [(wv[b], 8, 128)]))
        return pcs

    stg = []
    cast_seq = [DVE, ACT, DVE, ACT, POOL]

    def wconvert(u):
        slot = W[st["wi"] % 2]
        st["wi"] += 1
        last = {}
        for (off, parts) in unit_pieces(u):
            sg = stg[st["si"] % len(stg)]
            E = cast_seq[st["si"] % len(cast_seq)]
            st["si"] += 1
            o = 0
            for (src, a, b) in parts:
                k.dma(SP, sg.t[:, o:o + a * b].rearrange("p (a b) -> p a b", b=b), src, wr=(sg,), sbuf=sg)
                o += a * b
            k._deps(E, (), (slot,))
            last[E] = k.op(E, (lambda E=E, off=off, o=o, sg=sg: (nc.scalar.copy(out=slot.t[:, off:off + o], in_=sg.t[:, 0:o])
                                                               if E is ACT else
                                                               E.eng.tensor_copy(out=slot.t[:, off:off + o], in_=sg.t[:, 0:o]))),
                           rd=(sg,), wr=())
        slot.w = list(last.values())
        slot.r = {}
        nel = USZ if U_MERGE <= u < U_OUT else 8192
        k.dma(ACT, WS[u, :, 0:nel], slot.t[:, 0:nel], rd=(slot,), wr=(ws_b[u],), sbuf=slot)
        return slot

    def wload(u):
        slot = W[st["wi"] % 2]
        st["wi"] += 1
        nel = USZ if U_MERGE <= u < U_OUT else 8192
        k.dma(SP, slot.t[:, 0:nel], WS[u, :, 0:nel], rd=(ws_b[u],), wr=(slot,), sbuf=slot)
        return slot

    class WStream:
        def __init__(self, order, jit=False):
            self.order = list(order)
            self.pos = 0
            self.loaded = []
            self.jit = jit
            self.bg = None
            self.ncall = 0

        def _load(self):
            u = self.order[self.pos]
            self.loaded.append(wconvert(u) if self.jit else wload(u))
            self.pos += 1

        def prefetch(self):
            pass

        def get(self):
            if not self.loaded:
                self._load()
            s = self.loaded.pop(0)
            if self.pos < len(self.order):
                self._load()
            if self.bg is not None:
                self.ncall += 1
                if self.ncall % 2 == 0:
                    self.bg()
            return s

    def evac(E, out_ap, in_ap, rd, wr):
        k.copy(E, out_ap, in_ap, rd=rd, wr=wr)

    def build_xT(x_src, subs, pes):
        xst = [k.sb("xst%d" % i, [128, D], F32, pes) for i in range(2)]
        for si_, (t0, n) in enumerate(subs):
            xb = xst[si_ % 2]
            k.dma(SP, xb.t[0:n, :], x_src[t0:t0 + n, :], wr=(xb,), sbuf=xb)
            for g in range(4):
                bk = getbank()
                for i in range(4):
                    kc = 4 * g + i
                    k.tr(bk.t[:, i * 128:i * 128 + n], xb.t[0:n, kc * 128:(kc + 1) * 128], ident.t[0:n, 0:n],
                         rd=(xb, ident), wr=(bk,), sig=(i == 3))
                E = rr_eng([ACT, DVE])
                evac(E, xT.t[:, 4 * g:4 * g + 4, t0:t0 + n],
                     bk.t[:, :].rearrange("p (a b) -> p a b", b=128)[:, :, 0:n], rd=(bk,), wr=(xT,))

    def proj_tm(slot, sub, dst_bank):
        t0, n = sub
        wv = slot.t[:, 0:8192].rearrange("p (kc c) -> p kc c", c=512)
        for kc in range(16):
            k.mm(dst_bank.t[0:n, :], xT.t[:, kc, t0:t0 + n], wv[:, kc, :], kc == 0, kc == 15,
                 rd=(xT, slot), wr=(dst_bank,), sig=(kc == 15))

    def proj_fm(slot, cb, ntok, dst_bank):
        wv = slot.t[:, 0:8192].rearrange("p (kc c) -> p kc c", c=512)
        for kc in range(16):
            k.mm(dst_bank.t[:, 0:ntok], wv[:, kc, cb * 128:(cb + 1) * 128], xT.t[:, kc, 0:ntok], kc == 0, kc == 15,
                 rd=(xT, slot), wr=(dst_bank,), sig=(kc == 15))

    def rope_apply(tm, n, S):
        v = tm.t[0:n, :].rearrange("p (m d) -> p m d", d=64)
        x1, x2 = v[:, :, 0:8], v[:, :, 8:16]
        cs = rope.t[0:n, S, 0:8].unsqueeze(1).to_broadcast([n, 16, 8])
        sn_ = rope.t[0:n, S, 8:16].unsqueeze(1).to_broadcast([n, 16, 8])
        return x1, x2, cs, sn_

    def phase_a(ws, subs, ntok, rope_slots, k_out, v_out, store_kt, store_v, pes):
        tm = [k.sb("tm%d" % i, [128, 1024], F32, pes) for i in range(len(subs))]
        rt = [k.sb("ropet%d" % i, [128, 16, 8], F32, pes) for i in range(4)]
        ktst = k.sb("ktst", [128, 8, T], BF16, pes)
        vst = k.sb("vst", [128, 4, 8, 130], BF16, pes)
        k.memset(POOL, vst.t[:, :, :, 128:129], 1.0, wr=(vst,))
        k.memset(POOL, vst.t[:, :, :, 129:130], 0.0, wr=(vst,))
        for part in range(3):
            for half in range(2):
                ws.prefetch()
                slot = ws.get()
                ws.prefetch()
                for si_, sub in enumerate(subs):
                    bk = getbank()
                    proj_tm(slot, sub, bk)
                    n = sub[1]
                    E = rr_eng([ACT, DVE])
                    evac(E, tm[si_].t[0:n, half * 512:(half + 1) * 512], bk.t[0:n, :], rd=(bk,), wr=(tm[si_],))
            for si_, (t0, n) in enumerate(subs):
                tmb = tm[si_]
                if part < 2:
                    x1, x2, cs, sn_ = rope_apply(tmb, n, rope_slots[si_])
                    a, b_, c, d_ = [r.t[0:n] for r in rt]
                    rtb = tuple(rt)
                    k.tt(DVE, a, x1, cs, ALU.mult, rd=(tmb, rope), wr=(rt[0],))
                    k.tt(DVE, b_, x2, sn_, ALU.mult, rd=(tmb, rope), wr=(rt[1],))
                    k.tt(DVE, c, x2, cs, ALU.mult, rd=(tmb, rope), wr=(rt[2],))
                    k.tt(DVE, d_, x1, sn_, ALU.mult, rd=(tmb, rope), wr=(rt[3],))
                    k.tt(DVE, x1, a, b_, ALU.subtract, rd=(rt[0], rt[1]), wr=(tmb,))
                    k.tt(DVE, x2, c, d_, ALU.add, rd=(rt[2], rt[3]), wr=(tmb,))
                    if part == 1:
                        k.dma(ACT, k_out[t0:t0 + n, :], tmb.t[0:n, :], rd=(tmb,), sbuf=tmb, is_out=True)
                    for g in range(2):
                        bk = getbank()
                        for i in range(4):
                            h = 4 * g + i
                            k.tr(bk.t[:, i * 128:i * 128 + n], tmb.t[0:n, h * 128:(h + 1) * 128], ident.t[0:n, 0:n],
                                 rd=(tmb, ident), wr=(bk,), sig=(i == 3))
                        bv = bk.t[:, :].rearrange("p (a b) -> p a b", b=128)
                        if part == 0:
                            evac(ACT, qTz[0].t[0:64, 4 * g:4 * g + 4, t0:t0 + n], bv[0:64, :, 0:n], rd=(bk,), wr=(qTz[0],))
                            evac(DVE, qTz[1].t[64:128, 4 * g:4 * g + 4, t0:t0 + n], bv[64:128, :, 0:n], rd=(bk,), wr=(qTz[1],))
                        else:
                            E = rr_eng([ACT, DVE])
                            evac(E, ktst.t[:, 4 * g:4 * g + 4, t0:t0 + n], bv[:, :, 0:n], rd=(bk,), wr=(ktst,))
                else:
                    k.dma(ACT, v_out[t0:t0 + n, :], tmb.t[0:n, :], rd=(tmb,), sbuf=tmb, is_out=True)
                    E = rr_eng([ACT, DVE])
                    evac(E, vst.t[0:n, si_, :, 0:128], tmb.t[0:n, :].rearrange("p (h e) -> p h e", e=128),
                         rd=(tmb,), wr=(vst,))
            if part == 1:
                store_kt(ktst)
            if part == 2:
                store_v(vst)

    AX = mybir.AxisListType.X

    def group_store(dsts_srcs, src_buf, dst_bufs):
        ev = None
        for (dst, src) in dsts_srcs:
            ev = k.dma(ACT, dst, src, rd=(src_buf,), wr=(), sbuf=src_buf)
        for b in dst_bufs:
            for sem, val in list(b.r.items()):
                pass
            b.w = ev
            b.r = {}

    def rstd_from(ss_ap, buf, scale, n):
        k.ts(DVE, ss_ap, ss_ap, scale, EPS, ALU.mult, ALU.add, rd=(buf,), wr=(buf,))
        k.tt(POOL, ss_ap, ss_ap, m05.t[0:n, :], ALU.pow, rd=(buf, m05), wr=(buf,))

    def phase_z(ws, ntok):
        for half in range(2):
            ws.prefetch()
            slot = ws.get()
            ws.prefetch()
            for cb in range(4):
                bk = getbank()
                proj_fm(slot, cb, ntok, bk)
                k.actf(zaTh[0].t[:, half * 4 + cb, 0:ntok], bk.t[:, 0:ntok], AF.Silu, rd=(bk,), wr=(zaTh[0],))

    def attn_post(O, nq, m, h, qcol0, o1b, o2b, onb, stb):
        if m == 0:
            k.op(DVE, lambda: nc.vector.reciprocal(out=stb.t[0:nq, 0:1], in_=O.t[0:nq, 128:129]), rd=(O,), wr=(stb,))
            k.ts(DVE, o1b.t[0:nq, :], O.t[0:nq, 0:128], stb.t[0:nq, 0:1], None, ALU.mult, None, rd=(O, stb), wr=(o1b,))
            return
        k.op(DVE, lambda: nc.vector.reciprocal(out=stb.t[0:nq, 1:2], in_=O.t[0:nq, 128:129]), rd=(O,), wr=(stb,))
        k.tt(DVE, stb.t[0:nq, 1:2], stb.t[0:nq, 1:2], neg_lam[0:nq, :], ALU.mult, rd=(stb, lamv), wr=(stb,))
        k.stt(o2b.t[0:nq, :], O.t[0:nq, 0:128], stb.t[0:nq, 1:2], o1b.t[0:nq, :], ALU.mult, ALU.add,
              rd=(O, stb, o1b), wr=(o2b,))
        k.actf(onb.t[0:nq, :], o2b.t[0:nq, :], AF.Square, rd=(o2b,), wr=(onb, stb), accum=stb.t[0:nq, 2:3])
        rstd_from(stb.t[0:nq, 2:3], stb, 1.0 / 128.0, nq)
        k.stt(onb.t[0:nq, :], o2b.t[0:nq, :], stb.t[0:nq, 2:3], sn.t[0:nq, :], ALU.mult, ALU.mult,
              rd=(o2b, stb, sn), wr=(onb,))
        tb = getbank()
        k.tr(tb.t[:, 0:nq], onb.t[0:nq, :], ident.t[0:nq, 0:nq], rd=(onb, ident), wr=(tb,))
        k.tt(DVE, yaT.t[:, h, qcol0:qcol0 + nq], tb.t[:, 0:nq], zaTh[0].t[:, h, qcol0:qcol0 + nq], ALU.mult,
             rd=(tb, zaTh[0]), wr=(yaT,))

    def attn_prompt(j, pes):
        nk = 4 * (j + 1)
        nfull = 4 * j
        KTh = [k.sb("KTh%d" % i, [128, SEQ], BF16, pes) for i in range(2)]
        Vh = [k.sb("Vh%d" % i, [128, 32, 130], BF16, pes) for i in range(2)]
        PT = [k.sb("PT%d" % i, [128, 2, 512], BF16, pes) for i in range(3)]
        o1b = [k.sb("o1b%d" % i, [128, 128], F32, pes) for i in range(4)]
        o2b = [k.sb("o2b%d" % i, [128, 128], F32, pes) for i in range(2)]
        onb = [k.sb("onb%d" % i, [128, 128], F32, pes) for i in range(2)]
        stb = [k.sb("stb%d" % i, [128, 4], F32, pes) for i in range(4)]
        st["brot"] = [0, 1, 2, 3]
        ob = banks[4:8]

        def load(h):
            k.dma(SP, KTh[h % 2].t[:, 0:nk * 128], KTP_v[h, :, 0:nk * 128], rd=(ktp_b[h],), wr=(KTh[h % 2],), sbuf=KTh[h % 2])
            k.dma(SP, Vh[h % 2].t[:, 0:nk, :], VP_v[h, :, 0:nk, :], rd=(vp_b[h],), wr=(Vh[h % 2],), sbuf=Vh[h % 2])

        items = [(kt, kt + 1) for kt in range(0, nfull, 2)] + [(kt,) for kt in range(nfull, nk)]
        flat = [(h, m, it, idx == len(items) - 1) for h in range(8) for m in range(2) for idx, it in enumerate(items)]
        load(0)
        load(1)
        pend = []
        pi = 0

        def emit_pv(ent):
            h, m, item_, last, d_, pt_ = ent
            V = Vh[h % 2]
            for ii, kt_ in enumerate(item_):
                for qs in range(d_, 4):
                    k.mm(ob[qs].t[:, 0:130], pt_.t[:, ii, (qs - d_) * 128:(qs - d_ + 1) * 128], V.t[:, kt_, :],
                         kt_ == 0, kt_ == nfull + qs, rd=(pt_, V), wr=(ob[qs],),
                         sig=(qs == 3 and ii == len(item_) - 1))
            if last:
                for qs in range(4):
                    attn_post(ob[qs], 128, m, h, qs * 128, o1b[qs], o2b[qs % 2], onb[qs % 2], stb[qs])
                if m == 1 and h + 2 < 8:
                    load(h + 2)

        for (h, m, item, last) in flat:
            KT = KTh[h % 2]
            b0 = 2 * (pi % 2)
            pt = PT[pi % 3]
            pi += 1
            d = max(0, item[0] - nfull)
            q0 = 128 * d
            N = 512 - q0
            bks = [banks[b0 + ii] for ii in range(len(item))]
            for ii, kt in enumerate(item):
                k.mm(bks[ii].t[:, 0:N], KT.t[:, kt * 128:(kt + 1) * 128], qTz[m].t[:, h, q0:512], True, True,
                     rd=(KT, qTz[m]), wr=(bks[ii],), sig=True)
            if len(item) == 2:
                k.actf(pt.t[:, :, :], PS3[:, b0:b0 + 2, :], AF.Exp, rd=tuple(bks), wr=(pt,), scale=0.125)
            else:
                k.actf(pt.t[:, 0, 0:N], bks[0].t[:, 0:N], AF.Exp, rd=tuple(bks), wr=(pt,), scale=0.125)
                k.memset(POOL, pt.t[64:128, 0, 0:64], 0.0, wr=(pt,))
            pend.append((h, m, item, last, d, pt))
            if len(pend) > 2:
                emit_pv(pend.pop(0))
        while pend:
            emit_pv(pend.pop(0))
        st["brot"] = list(range(8))

    def hgrn_prep(h, hh, ntok, csz, qh, ff, tmp, rm, qdT, kdT, qs_writer, ksf, decay):
        nch = ntok // csz
        mid = (csz - 1) // 2
        tg, tk, tb_, t1, e1, e3 = [t_.t[:, 0:ntok] for t_ in tmp[0:6]]
        Tg, Tk, Tb, T1, E1b, E3b = tmp[0:6]
        k.actf(tg, ff.t[:, hh, 0:ntok], AF.Ln, rd=(ff,), wr=(Tg,))
        k.ts(DVE, tk, ff.t[:, hh, 0:ntok], -1.0, 1.0, ALU.mult, ALU.add, rd=(ff,), wr=(Tk,))
        k.op(DVE, lambda: nc.vector.tensor_tensor_scan(out=tb_, data0=rm.t[:, 0:ntok], data1=tg, initial=0.0,
                                                       op0=ALU.mult, op1=ALU.add), rd=(rm, Tg), wr=(Tb,))
        b3 = tb_.rearrange("p (c t) -> p c t", t=csz)
        k.tt(DVE, t1.rearrange("p (c t) -> p c t", t=csz), b3, b3[:, :, mid:mid + 1].to_broadcast([128, nch, csz]),
             ALU.subtract, rd=(Tb,), wr=(T1,))
        k.actf(e1, t1, AF.Exp, rd=(T1,), wr=(E1b,))
        k.tt(POOL, qdT.t[:, hh, 0:ntok], qh.t[:, hh, 0:ntok], e1, ALU.mult, rd=(qh, E1b), wr=(qdT,))
        k.actf(e1, t1, AF.Exp, rd=(T1,), wr=(E1b,), scale=-1.0)
        k.tt(POOL, kdT.t[:, hh, 0:ntok], tk, e1, ALU.mult, rd=(Tk, E1b), wr=(kdT,))
        k.actf(e3, tb_, AF.Exp, rd=(Tb,), wr=(E3b,))
        qs_writer(hh, qh.t[:, hh, 0:ntok], e3, qh, E3b)
        k.copy(POOL, decay.t[:, hh, 0:nch], e3.rearrange("p (c t) -> p c t", t=csz)[:, :, csz - 1], rd=(E3b,), wr=(decay,))
        k.tt(DVE, t1.rearrange("p (c t) -> p c t", t=csz), b3[:, :, csz - 1:csz].to_broadcast([128, nch, csz]), b3,
             ALU.subtract, rd=(Tb,), wr=(T1,))
        k.actf(e1, t1, AF.Exp, rd=(T1,), wr=(E1b,))
        k.tt(DVE, ksf.t[:, 0:ntok], tk, e1, ALU.mult, rd=(Tk, E1b), wr=(ksf,))

    def ob_post1(obank, c0, nq, h, onb, stb, si):
        k.actf(onb.t[0:nq, :], obank.t[0:nq, c0:c0 + 128], AF.Square, rd=(obank,), wr=(onb, stb), accum=stb.t[0:nq, si:si + 1])
        rstd_from(stb.t[0:nq, si:si + 1], stb, 1.0 / 128.0, nq)
        k.stt(onb.t[0:nq, :], obank.t[0:nq, c0:c0 + 128], stb.t[0:nq, si:si + 1], gain.t[0:nq, h * 128:(h + 1) * 128],
              ALU.mult, ALU.mult, rd=(obank, stb, gain), wr=(onb,))

    def ob_post2(nq, h, gz_ap, gz_buf, qcol0, onb):
        tb = getbank()
        k.tr(tb.t[:, 0:nq], onb.t[0:nq, :], ident.t[0:nq, 0:nq], rd=(onb, ident), wr=(tb,))
        k.tt(DVE, ybT.t[:, h, qcol0:qcol0 + nq], tb.t[:, 0:nq], gz_ap, ALU.mult, rd=(tb, gz_buf), wr=(ybT,))

    def ob_post(obank, c0, nq, h, gz_ap, gz_buf, qcol0, onb, stb, si):
        ob_post1(obank, c0, nq, h, onb, stb, si)
        ob_post2(nq, h, gz_ap, gz_buf, qcol0, onb)

    def hgrn_proj(ws, g, ntok, subs, qh, ff, vB, gz, og, tmp, preps):
        preps = list(preps)
        for which in range(5):
            ws.prefetch()
            slot = ws.get()
            ws.prefetch()
            if which == 2:
                for si_, sub in enumerate(subs):
                    bk = getbank()
                    proj_tm(slot, sub, bk)
                    evac(rr_eng([ACT, DVE]), vB.t[0:sub[1], si_, :], bk.t[0:sub[1], :], rd=(bk,), wr=(vB,))
                    if preps:
                        preps.pop(0)()
                while preps:
                    preps.pop(0)()
                continue
            for hh in range(4):
                h = 4 * g + hh
                bk = getbank()
                proj_fm(slot, hh, ntok, bk)
                src = bk.t[:, 0:ntok]
                if which == 0:
                    k.actf(qh.t[:, hh, 0:ntok], src, AF.Silu, rd=(bk,), wr=(qh,))
                elif which == 1:
                    k.actf(ff.t[:, hh, 0:ntok], src, AF.Sigmoid, rd=(bk,), wr=(ff,))
                    k.ts(DVE, ff.t[:, hh, 0:ntok], ff.t[:, hh, 0:ntok], lbv.t[:, 1, h:h + 1], lbv.t[:, 0, h:h + 1],
                         ALU.mult, ALU.add, rd=(ff, lbv), wr=(ff,))
                elif which == 3:
                    k.actf(og.t[:, hh, 0:ntok], src, AF.Sigmoid, rd=(bk,), wr=(og,))
                else:
                    k.actf(tmp[5].t[:, 0:ntok], src, AF.Silu, rd=(bk,), wr=(tmp[5],))
                    k.tt(DVE, og.t[:, hh, 0:ntok], tmp[5].t[:, 0:ntok], og.t[:, hh, 0:ntok], ALU.mult,
                         rd=(tmp[5], og), wr=(og,))

    def phase_b_prompt(ws, j):
        for g in range(2):
            with ExitStack() as pes:
                qh = k.sb("qh", [128, 4, T], F32, pes)
                ff = k.sb("ff", [128, 4, T], F32, pes)
                vB = k.sb("vB", [128, 4, 512], BF16, pes)
                gz = None
                og = k.sb("og", [128, 4, T], BF16, pes)
                qdT = k.sb("qdT", [128, 4, T], BF16, pes)
                kdT = k.sb("kdT", [128, 4, T], BF16, pes)
                qsE = k.sb("qsE", [128, 4, 4, 128], BF16, pes)
                qsO = k.sb("qsO", [128, 4, 4, 128], BF16, pes)
                ks_tm = k.sb("ks_tm", [128, 4, 4, 128], BF16, pes)
                decay = k.sb("decay", [128, 4, 8], F32, pes)
                tmp = [k.sb("htmp%d" % i, [128, T], F32, pes) for i in range(6)]
                scTs = [k.sb("scT%d" % i, [128, 4, 128], BF16, pes) for i in range(2)]
                onb = [k.sb("onbB%d" % i, [128, 128], F32, pes) for i in range(8)]
                stb = k.sb("stbB", [128, 16], F32, pes)
                for s_ in scTs:
                    k.memset(POOL, s_.t[:], 0.0, wr=(s_,))
                k.memset(POOL, qsE.t[:], 0.0, wr=(qsE,))
                k.memset(POOL, qsO.t[:], 0.0, wr=(qsO,))
                ksf = [k.sb("ksf%d" % i, [128, T], F32, pes) for i in range(4)]

                def qs_writer(hh, q_ap, e3_ap, qbuf, ebuf):
                    qv = q_ap.rearrange("p (s two t) -> p s two t", two=2, t=64)
                    ev_ = e3_ap.rearrange("p (s two t) -> p s two t", two=2, t=64)
                    k.tt(DVE, qsE.t[:, hh, :, 0:64], qv[:, :, 0, :], ev_[:, :, 0, :], ALU.mult, rd=(qbuf, ebuf), wr=(qsE,))
                    k.tt(DVE, qsO.t[:, hh, :, 64:128], qv[:, :, 1, :], ev_[:, :, 1, :], ALU.mult, rd=(qbuf, ebuf), wr=(qsO,))

                def ks_writer(hh, Tg):
                    bk = getbank()
                    for s_ in range(4):
                        k.tr(bk.t[:, s_ * 128:(s_ + 1) * 128], Tg.t[:, s_ * 128:(s_ + 1) * 128], ident.t[:, :],
                             rd=(Tg, ident), wr=(bk,), sig=(s_ == 3))
                    evac(rr_eng([ACT, DVE]), ks_tm.t[:, hh, :, :], bk.t[:, :].rearrange("p (s d) -> p s d", d=128), rd=(bk,), wr=(ks_tm,))

                preps = [lambda hh=hh: hgrn_prep(4 * g + hh, hh, T, 64, qh, ff, tmp, rmask, qdT, kdT, qs_writer, ksf[hh], decay)
                         for hh in range(4)]
                hgrn_proj(ws, g, T, subs4, qh, ff, vB, gz, og, tmp, preps)
                for hh in range(4):
                    ks_writer(hh, ksf[hh])

                def post2(cp_):
                    cs2 = slice(cp_ * 128, (cp_ + 1) * 128)
                    for hh in range(4):
                        ob_post2(128, 4 * g + hh, og.t[:, hh, cs2], og, cp_ * 128, onb[(cp_ % 2) * 4 + hh])

                for cp in range(4):
                    cs_ = slice(cp * 128, (cp + 1) * 128)
                    scb = getbank()
                    for hh in range(4):
                        k.mm(scb.t[:, hh * 128:(hh + 1) * 128], kdT.t[:, hh, cs_], qdT.t[:, hh, cs_], True, True,
                             rd=(kdT, qdT), wr=(scb,), sig=(hh == 3))
                    scT = scTs[cp % 2]
                    scv = scb.t[:, :].rearrange("p (h t) -> p h t", t=128)
                    k.tt(DVE, scT.t[0:64, :, 0:64], scv[0:64, :, 0:64], tri.t[0:64, :].unsqueeze(1).to_broadcast([64, 4, 64]),
                         ALU.mult, rd=(scb, tri), wr=(scT,))
                    k.tt(DVE, scT.t[64:128, :, 64:128], scv[64:128, :, 64:128],
                         tri.t[64:128, :].unsqueeze(1).to_broadcast([64, 4, 64]), ALU.mult, rd=(scb, tri), wr=(scT,))
                    for par in range(2):
                        ps_ = slice(par * 64, (par + 1) * 64)
                        dsb = getbank()
                        for hh in range(4):
                            k.mm(dsb.t[:, hh * 128:(hh + 1) * 128], ks_tm.t[ps_, hh, cp, :], vB.t[ps_, cp, hh * 128:(hh + 1) * 128],
                                 True, True, rd=(ks_tm, vB), wr=(dsb,), sig=(hh == 3))
                        if par == 0 and cp > 0:
                            post2(cp - 1)
                        if par == 1:
                            obk = getbank()
                            for hh in range(4):
                                h = 4 * g + hh
                                oc_ = slice(hh * 128, (hh + 1) * 128)
                                k.mm(obk.t[:, oc_], scT.t[:, hh, :], vB.t[:, cp, oc_], True, False, rd=(scT, vB), wr=(obk,), sig=False)
                                k.mm(obk.t[:, oc_], qsE.t[:, hh, cp, :], Sbf[h][1].t[:], False, False, rd=(qsE, Sbf[h][1]), wr=(obk,), sig=False)
                                k.mm(obk.t[:, oc_], qsO.t[:, hh, cp, :], Sbf[h][0].t[:], False, True, rd=(qsO, Sbf[h][0]), wr=(obk,), sig=True)
                        for hh in range(4):
                            h = 4 * g + hh
                            k.stt(Sst[h].t[:], Sst[h].t[:], decay.t[:, hh, 2 * cp + par:2 * cp + par + 1],
                                  dsb.t[:, hh * 128:(hh + 1) * 128], ALU.mult, ALU.add, rd=(Sst[h], decay, dsb), wr=(Sst[h],))
                            k.copy(rr_eng([ACT, POOL]), Sbf[h][par].t[:], Sst[h].t[:], rd=(Sst[h],), wr=(Sbf[h][par],))
                    for hh in range(4):
                        ob_post1(obk, hh * 128, 128, 4 * g + hh, onb[(cp % 2) * 4 + hh], stb, cp * 4 + hh)
                post2(3)
                k.barrier()

    def phase_c(ws, ntok, groups, pes):
        qcT = k.sb("qcT", [128, 8, ntok], BF16, pes)
        zcs = k.sb("zcs", [128, 8, ntok], BF16, pes)
        PTm = [k.sb("PTm%d" % i, [128, 2, ntok], BF16, pes) for i in range(2)]
        rs = [k.sb("rsC%d" % i, [128, ntok], F32, pes) for i in range(2)]
        tc_ = [k.sb("tmpC%d" % i, [128, ntok], F32, pes) for i in range(2)]
        for which in range(2):
            for half in range(2):
                ws.prefetch()
                slot = ws.get()
                ws.prefetch()
                for cb in range(4):
                    bk = getbank()
                    proj_fm(slot, cb, ntok, bk)
                    if which == 0:
                        evac(rr_eng([ACT, DVE]), qcT.t[:, half * 4 + cb, 0:ntok], bk.t[:, 0:ntok], rd=(bk,), wr=(qcT,))
                    else:
                        k.actf(zcs.t[:, half * 4 + cb, 0:ntok], bk.t[:, 0:ntok], AF.Silu, rd=(bk,), wr=(zcs,))
        it = 0
        for (c0, n, mkb, mvb_) in groups:
            cs_ = slice(c0, c0 + n)
            for h in range(4):
                pt = PTm[it % 2]
                r_ = rs[it % 2]
                it += 1
                for mt in range(2):
                    sbk = getbank()
                    for half in range(2):
                        k.mm(sbk.t[:, 0:n], mkb.t[:, 2 * h + half, mt * 128:(mt + 1) * 128], qcT.t[:, 2 * h + half, cs_],
                             half == 0, half == 1, rd=(mkb, qcT), wr=(sbk,), sig=(half == 1))
                    k.actf(pt.t[:, mt, 0:n], sbk.t[:, 0:n], AF.Exp, rd=(sbk,), wr=(pt,), scale=1.0 / 16.0)
                smb = getbank()
                for mt in range(2):
                    k.mm(smb.t[:, 0:n], ones_bf.t[:, :], pt.t[:, mt, 0:n], mt == 0, mt == 1, rd=(ones_bf, pt), wr=(smb,), sig=(mt == 1))
                k.op(DVE, lambda: nc.vector.reciprocal(out=r_.t[:, 0:n], in_=smb.t[:, 0:n]), rd=(smb,), wr=(r_,))
                for eh in range(2):
                    obk = getbank()
                    ch = 2 * h + eh
                    for mt in range(2):
                        k.mm(obk.t[:, 0:n], mvb_.t[:, mt, ch * 128:(ch + 1) * 128], pt.t[:, mt, 0:n], mt == 0, mt == 1,
                             rd=(mvb_, pt), wr=(obk,), sig=(mt == 1))
                    t_ = tc_[eh]
                    k.tt(DVE, t_.t[:, 0:n], obk.t[:, 0:n], r_.t[:, 0:n], ALU.mult, rd=(obk, r_), wr=(t_,))
                    k.tt(POOL, ycT.t[:, ch, cs_], t_.t[:, 0:n], zcs.t[:, ch, cs_], ALU.mult, rd=(t_, zcs), wr=(ycT,))

    def phase_merge_out(ws, ntok, subs, x_src, y_dst, pes):
        mergedT = k.sb("mergedT", [128, 16, T], BF16, pes)
        sg = [k.sb("sgM%d" % i, [128, T], F32, pes) for i in range(3)]
        acc = [k.sb("accM%d" % i, [128, T], F32, pes) for i in range(2)]
        tmpm = [k.sb("tmpM%d" % i, [128, T], F32, pes) for i in range(2)]
        xr = [k.sb("xr%d" % i, [128, D], F32, pes) for i in range(len(subs))]
        stat = k.sb("lnstat", [128, 4, 6], F32, pes)
        mv_ = k.sb("lnmv", [128, 4], F32, pes)
        gam = k.sb("gam", [128, D], F32, pes)
        bet = k.sb("bet", [128, D], F32, pes)
        k.dma(SP, gam.t[:], lng[0:1, :].to_broadcast([128, D]), wr=(gam,), sbuf=gam)
        k.dma(SP, bet.t[:], lnb[0:1, :].to_broadcast([128, D]), wr=(bet,), sbuf=bet)
        for si_, (t0, n) in enumerate(subs):
            k.dma(SP, xr[si_].t[0:n, :], x_src[t0:t0 + n, :], wr=(xr[si_],), sbuf=xr[si_])
        yTs = (yaT, ybT, ycT)
        for oc in range(16):
            ws.prefetch()
            slot = ws.get()
            ws.prefetch()
            a_ = acc[oc % 2]
            for br in range(3):
                gb = getbank()
                gv = slot.t[:, br * 2048:(br + 1) * 2048].rearrange("p (kc c) -> p kc c", c=128)
                for kc in range(16):
                    k.mm(gb.t[:, 0:ntok], gv[:, kc, :], xT.t[:, kc, 0:ntok], kc == 0, kc == 15, rd=(slot, xT), wr=(gb,), sig=(kc == 15))
                yb_ = getbank()
                wv = slot.t[:, 6144 + br * 1024:6144 + (br + 1) * 1024].rearrange("p (kc c) -> p kc c", c=128)
                for kc in range(8):
                    k.mm(yb_.t[:, 0:ntok], wv[:, kc, :], yTs[br].t[:, kc, 0:ntok], kc == 0, kc == 7, rd=(slot, yTs[br]), wr=(yb_,), sig=(kc == 7))
                s_ = sg[br]
                k.actf(s_.t[:, 0:ntok], gb.t[:, 0:ntok], AF.Sigmoid, rd=(gb,), wr=(s_,))
                if br == 0:
                    k.tt(DVE, a_.t[:, 0:ntok], s_.t[:, 0:ntok], yb_.t[:, 0:ntok], ALU.mult, rd=(s_, yb_), wr=(a_,))
                else:
                    t_ = tmpm[br - 1]
                    k.tt(DVE, t_.t[:, 0:ntok], s_.t[:, 0:ntok], yb_.t[:, 0:ntok], ALU.mult, rd=(s_, yb_), wr=(t_,))
                    if br == 1:
                        k.tt(POOL, a_.t[:, 0:ntok], a_.t[:, 0:ntok], t_.t[:, 0:ntok], ALU.add, rd=(a_, t_), wr=(a_,))
                    else:
                        k.tt(POOL, mergedT.t[:, oc, 0:ntok], a_.t[:, 0:ntok], t_.t[:, 0:ntok], ALU.add, rd=(a_, t_), wr=(mergedT,))
        for cb in range(4):
            ws.prefetch()
            slot = ws.get()
            ws.prefetch()
            wv = slot.t[:, 0:8192].rearrange("p (kc c) -> p kc c", c=512)
            for si_, (t0, n) in enumerate(subs):
                bk = getbank()
                for kc in range(16):
                    k.mm(bk.t[0:n, :], mergedT.t[:, kc, t0:t0 + n], wv[:, kc, :], kc == 0, kc == 15, rd=(mergedT, slot), wr=(bk,), sig=(kc == 15))
                xc = xr[si_].t[0:n, cb * 512:(cb + 1) * 512]
                k.stt(xc, xc, ALPHA, bk.t[0:n, :], ALU.mult, ALU.add, rd=(xr[si_], bk), wr=(xr[si_],))
        for si_, (t0, n) in enumerate(subs):
            xb = xr[si_]
            for c in range(4):
                k.op(DVE, lambda c=c: nc.vector.bn_stats(out=stat.t[0:n, c, :], in_=xb.t[0:n, c * 512:(c + 1) * 512]), rd=(xb,), wr=(stat,))
            k.op(DVE, lambda: nc.vector.bn_aggr(out=mv_.t[0:n, 0:2], in_=stat.t[0:n, :, :].rearrange("p a b -> p (a b)")), rd=(stat,), wr=(mv_,))
            k.ts(DVE, mv_.t[0:n, 2:3], mv_.t[0:n, 1:2], EPS, None, ALU.add, None, rd=(mv_,), wr=(mv_,))
            k.tt(POOL, mv_.t[0:n, 2:3], mv_.t[0:n, 2:3], m05.t[0:n, :], ALU.pow, rd=(mv_, m05), wr=(mv_,))
            k.ts(DVE, mv_.t[0:n, 3:4], mv_.t[0:n, 0:1], mv_.t[0:n, 2:3], -1.0, ALU.mult, ALU.mult, rd=(mv_,), wr=(mv_,))
            k.actf(xb.t[0:n, :], xb.t[0:n, :], AF.Identity, rd=(xb, mv_), wr=(xb,), scale=mv_.t[0:n, 2:3], bias=mv_.t[0:n, 3:4])
            k.tt(DVE, xb.t[0:n, :], xb.t[0:n, :], gam.t[0:n, :], ALU.mult, rd=(xb, gam), wr=(xb,))
            k.tt(POOL, xb.t[0:n, :], xb.t[0:n, :], bet.t[0:n, :], ALU.add, rd=(xb, bet), wr=(xb,))
            k.dma(ACT, y_dst[t0:t0 + n, :], xb.t[0:n, :], rd=(xb,), sbuf=xb, is_out=True)

    def mem_transposes(src_bufs, n_sub, dstT):
        for s_ in range(n_sub):
            for g in range(2):
                bk = getbank()
                for i in range(4):
                    ch = 4 * g + i
                    k.tr(bk.t[:, i * 128:(i + 1) * 128], src_bufs[s_].t[:, ch * 128:(ch + 1) * 128], ident.t[:, :],
                         rd=(src_bufs[s_], ident), wr=(bk,), sig=(i == 3))
                evac(rr_eng([ACT, DVE]), dstT.t[:, 4 * g:4 * g + 4, s_ * 128:(s_ + 1) * 128],
                     bk.t[:, :].rearrange("p (a b) -> p a b", b=128), rd=(bk,), wr=(dstT,))

    def phase_mem_prompt():
        with ExitStack() as pes:
            subs2 = [(0, 128), (128, 128)]
            build_xT(mem, subs2, pes)
            ws = WStream([U_MK, U_MK + 1, U_MV, U_MV + 1], jit=True)
            mtm = [k.sb("mtm%d" % i, [128, 1024], F32, pes) for i in range(2)]
            for which in range(2):
                for half in range(2):
                    ws.prefetch()
                    slot = ws.get()
                    ws.prefetch()
                    for si_, sub in enumerate(subs2):
                        bk = getbank()
                        proj_tm(slot, sub, bk)
                        evac(rr_eng([ACT, DVE]), mtm[si_].t[:, half * 512:(half + 1) * 512], bk.t[:, :], rd=(bk,), wr=(mtm[si_],))
                for si_, (t0, n) in enumerate(subs2):
                    dst = mk_p if which == 0 else mv_p
                    k.dma(ACT, dst[t0:t0 + n, :], mtm[si_].t[:, :], rd=(mtm[si_],), sbuf=mtm[si_], is_out=True)
                if which == 0:
                    mem_transposes(mtm, 2, mkT)
                else:
                    for si_ in range(2):
                        evac(rr_eng([ACT, DVE]), mvb.t[:, si_, :], mtm[si_].t[:, :], rd=(mtm[si_],), wr=(mvb,))
            k.barrier()

    KTS_v = KTS
    VS_v = VS.rearrange("i p (s e) -> i p s e", e=130)
    subs1 = [(0, 64)]

    def phase_s_cache():
        with ExitStack() as pes:
            ckst = [[k.sb("ckst%d_%d" % (S, i), [128, 1024], F32, pes) for i in range(2)] for S in range(2)]
            cvst = [[k.sb("cvst%d_%d" % (S, i), [128, 1024], F32, pes) for i in range(2)] for S in range(2)]
            ktst = [k.sb("ktstS%d" % S, [128, 8, 256], BF16, pes) for S in range(2)]
            vst = [k.sb("vstS%d" % S, [128, 2, 8, 130], BF16, pes) for S in range(2)]
            for S in range(2):
                k.memset(POOL, vst[S].t[:, :, :, 128:129], 1.0, wr=(vst[S],))
                k.memset(POOL, vst[S].t[:, :, :, 129:130], 0.0, wr=(vst[S],))
            its = [(b, jj) for b in range(4) for jj in range(16)]

            def loads(it):
                b, jj = its[it]
                S = it % 2
                for s_ in range(2):
                    r0 = jj * 256 + s_ * 128
                    k.dma(SP, ckst[S][s_].t[:, :], ck[b, r0:r0 + 128, :], wr=(ckst[S][s_],), sbuf=ckst[S][s_])
                    k.dma(SP, cvst[S][s_].t[:, :], cv[b, r0:r0 + 128, :], wr=(cvst[S][s_],), sbuf=cvst[S][s_])

            loads(0)
            for it, (b, jj) in enumerate(its):
                if it + 1 < len(its):
                    loads(it + 1)
                S = it % 2
                for s_ in range(2):
                    for g in range(2):
                        bk = getbank()
                        for i in range(4):
                            h = 4 * g + i
                            k.tr(bk.t[:, i * 128:(i + 1) * 128], ckst[S][s_].t[:, h * 128:(h + 1) * 128], ident.t[:, :],
                                 rd=(ckst[S][s_], ident), wr=(bk,), sig=(i == 3))
                        evac(rr_eng([ACT, DVE]), ktst[S].t[:, 4 * g:4 * g + 4, s_ * 128:(s_ + 1) * 128],
                             bk.t[:, :].rearrange("p (a b) -> p a b", b=128), rd=(bk,), wr=(ktst[S],))
                    evac(rr_eng([POOL, DVE]), vst[S].t[:, s_, :, 0:128], cvst[S][s_].t[:, :].rearrange("p (h e) -> p h e", e=128),
                         rd=(cvst[S][s_],), wr=(vst[S],))
                group_store([(KTS_v[b * 8 + h, :, jj * 256:(jj + 1) * 256], ktst[S].t[:, h, :]) for h in range(8)],
                            ktst[S], [kts_b[b * 8 + h] for h in range(8)])
                group_store([(VS_v[b * 8 + h, :, 2 * jj:2 * jj + 2, :], vst[S].t[:, :, h, :]) for h in range(8)],
                            vst[S], [vs_b[b * 8 + h] for h in range(8)])
            k.barrier()

    cache = {"step": 0, "sets": None}
    NSTEP = 4 * 32

    def cache_alloc(es_):
        sets = []
        for S in range(2):
            d = dict(ck=k.sb("cck%d" % S, [128, 1024], F32, es_), cv=k.sb("ccv%d" % S, [128, 1024], F32, es_),
                     kt=k.sb("ckt%d" % S, [128, 8, 128], BF16, es_), v=k.sb("cvv%d" % S, [128, 8, 130], BF16, es_))
            k.memset(POOL, d["v"].t[:, :, 128:129], 1.0, wr=(d["v"],))
            k.memset(POOL, d["v"].t[:, :, 129:130], 0.0, wr=(d["v"],))
            sets.append(d)
        cache["sets"] = sets
        k.local_bufs = [b for b in k.local_bufs if all(b is not x for d in sets for x in d.values())]

    def cache_loads(s_):
        if s_ >= NSTEP:
            return
        b, r = divmod(s_, 32)
        d = cache["sets"][s_ % 2]
        k.dma(SP, d["ck"].t[:, :], ck[b, r * 128:(r + 1) * 128, :], wr=(d["ck"],), sbuf=d["ck"])
        k.dma(SP, d["cv"].t[:, :], cv[b, r * 128:(r + 1) * 128, :], wr=(d["cv"],), sbuf=d["cv"])

    def cache_step():
        s_ = cache["step"]
        if s_ >= NSTEP or cache["sets"] is None:
            return
        if s_ == 0:
            cache_loads(0)
        cache_loads(s_ + 1)
        b, r = divmod(s_, 32)
        d = cache["sets"][s_ % 2]
        for g in range(2):
            bk = getbank()
            for i in range(4):
                h = 4 * g + i
                k.tr(bk.t[:, i * 128:(i + 1) * 128], d["ck"].t[:, h * 128:(h + 1) * 128], ident.t[:, :],
                     rd=(d["ck"], ident), wr=(bk,), sig=(i == 3))
            evac(DVE, d["kt"].t[:, 4 * g:4 * g + 4, :], bk.t[:, :].rearrange("p (a b) -> p a b", b=128), rd=(bk,), wr=(d["kt"],))
        evac(POOL, d["v"].t[:, :, 0:128], d["cv"].t[:, :].rearrange("p (h e) -> p h e", e=128), rd=(d["cv"],), wr=(d["v"],))
        hb = [kts_b[b * 8 + h] for h in range(8)]
        ev = k.dma(SP, KTS_v[b * 8:(b + 1) * 8, :, r * 128:(r + 1) * 128].rearrange("h p t -> p h t"), d["kt"].t[:, :, :],
                   rd=(d["kt"],), wr=(), sbuf=d["kt"])
        for x in hb:
            x.w = ev
            x.r = {}
        vb_ = [vs_b[b * 8 + h] for h in range(8)]
        ev = k.dma(SP, VS_v[b * 8:(b + 1) * 8, :, r, :].rearrange("h p e -> p h e"), d["v"].t[:, :, :],
                   rd=(d["v"],), wr=(), sbuf=d["v"])
        for x in vb_:
            x.w = ev
            x.r = {}
        cache["step"] = s_ + 1

    def attn_sample(pes):
        KTh = [k.sb("KThS%d" % i, [128, SEQ + 16], BF16, pes) for i in range(2)]
        Vh = [k.sb("VhS%d" % i, [128, 33, 130], BF16, pes) for i in range(2)]
        PT = [k.sb("PTS%d" % i, [128, 512], BF16, pes) for i in range(2)]
        PTt = [k.sb("PTtS%d" % i, [16, 16], BF16, pes) for i in range(2)]
        o1b = [k.sb("o1bS%d" % i, [128, 128], F32, pes) for i in range(2)]
        o2b = [k.sb("o2bS%d" % i, [128, 128], F32, pes) for i in range(2)]
        onb = [k.sb("onbS%d" % i, [128, 128], F32, pes) for i in range(2)]
        stb = [k.sb("stbS%d" % i, [128, 4], F32, pes) for i in range(2)]

        def load(i):
            k.dma(SP, KTh[i % 2].t[:, :], KTS_v[i, :, :], rd=(kts_b[i],), wr=(KTh[i % 2],), sbuf=KTh[i % 2])
            k.dma(SP, Vh[i % 2].t[:, :, :], VS_v[i, :, :, :], rd=(vs_b[i],), wr=(Vh[i % 2],), sbuf=Vh[i % 2])

        load(0)
        pi = 0
        for i in range(32):
            b, h = i // 8, i % 8
            if i + 1 < 32:
                load(i + 1)
            KT, V = KTh[i % 2], Vh[i % 2]
            qc = slice(b * 16, (b + 1) * 16)
            for m in range(2):
                ms = slice(m * 64, (m + 1) * 64)
                pt, ptt = PT[pi % 2], PTt[pi % 2]
                pi += 1
                sbk = getbank()
                for kt in range(32):
                    k.mm(sbk.t[:, kt * 16:(kt + 1) * 16], KT.t[:, kt * 128:(kt + 1) * 128], qTz[m].t[:, h, qc], True, True,
                         rd=(KT, qTz[m]), wr=(sbk,), sig=(kt == 31))
                k.actf(pt.t[:, :], sbk.t[:, :], AF.Exp, rd=(sbk,), wr=(pt,), scale=0.125)
                sbt = getbank()
                k.mm(sbt.t[0:16, 0:16], KT.t[:, SEQ:SEQ + 16], qTz[m].t[:, h, qc], True, True, rd=(KT, qTz[m]), wr=(sbt,), sig=True)
                k.actf(ptt.t[:, :], sbt.t[0:16, 0:16], AF.Exp, rd=(sbt,), wr=(ptt,), scale=0.125)
                O = getbank()
                for kt in range(32):
                    k.mm(O.t[0:16, 0:130], pt.t[:, kt * 16:(kt + 1) * 16], V.t[:, kt, :], kt == 0, False, rd=(pt, V), wr=(O,), sig=False)
                k.mm(O.t[0:16, 0:130], ptt.t[:, :], V.t[0:16, 32, :], False, True, rd=(ptt, V), wr=(O,), sig=True)
                attn_post(O, 16, m, h, b * 16, o1b[i % 2], o2b[i % 2], onb[i % 2], stb[i % 2])

    def phase_b_sample(ws):
        for g in range(2):
            with ExitStack() as pes:
                qh = k.sb("qhS", [128, 4, 64], F32, pes)
                ff = k.sb("ffS", [128, 4, 64], F32, pes)
                vB = k.sb("vBS", [128, 1, 512], BF16, pes)
                vBm = k.sb("vBmS", [64, 4, 512], BF16, pes)
                gz = None
                og = k.sb("ogS", [128, 4, 64], BF16, pes)
                qdT = k.sb("qdTS", [128, 4, 64], BF16, pes)
                kdT = k.sb("kdTS", [128, 4, 64], BF16, pes)
                qsZ = k.sb("qsZS", [128, 4, 4, 64], BF16, pes)
                ks_tm = k.sb("ks_tmS", [64, 4, 128], BF16, pes)
                decay = k.sb("decayS", [128, 4, 4], F32, pes)
                tmp = [k.sb("htmpS%d" % i, [128, 64], F32, pes) for i in range(6)]
                scT = [k.sb("scTS%d" % i, [64, 64], BF16, pes) for i in range(2)]
                onb = [k.sb("onbBS%d" % i, [128, 128], F32, pes) for i in range(2)]
                stb = k.sb("stbBS", [128, 8], F32, pes)
                S0f = [k.sb("S0f%d" % i, [128, 128], F32, pes) for i in range(8)]
                S0b = [k.sb("S0b%d" % i, [128, 128], BF16, pes) for i in range(8)]
                k.memset(POOL, qsZ.t[:], 0.0, wr=(qsZ,))
                ksf = [k.sb("ksfS%d" % i, [128, 64], F32, pes) for i in range(4)]
                for b in []:
                    k.ts(DVE, vBm.t[0:64, b, :], vB.t[0:64, 0, :], rowmask.t[0:64, b:b + 1], None, ALU.mult, None,
                         rd=(vB, rowmask), wr=(vBm,))

                def qs_writer(hh, q_ap, e3_ap, qbuf, ebuf):
                    for b in range(4):
                        c_ = slice(b * 16, (b + 1) * 16)
                        k.tt(DVE, qsZ.t[:, hh, b, c_], q_ap[:, c_], e3_ap[:, c_], ALU.mult, rd=(qbuf, ebuf), wr=(qsZ,))

                def ks_writer(hh, Tg):
                    bk = getbank()
                    k.tr(bk.t[0:64, 0:128], Tg.t[:, 0:64], ident.t[:, :], rd=(Tg, ident), wr=(bk,))
                    evac(rr_eng([ACT, DVE]), ks_tm.t[0:64, hh, :], bk.t[0:64, 0:128], rd=(bk,), wr=(ks_tm,))

                preps = [lambda hh=hh: hgrn_prep(4 * g + hh, hh, 64, 16, qh, ff, tmp, rmask16, qdT, kdT, qs_writer, ksf[hh], decay)
                         for hh in range(4)]
                hgrn_proj(ws, g, 64, subs1, qh, ff, vB, gz, og, tmp, preps)
                for b in range(4):
                    k.ts(DVE, vBm.t[0:64, b, :], vB.t[0:64, 0, :], rowmask.t[0:64, b:b + 1], None, ALU.mult, None,
                         rd=(vB, rowmask), wr=(vBm,))
                for hh in range(4):
                    ks_writer(hh, ksf[hh])
                for hh in range(4):
                    h = 4 * g + hh
                    hc = slice(hh * 128, (hh + 1) * 128)
                    for b in range(4):
                        sf, sbb = S0f[(hh % 2) * 4 + b], S0b[(hh % 2) * 4 + b]
                        k.dma(SP, sf.t[:, :], s0[b * 8 + h], wr=(sf,), sbuf=sf)
                        evac(rr_eng([ACT, POOL]), sbb.t[:, :], sf.t[:, :], rd=(sf,), wr=(sbb,))
                    scb = getbank()
                    k.mm(scb.t[0:64, 0:64], kdT.t[:, hh, 0:64], qdT.t[:, hh, 0:64], True, True, rd=(kdT, qdT), wr=(scb,), sig=True)
                    sc_ = scT[hh % 2]
                    k.tt(DVE, sc_.t[:, :], scb.t[0:64, 0:64], tri16.t[:, :], ALU.mult, rd=(scb, tri16), wr=(sc_,))
                    obk = getbank()
                    k.mm(obk.t[0:64, 0:128], sc_.t[:, :], vB.t[0:64, 0, hc], True, False, rd=(sc_, vB), wr=(obk,), sig=False)
                    for b in range(4):
                        sbb = S0b[(hh % 2) * 4 + b]
                        k.mm(obk.t[0:64, 0:128], qsZ.t[:, hh, b, :], sbb.t[:, :], False, b == 3, rd=(qsZ, sbb), wr=(obk,), sig=(b == 3))
                    dsb = getbank()
                    for b in range(4):
                        k.mm(dsb.t[:, b * 128:(b + 1) * 128], ks_tm.t[0:64, hh, :], vBm.t[0:64, b, hc], True, True,
                             rd=(ks_tm, vBm), wr=(dsb,), sig=(b == 3))
                    for b in range(4):
                        sf = S0f[(hh % 2) * 4 + b]
                        k.stt(sf.t[:, :], sf.t[:, :], decay.t[:, hh, b:b + 1], dsb.t[:, b * 128:(b + 1) * 128], ALU.mult, ALU.add,
                              rd=(sf, decay, dsb), wr=(sf,))
                        k.dma(ACT, hg_s[b * 8 + h], sf.t[:, :], rd=(sf,), sbuf=sf, is_out=True)
                    ob_post(obk, 0, 64, h, og.t[:, hh, 0:64], og, 0, onb[hh % 2], stb, hh)
                k.barrier()

    def run_sample():
        ws = WStream(TILE_UNITS)
        outer = ExitStack()
        alloc_q_za(outer)
        with ExitStack() as pes:
            build_xT(xs, subs1, pes)

            def store_kt(ktst):
                group_store([(KTS_v[b * 8 + h, :, SEQ:SEQ + 16], ktst.t[:, h, b * 16:(b + 1) * 16])
                             for b in range(4) for h in range(8)], ktst, kts_b)

            def store_v(vst):
                group_store([(VS_v[b * 8 + h, 0:16, 32, :], vst.t[b * 16:(b + 1) * 16, 0, h, :])
                             for b in range(4) for h in range(8)], vst, vs_b)

            with nc.allow_non_contiguous_dma(reason="small per-sequence K^T column writes"):
                phase_a(ws, subs1, 64, [32], k_s, v_s, store_kt, store_v, pes)
            phase_z(ws, 64)
            k.barrier()
        with ExitStack() as pes:
            attn_sample(pes)
            k.barrier()
        outer.close()
        phase_b_sample(ws)
        with ExitStack() as pes:
            mkTs = [k.sb("mkTs%d" % b, [128, 8, 256], BF16, pes) for b in range(4)]
            mvbs = [k.sb("mvbs%d" % b, [128, 2, 1024], BF16, pes) for b in range(4)]
            mst = [k.sb("mst%d" % i, [128, 1024], F32, pes) for i in range(4)]
            for b in range(4):
                for s_ in range(2):
                    k.dma(SP, mst[s_].t[:, :], cmk[b, s_ * 128:(s_ + 1) * 128, :], wr=(mst[s_],), sbuf=mst[s_])
                    k.dma(SP, mst[2 + s_].t[:, :], cmv[b, s_ * 128:(s_ + 1) * 128, :], wr=(mst[2 + s_],), sbuf=mst[2 + s_])
                mem_transposes(mst[0:2], 2, mkTs[b])
                for s_ in range(2):
                    evac(rr_eng([ACT, DVE]), mvbs[b].t[:, s_, :], mst[2 + s_].t[:, :], rd=(mst[2 + s_],), wr=(mvbs[b],))
            phase_c(ws, 64, [(b * 16, 16, mkTs[b], mvbs[b]) for b in range(4)], pes)
            k.barrier()
        with ExitStack() as pes:
            phase_merge_out(ws, 64, subs1, xs, y_s, pes)
            k.barrier()

    def alloc_q_za(pes):
        for m in range(2):
            qTz[m] = k.sb("qTz%d" % m, [128, 8, T], BF16, pes)
        zaTh[0] = k.sb("zaT", [128, 8, T], BF16, pes)
        k.memset(POOL, qTz[0].t[64:128, :, :], 0.0, wr=(qTz[0],))
        k.memset(POOL, qTz[1].t[0:64, :, :], 0.0, wr=(qTz[1],))

    KTP_v = KTP
    VP_v = VP.rearrange("h p (s e) -> h p s e", e=130)
    subs4 = [(i * 128, 128) for i in range(4)]
    TILE_UNITS = list(range(0, 42))

    for h in range(8):
        k.memset(POOL, Sst[h].t[:], 0.0, wr=(Sst[h],))
        for p_ in range(2):
            k.memset(POOL, Sbf[h][p_].t[:], 0.0, wr=(Sbf[h][p_],))

    es_stg = ExitStack()
    for i in range(4):
        stg.append(k.sb("stg%d" % i, [128, 1024], F32, es_stg))
    k.local_bufs = [b for b in k.local_bufs if all(b is not x for x in stg)]
    es_cache = ExitStack()
    phase_mem_prompt()
    if not JIT_TILE0:
        for u in range(0, 42):
            wconvert(u)
        k.barrier()

    NT_RUN = NT
    for j in range(NT_RUN):
        if j == 1:
            es_stg.close()
            with nc.allow_non_contiguous_dma(reason="per-head scatter of converted cache tiles"):
                cache_alloc(es_cache)
        ws = WStream(TILE_UNITS, jit=(j == 0 and JIT_TILE0))
        if j >= 1:
            ws.bg = cache_step
        outer = ExitStack()
        alloc_q_za(outer)
        with ExitStack() as pes:
            build_xT(xp[j * T:(j + 1) * T, :], subs4, pes)

            def store_kt(ktst, j=j):
                group_store([(KTP_v[h, :, j * T:(j + 1) * T], ktst.t[:, h, 0:T]) for h in range(8)], ktst, ktp_b)

            def store_v(vst, j=j):
                group_store([(VP_v[h, :, 4 * j:4 * j + 4, :], vst.t[:, :, h, :]) for h in range(8)], vst, vp_b)

            phase_a(ws, subs4, T, [4 * j + i for i in range(4)],
                    k_p[j * T:(j + 1) * T, :], v_p[j * T:(j + 1) * T, :], store_kt, store_v, pes)
            phase_z(ws, T)
            k.barrier()
        with ExitStack() as pes:
            attn_prompt(j, pes)
            k.barrier()
        outer.close()
        phase_b_prompt(ws, j)
        with ExitStack() as pes:
            phase_c(ws, T, [(0, T, mkT, mvb)], pes)
            k.barrier()
        with ExitStack() as pes:
            phase_merge_out(ws, T, subs4, xp[j * T:(j + 1) * T, :], y_p[j * T:(j + 1) * T, :], pes)
            k.barrier()
    for h in range(8):
        k.dma(ACT, hg_p[h], Sst[h].t[:], rd=(Sst[h],), sbuf=Sst[h], is_out=True)
    while cache["step"] < NSTEP:
        cache_step()
    k.barrier()
    es_cache.close()
    if RUN_SAMPLE:
        run_sample()

    k.finish()
    es.close()
    return nc


_CACHE = {}


def _consts():
    p = np.arange(128)
    inv = np.power(500000.0, -np.arange(0, 16, 2, dtype=np.float32) / 16.0).astype(np.float32)
    rope = np.zeros((128, 33, 16), np.float32)
    for S in range(33):
        pos = (128 * S + p) if S < 32 else (4096 + (p % 16))
        ang = pos.astype(np.float32)[:, None] * inv[None, :]
        rope[:, S, 0:8] = np.cos(ang)
        rope[:, S, 8:16] = np.sin(ang)
    t = np.arange(64)
    tri = ((p[:, None] % 64) <= t[None, :]).astype(np.float32)
    rmask = (np.arange(512) % 64 != 0).astype(np.float32)[None, :].repeat(128, 0)
    rmask16 = (np.arange(64) % 16 != 0).astype(np.float32)[None, :].repeat(128, 0)
    s = np.arange(64)
    tri16 = ((s[:, None] // 16 == s[None, :] // 16) & (s[:, None] <= s[None, :])).astype(np.float32)
    rowmask = (s[:, None] // 16 == np.arange(4)[None, :]).astype(np.float32)
    return {
        "c_ident": np.eye(128, dtype=np.float32),
        "c_rope": np.ascontiguousarray(rope.reshape(128, 33 * 16)),
        "c_tri": np.ascontiguousarray(tri),
        "c_rmask": np.ascontiguousarray(rmask),
        "c_rmask16": np.ascontiguousarray(rmask16),
        "c_tri16": np.ascontiguousarray(tri16),
        "c_rowmask": np.ascontiguousarray(rowmask),
    }


def kernel(x_prompt, x_sample, cache_attn_k, cache_attn_v, state_hgrn, cache_mem_k, cache_mem_v,
           mem_prompt, w_in, lambda_q1, lambda_k1, lambda_q2, lambda_k2, attn_sub_norm,
           hgrn_lb_logits, hgrn_norm, w_mem_k, w_mem_v, w_branch_a, w_branch_b, w_branch_c,
           w_out, ln_gamma, ln_beta):
    f = lambda a: np.ascontiguousarray(np.asarray(a, dtype=np.float32))
    if "nc" not in _CACHE:
        _CACHE["nc"] = build_program()
    nc = _CACHE["nc"]
    consts = _consts()
    shared = {
        "w_in": f(w_in)[0], "w_mk": f(w_mem_k)[0], "w_mv": f(w_mem_v)[0],
        "w_a": f(w_branch_a)[0], "w_b": f(w_branch_b)[0], "w_c": f(w_branch_c)[0], "w_o": f(w_out)[0],
        "lam4": np.ascontiguousarray(np.concatenate([f(lambda_q1), f(lambda_k1), f(lambda_q2), f(lambda_k2)], 0)),
        "subn": f(attn_sub_norm), "lbl": f(hgrn_lb_logits), "hgn": f(hgrn_norm),
        "lng": f(ln_gamma), "lnb": f(ln_beta),
    }
    shared.update(consts)
    xp_, xs_ = f(x_prompt), f(x_sample)
    ck_, cv_ = f(cache_attn_k)[0], f(cache_attn_v)[0]
    s0_, cmk_, cmv_, mem_ = f(state_hgrn)[0], f(cache_mem_k)[0], f(cache_mem_v)[0], f(mem_prompt)
    in_maps = []
    for c in range(8):
        m = dict(shared)
        m["xp"] = xp_[c]
        m["xs"] = xs_[4 * c:4 * c + 4].reshape(64, D)
        m["ck"] = ck_[4 * c:4 * c + 4].reshape(4, SEQ, 1024)
        m["cv"] = cv_[4 * c:4 * c + 4].reshape(4, SEQ, 1024)
        m["s0"] = s0_[4 * c:4 * c + 4].reshape(32, 128, 128)
        m["cmk"] = cmk_[4 * c:4 * c + 4].reshape(4, 256, 1024)
        m["cmv"] = cmv_[4 * c:4 * c + 4].reshape(4, 256, 1024)
        m["mem"] = mem_[c]
        in_maps.append(m)
    res = run_bass_kernel_spmd(nc, in_maps, core_ids=list(range(8)))
    R = res.results
    cat = lambda name: np.stack([np.asarray(R[c][name]) for c in range(8)], 0)
    y_prompt = cat("y_p")
    y_sample = cat("y_s").reshape(32, 16, D)
    k_prompt = cat("k_p").reshape(1, 8, SEQ, 8, 2, 64)
    v_prompt = cat("v_p").reshape(1, 8, SEQ, 8, 128)
    hgrn_prompt = cat("hg_p").reshape(1, 8, 8, 128, 128)
    mem_k_prompt = cat("mk_p").reshape(1, 8, 256, 4, 256)
    mem_v_prompt = cat("mv_p").reshape(1, 8, 256, 4, 256)
    k_sample = cat("k_s").reshape(1, 32, 16, 8, 2, 64)
    v_sample = cat("v_s").reshape(1, 32, 16, 8, 128)
    hgrn_sample = cat("hg_s").reshape(1, 32, 8, 128, 128)
    return (y_prompt, y_sample, k_prompt, v_prompt, hgrn_prompt, mem_k_prompt, mem_v_prompt,
            k_sample, v_sample, hgrn_sample)
```

```python
import numpy as np
from contextlib import ExitStack
import concourse.bass as bass
import concourse.mybir as mybir
from concourse.bass_utils import run_bass_kernel_spmd

F32 = mybir.dt.float32
BF16 = mybir.dt.bfloat16
AF = mybir.ActivationFunctionType
ALU = mybir.AluOpType

D = 2048
SEQ = 4096
NIN = 17408
T = 512
NT = SEQ // T
ALPHA = 2.0 ** 0.25
EPS = 1e-5
LAM_INIT = 0.8 - 0.6 * 1.0
NU_BIG = 22
U_MERGE = 22
U_OUT = 38
U_MK = 42
U_MV = 44
NUNITS = 46
RUN_SAMPLE = True
JIT_TILE0 = True
USZ = 9216
BIG_ORDER = [0, 1, 2, 3, 4, 5, 6, 7, 8, 10, 12, 14, 16, 9, 11, 13, 15, 17, 18, 19, 20, 21]


class Sem:
    def __init__(self, h):
        self.h = h
        self.n = 0


class Buf:
    def __init__(self, name, t=None):
        self.name = name
        self.t = t
        self.w = None
        self.r = {}
        self.sem = None


class Eng:
    def __init__(self, name, eng, sem):
        self.name = name
        self.eng = eng
        self.sem = sem
        self.seen = {}


class K:
    def __init__(self, nc, es):
        self.nc = nc
        self.es = es
        self.nsem = 0
        self.pe = Eng("pe", nc.tensor, self.new_sem("pe"))
        self.act = Eng("act", nc.scalar, self.new_sem("act"))
        self.dve = Eng("dve", nc.vector, self.new_sem("dve"))
        self.pool = Eng("pool", nc.gpsimd, self.new_sem("pool"))
        self.sp = Eng("sp", nc.sync, self.new_sem("sp"))
        self.engines = [self.pe, self.act, self.dve, self.pool, self.sp]
        self.dma_sems = []
        self.out_evs = {}
        self.rr = 0
        self.cast_rr = 0
        self.local_bufs = []
        self.sem_pool = []

    def new_sem(self, name):
        self.nsem += 1
        return Sem(self.es.enter_context(self.nc.semaphore("s%d_%s" % (self.nsem, name))))

    def sb(self, name, shape, dt, es=None):
        self.nbuf = getattr(self, "nbuf", 0) + 1
        name = "%s_%d" % (name, self.nbuf)
        t = (es or self.es).enter_context(self.nc.sbuf_tensor(name, list(shape), dt))
        b = Buf(name, t)
        if es is not None:
            self.local_bufs.append(b)
        return b

    def wait(self, E, ev):
        sem, val = ev
        if E is self.pe and sem is E.sem:
            return
        if E.seen.get(sem, 0) >= val:
            return
        E.eng.wait_ge(sem.h, val)
        E.seen[sem] = val

    def _deps(self, E, rd, wr):
        for b in rd:
            if b.w is not None:
                for ev in (b.w if isinstance(b.w, list) else [b.w]):
                    self.wait(E, ev)
        for b in wr:
            if b.w is not None:
                for ev in (b.w if isinstance(b.w, list) else [b.w]):
                    self.wait(E, ev)
            for sem, val in b.r.items():
                self.wait(E, (sem, val))

    def _record(self, ev, rd, wr):
        sem, val = ev
        for b in rd:
            if b.r.get(sem, 0) < val:
                b.r[sem] = val
        for b in wr:
            b.w = ev
            b.r = {}

    def op(self, E, fn, rd=(), wr=(), sig=True):
        self._deps(E, rd, wr)
        inst = fn()
        if sig:
            E.sem.n += 1
            inst.then_inc(E.sem.h, 1)
            ev = (E.sem, E.sem.n)
        else:
            ev = (E.sem, E.sem.n + 1)
        self._record(ev, rd, wr)
        return ev

    def dma(self, Q, out, in_, rd=(), wr=(), sbuf=None, is_out=False):
        self._deps(Q, rd, wr)
        inst = Q.eng.dma_start(out=out, in_=in_)
        if sbuf.sem is None:
            if self.sem_pool:
                sbuf.sem = self.sem_pool.pop()
            else:
                sbuf.sem = self.new_sem("d%d" % len(self.dma_sems))
                self.dma_sems.append(sbuf.sem)
        sbuf.sem.n += 16
        inst.then_inc(sbuf.sem.h, 16)
        ev = (sbuf.sem, sbuf.sem.n)
        self._record(ev, rd, wr)
        if is_out:
            self.out_evs[sbuf.sem] = sbuf.sem.n
        return ev

    def barrier(self):
        self._barrier()
        for b in self.local_bufs:
            if b.sem is not None:
                self.sem_pool.append(b.sem)
                b.sem = None
        self.local_bufs = []

    def _barrier(self):
        evs = [(e.sem, e.sem.n) for e in self.engines if e.sem.n > 0]
        evs += [(s, s.n) for s in self.dma_sems if s.n > 0]
        for E in self.engines:
            for ev in evs:
                if ev[0] is E.sem:
                    continue
                self.wait(E, ev)

    def finish(self):
        for sem, val in self.out_evs.items():
            self.wait(self.sp, (sem, val))
        for e in self.engines:
            if e is not self.sp and e.sem.n > 0:
                self.wait(self.sp, (e.sem, e.sem.n))

    def mm(self, out_ap, lhsT, rhs, start, stop, rd, wr, sig):
        nc = self.nc
        return self.op(self.pe, lambda: nc.tensor.matmul(out_ap, lhsT=lhsT, rhs=rhs, start=start, stop=stop),
                       rd=rd, wr=wr, sig=sig)

    def tr(self, out_ap, in_ap, ident_ap, rd, wr, sig=True):
        nc = self.nc
        return self.op(self.pe, lambda: nc.tensor.transpose(out_ap, in_ap, ident_ap), rd=rd, wr=wr, sig=sig)

    def actf(self, out_ap, in_ap, func, rd, wr, scale=None, bias=None, accum=None):
        nc = self.nc
        kw = {}
        if scale is not None:
            kw["scale"] = scale
        if bias is not None:
            kw["bias"] = bias
        if accum is not None:
            kw["accum_out"] = accum
        return self.op(self.act, lambda: nc.scalar.activation(out=out_ap, in_=in_ap, func=func, **kw), rd=rd, wr=wr)

    def tt(self, E, out_ap, in0, in1, op, rd, wr):
        return self.op(E, lambda: E.eng.tensor_tensor(out=out_ap, in0=in0, in1=in1, op=op), rd=rd, wr=wr)

    def ts(self, E, out_ap, in0, s1, s2, op0, op1, rd, wr):
        if op1 is None:
            return self.op(E, lambda: E.eng.tensor_scalar(out=out_ap, in0=in0, scalar1=s1, scalar2=None, op0=op0),
                           rd=rd, wr=wr)
        return self.op(E, lambda: E.eng.tensor_scalar(out=out_ap, in0=in0, scalar1=s1, scalar2=s2, op0=op0, op1=op1),
                       rd=rd, wr=wr)

    def stt(self, out_ap, in0, scalar, in1, op0, op1, rd, wr):
        nc = self.nc
        return self.op(self.dve, lambda: nc.vector.scalar_tensor_tensor(out=out_ap, in0=in0, scalar=scalar, in1=in1,
                                                                       op0=op0, op1=op1), rd=rd, wr=wr)

    def copy(self, E, out_ap, in_ap, rd, wr):
        if E is self.act:
            return self.op(E, lambda: self.nc.scalar.copy(out=out_ap, in_=in_ap), rd=rd, wr=wr)
        return self.op(E, lambda: E.eng.tensor_copy(out=out_ap, in_=in_ap), rd=rd, wr=wr)

    def memset(self, E, ap, val, wr):
        return self.op(E, lambda: E.eng.memset(ap, val), rd=(), wr=wr)


def build_program():
    nc = bass.Bass("TRN2", target_bir_lowering=False)
    es = ExitStack()
    k = K(nc, es)

    def din(name, shape):
        return nc.dram_tensor(name, list(shape), F32, kind="ExternalInput").ap()

    def dout(name, shape):
        return nc.dram_tensor(name, list(shape), F32, kind="ExternalOutput").ap()

    xp = din("xp", [SEQ, D])
    xs = din("xs", [64, D])
    ck = din("ck", [4, SEQ, 1024])
    cv = din("cv", [4, SEQ, 1024])
    s0 = din("s0", [32, 128, 128])
    cmk = din("cmk", [4, 256, 1024])
    cmv = din("cmv", [4, 256, 1024])
    mem = din("mem", [256, D])
    w_in = din("w_in", [D, NIN])
    w_mk = din("w_mk", [D, 1024])
    w_mv = din("w_mv", [D, 1024])
    w_a = din("w_a", [1024, D])
    w_b = din("w_b", [1024, D])
    w_c = din("w_c", [1024, D])
    w_o = din("w_o", [D, D])
    lam4 = din("lam4", [4, 64])
    subn = din("subn", [1, 128])
    lbl = din("lbl", [2, 1024])
    hgn = din("hgn", [1, 1024])
    lng = din("lng", [1, D])
    lnb = din("lnb", [1, D])
    c_ident = din("c_ident", [128, 128])
    c_rope = din("c_rope", [128, 33 * 16])
    c_tri = din("c_tri", [128, 64])
    c_rmask = din("c_rmask", [128, 512])
    c_rmask16 = din("c_rmask16", [128, 64])
    c_tri16 = din("c_tri16", [64, 64])
    c_rowmask = din("c_rowmask", [64, 4])

    y_p = dout("y_p", [SEQ, D])
    y_s = dout("y_s", [64, D])
    k_p = dout("k_p", [SEQ, 1024])
    v_p = dout("v_p", [SEQ, 1024])
    hg_p = dout("hg_p", [8, 128, 128])
    mk_p = dout("mk_p", [256, 1024])
    mv_p = dout("mv_p", [256, 1024])
    k_s = dout("k_s", [64, 1024])
    v_s = dout("v_s", [64, 1024])
    hg_s = dout("hg_s", [32, 128, 128])

    WS = nc.dram_tensor("ws_scratch", [NUNITS, 128, USZ], BF16, kind="Internal").ap()
    KTP = nc.dram_tensor("ktp_scratch", [8, 128, SEQ], BF16, kind="Internal").ap()
    VP = nc.dram_tensor("vp_scratch", [8, 128, 32 * 130], BF16, kind="Internal").ap()
    KTS = nc.dram_tensor("kts_scratch", [32, 128, SEQ + 16], BF16, kind="Internal").ap()
    VS = nc.dram_tensor("vs_scratch", [32, 128, 33 * 130], BF16, kind="Internal").ap()
    ws_b = [Buf("ws%d" % u) for u in range(NUNITS)]
    ktp_b = [Buf("ktp%d" % h) for h in range(8)]
    vp_b = [Buf("vp%d" % h) for h in range(8)]
    kts_b = [Buf("kts%d" % i) for i in range(32)]
    vs_b = [Buf("vs%d" % i) for i in range(32)]

    ident = k.sb("ident", [128, 128], F32)
    rope = k.sb("rope", [128, 33, 16], F32)
    tri = k.sb("tri", [128, 64], F32)
    rmask = k.sb("rmask", [128, 512], F32)
    rmask16 = k.sb("rmask16", [128, 64], F32)
    tri16 = k.sb("tri16", [64, 64], F32)
    rowmask = k.sb("rowmask", [64, 4], F32)
    gain = k.sb("gain", [128, 1024], F32)
    sn = k.sb("sn", [128, 128], F32)
    ones_bf = k.sb("ones_bf", [128, 128], BF16)
    lamw = k.sb("lamw", [128, 4, 64], F32)
    lamv = k.sb("lamv", [128, 8], F32)
    lbt = k.sb("lbt", [128, 2, 8], F32)
    lbv = k.sb("lbv", [128, 2, 8], F32)
    m05 = k.sb("m05", [128, 1], F32)
    W = [k.sb("wslot%d" % i, [128, USZ], BF16) for i in range(2)]
    xT = k.sb("xT", [128, 16, T], BF16)
    yaT = k.sb("yaT", [128, 8, T], BF16)
    ybT = k.sb("ybT", [128, 8, T], BF16)
    ycT = k.sb("ycT", [128, 8, T], BF16)
    qTz = [None, None]
    zaTh = [None]
    Sst = [k.sb("S%d" % h, [128, 128], F32) for h in range(8)]
    Sbf = [[k.sb("Sbf%d_%d" % (h, p), [128, 128], BF16) for p in range(2)] for h in range(8)]
    mkT = k.sb("mkT", [128, 8, 256], BF16)
    mvb = k.sb("mvb", [128, 2, 1024], BF16)

    PS = es.enter_context(nc.psum_tensor("psum_all", [128, 4096], F32))
    PS3 = PS[:, :].rearrange("p (b n) -> p b n", n=512)
    banks = [Buf("bank%d" % i, PS[:, i * 512:(i + 1) * 512]) for i in range(8)]
    st = {"brot": list(range(8)), "bi": 0, "wi": 0, "si": 0}

    def getbank():
        b = banks[st["brot"][st["bi"] % len(st["brot"])]]
        st["bi"] += 1
        return b

    def rr_eng(engs):
        k.rr += 1
        return engs[k.rr % len(engs)]

    SP, PE, ACT, DVE, POOL = k.sp, k.pe, k.act, k.dve, k.pool

    def load_const(buf, src, shape_ap=None):
        k.dma(SP, buf.t[:] if shape_ap is None else shape_ap, src, rd=(), wr=(buf,), sbuf=buf)

    load_const(ident, c_ident)
    k.dma(SP, rope.t[:].rearrange("p a b -> p (a b)"), c_rope, wr=(rope,), sbuf=rope)
    load_const(tri, c_tri)
    load_const(rmask, c_rmask)
    load_const(rmask16, c_rmask16)
    load_const(tri16, c_tri16)
    load_const(rowmask, c_rowmask)
    k.dma(SP, gain.t[:], hgn[0:1, :].to_broadcast([128, 1024]), wr=(gain,), sbuf=gain)
    k.dma(SP, sn.t[:], subn[0:1, :].to_broadcast([128, 128]), wr=(sn,), sbuf=sn)
    for i in range(4):
        k.dma(SP, lamw.t[:, i, :], lam4[i:i + 1, :].to_broadcast([128, 64]), wr=(lamw,), sbuf=lamw)
    with nc.allow_non_contiguous_dma(reason="tiny lb logits transpose load"):
        for l in range(2):
            k.dma(SP, lbt.t[:, l, :], lbl[l:l + 1, :].rearrange("o (h d) -> d (o h)", d=128), wr=(lbt,), sbuf=lbt)
    k.memset(DVE, ones_bf.t[:], 1.0, wr=(ones_bf,))
    k.memset(DVE, m05.t[:], -0.5, wr=(m05,))
    k.ts(DVE, sn.t[:], sn.t[:], 1.0 - LAM_INIT, None, ALU.mult, None, rd=(sn,), wr=(sn,))
    with ExitStack() as les:
        lt = k.sb("lam_tmp", [128, 64], F32, les)
        for j in range(2):
            k.tt(DVE, lt.t[:], lamw.t[:, 2 * j, :], lamw.t[:, 2 * j + 1, :], ALU.mult, rd=(lamw,), wr=(lt,))
            k.op(DVE, lambda j=j: nc.vector.reduce_sum(out=lamv.t[:, j:j + 1], in_=lt.t[:],
                                                       axis=mybir.AxisListType.X), rd=(lt,), wr=(lamv,))
        k.actf(lamv.t[:, 2:4], lamv.t[:, 0:2], AF.Exp, rd=(lamv,), wr=(lamv,))
        k.tt(DVE, lamv.t[:, 4:5], lamv.t[:, 3:4], lamv.t[:, 2:3], ALU.subtract, rd=(lamv,), wr=(lamv,))
        k.ts(DVE, lamv.t[:, 4:5], lamv.t[:, 4:5], -LAM_INIT, None, ALU.add, None, rd=(lamv,), wr=(lamv,))
        k.tt(DVE, lbv.t[:, 0, :], lbt.t[:, 0, :], lbt.t[:, 1, :], ALU.subtract, rd=(lbt,), wr=(lbv,))
        k.actf(lbv.t[:, 0, :], lbv.t[:, 0, :], AF.Sigmoid, rd=(lbv,), wr=(lbv,))
        k.ts(DVE, lbv.t[:, 1, :], lbv.t[:, 0, :], -1.0, 1.0, ALU.mult, ALU.add, rd=(lbv,), wr=(lbv,))
        k.barrier()
    neg_lam = lamv.t[:, 4:5]

    w_in_v = w_in.rearrange("(kc p) n -> p kc n", p=128)
    w_o_v = w_o.rearrange("(kc p) n -> p kc n", p=128)
    w_mk_v = w_mk.rearrange("(kc p) n -> p kc n", p=128)
    w_mv_v = w_mv.rearrange("(kc p) n -> p kc n", p=128)
    w_abc_v = [w.rearrange("(kc p) n -> p kc n", p=128) for w in (w_a, w_b, w_c)]

    def unit_pieces(u):
        if u < NU_BIG or u >= U_OUT:
            if u < NU_BIG:
                src, c0 = w_in_v, 512 * BIG_ORDER[u]
            elif u < U_MK:
                src, c0 = w_o_v, 512 * (u - U_OUT)
            elif u < U_MV:
                src, c0 = w_mk_v, 512 * (u - U_MK)
            else:
                src, c0 = w_mv_v, 512 * (u - U_MV)
            return [(1024 * q, [(src[:, 2 * q:2 * q + 2, c0:c0 + 512], 2, 512)]) for q in range(8)]
        oc = u - U_MERGE
        g = [w_in_v[:, :, 11264 + 2048 * b + 128 * oc: 11264 + 2048 * b + 128 * oc + 128] for b in range(3)]
        wv = [w_abc_v[b][:, :, 128 * oc:128 * oc + 128] for b in range(3)]
        pcs = []
        for b in range(3):
            for q in range(2):
                pcs.append((2048 * b + 1024 * q, [(g[b][:, 8 * q:8 * q + 8, :], 8, 128)]))
        for b in range(3):
            pcs.append((6144 + 1024 * b, [(wv[b], 8, 128)]))
        return pcs

    stg = []
    cast_seq = [DVE, ACT, DVE, ACT, POOL]

    def wconvert(u):
        slot = W[st["wi"] % 2]
        st["wi"] += 1
        last = {}
        for (off, parts) in unit_pieces(u):
            sg = stg[st["si"] % len(stg)]
            E = cast_seq[st["si"] % len(cast_seq)]
            st["si"] += 1
            o = 0
            for (src, a, b) in parts:
                k.dma(SP, sg.t[:, o:o + a * b].rearrange("p (a b) -> p a b", b=b), src, wr=(sg,), sbuf=sg)
                o += a * b
            k._deps(E, (), (slot,))
            last[E] = k.op(E, (lambda E=E, off=off, o=o, sg=sg: (nc.scalar.copy(out=slot.t[:, off:off + o], in_=sg.t[:, 0:o])
                                                               if E is ACT else
                                                               E.eng.tensor_copy(out=slot.t[:, off:off + o], in_=sg.t[:, 0:o]))),
                           rd=(sg,), wr=())
        slot.w = list(last.values())
        slot.r = {}
        nel = USZ if U_MERGE <= u < U_OUT else 8192
        k.dma(ACT, WS[u, :, 0:nel], slot.t[:, 0:nel], rd=(slot,), wr=(ws_b[u],), sbuf=slot)
        return slot

    def wload(u):
        slot = W[st["wi"] % 2]
        st["wi"] += 1
        nel = USZ if U_MERGE <= u < U_OUT else 8192
        k.dma(SP, slot.t[:, 0:nel], WS[u, :, 0:nel], rd=(ws_b[u],), wr=(slot,), sbuf=slot)
        return slot

    class WStream:
        def __init__(self, order, jit=False):
            self.order = list(order)
            self.pos = 0
            self.loaded = []
            self.jit = jit
            self.bg = None
            self.ncall = 0

        def _load(self):
            u = self.order[self.pos]
            self.loaded.append(wconvert(u) if self.jit else wload(u))
            self.pos += 1

        def prefetch(self):
            pass

        def get(self):
            if not self.loaded:
                self._load()
            s = self.loaded.pop(0)
            if self.pos < len(self.order):
                self._load()
            if self.bg is not None:
                self.ncall += 1
                if self.ncall % 2 == 0:
                    self.bg()
            return s

    def evac(E, out_ap, in_ap, rd, wr):
        k.copy(E, out_ap, in_ap, rd=rd, wr=wr)

    def build_xT(x_src, subs, pes):
        xst = [k.sb("xst%d" % i, [128, D], F32, pes) for i in range(2)]
        for si_, (t0, n) in enumerate(subs):
            xb = xst[si_ % 2]
            k.dma(SP, xb.t[0:n, :], x_src[t0:t0 + n, :], wr=(xb,), sbuf=xb)
            for g in range(4):
                bk = getbank()
                for i in range(4):
                    kc = 4 * g + i
                    k.tr(bk.t[:, i * 128:i * 128 + n], xb.t[0:n, kc * 128:(kc + 1) * 128], ident.t[0:n, 0:n],
                         rd=(xb, ident), wr=(bk,), sig=(i == 3))
                E = rr_eng([ACT, DVE])
                evac(E, xT.t[:, 4 * g:4 * g + 4, t0:t0 + n],
                     bk.t[:, :].rearrange("p (a b) -> p a b", b=128)[:, :, 0:n], rd=(bk,), wr=(xT,))

    def proj_tm(slot, sub, dst_bank):
        t0, n = sub
        wv = slot.t[:, 0:8192].rearrange("p (kc c) -> p kc c", c=512)
        for kc in range(16):
            k.mm(dst_bank.t[0:n, :], xT.t[:, kc, t0:t0 + n], wv[:, kc, :], kc == 0, kc == 15,
                 rd=(xT, slot), wr=(dst_bank,), sig=(kc == 15))

    def proj_fm(slot, cb, ntok, dst_bank):
        wv = slot.t[:, 0:8192].rearrange("p (kc c) -> p kc c", c=512)
        for kc in range(16):
            k.mm(dst_bank.t[:, 0:ntok], wv[:, kc, cb * 128:(cb + 1) * 128], xT.t[:, kc, 0:ntok], kc == 0, kc == 15,
                 rd=(xT, slot), wr=(dst_bank,), sig=(kc == 15))

    def rope_apply(tm, n, S):
        v = tm.t[0:n, :].rearrange("p (m d) -> p m d", d=64)
        x1, x2 = v[:, :, 0:8], v[:, :, 8:16]
        cs = rope.t[0:n, S, 0:8].unsqueeze(1).to_broadcast([n, 16, 8])
        sn_ = rope.t[0:n, S, 8:16].unsqueeze(1).to_broadcast([n, 16, 8])
        return x1, x2, cs, sn_

    def phase_a(ws, subs, ntok, rope_slots, k_out, v_out, store_kt, store_v, pes):
        tm = [k.sb("tm%d" % i, [128, 1024], F32, pes) for i in range(len(subs))]
        rt = [k.sb("ropet%d" % i, [128, 16, 8], F32, pes) for i in range(4)]
        ktst = k.sb("ktst", [128, 8, T], BF16, pes)
        vst = k.sb("vst", [128, 4, 8, 130], BF16, pes)
        k.memset(POOL, vst.t[:, :, :, 128:129], 1.0, wr=(vst,))
        k.memset(POOL, vst.t[:, :, :, 129:130], 0.0, wr=(vst,))
        for part in range(3):
            for half in range(2):
                ws.prefetch()
                slot = ws.get()
                ws.prefetch()
                for si_, sub in enumerate(subs):
                    bk = getbank()
                    proj_tm(slot, sub, bk)
                    n = sub[1]
                    E = rr_eng([ACT, DVE])
                    evac(E, tm[si_].t[0:n, half * 512:(half + 1) * 512], bk.t[0:n, :], rd=(bk,), wr=(tm[si_],))
            for si_, (t0, n) in enumerate(subs):
                tmb = tm[si_]
                if part < 2:
                    x1, x2, cs, sn_ = rope_apply(tmb, n, rope_slots[si_])
                    a, b_, c, d_ = [r.t[0:n] for r in rt]
                    rtb = tuple(rt)
                    k.tt(DVE, a, x1, cs, ALU.mult, rd=(tmb, rope), wr=(rt[0],))
                    k.tt(DVE, b_, x2, sn_, ALU.mult, rd=(tmb, rope), wr=(rt[1],))
                    k.tt(DVE, c, x2, cs, ALU.mult, rd=(tmb, rope), wr=(rt[2],))
                    k.tt(DVE, d_, x1, sn_, ALU.mult, rd=(tmb, rope), wr=(rt[3],))
                    k.tt(DVE, x1, a, b_, ALU.subtract, rd=(rt[0], rt[1]), wr=(tmb,))
                    k.tt(DVE, x2, c, d_, ALU.add, rd=(rt[2], rt[3]), wr=(tmb,))
                    if part == 1:
                        k.dma(ACT, k_out[t0:t0 + n, :], tmb.t[0:n, :], rd=(tmb,), sbuf=tmb, is_out=True)
                    for g in range(2):
                        bk = getbank()
                        for i in range(4):
                            h = 4 * g + i
                            k.tr(bk.t[:, i * 128:i * 128 + n], tmb.t[0:n, h * 128:(h + 1) * 128], ident.t[0:n, 0:n],
                                 rd=(tmb, ident), wr=(bk,), sig=(i == 3))
                        bv = bk.t[:, :].rearrange("p (a b) -> p a b", b=128)
                        if part == 0:
                            evac(ACT, qTz[0].t[0:64, 4 * g:4 * g + 4, t0:t0 + n], bv[0:64, :, 0:n], rd=(bk,), wr=(qTz[0],))
                            evac(DVE, qTz[1].t[64:128, 4 * g:4 * g + 4, t0:t0 + n], bv[64:128, :, 0:n], rd=(bk,), wr=(qTz[1],))
                        else:
                            E = rr_eng([ACT, DVE])
                            evac(E, ktst.t[:, 4 * g:4 * g + 4, t0:t0 + n], bv[:, :, 0:n], rd=(bk,), wr=(ktst,))
                else:
                    k.dma(ACT, v_out[t0:t0 + n, :], tmb.t[0:n, :], rd=(tmb,), sbuf=tmb, is_out=True)
                    E = rr_eng([ACT, DVE])
                    evac(E, vst.t[0:n, si_, :, 0:128], tmb.t[0:n, :].rearrange("p (h e) -> p h e", e=128),
                         rd=(tmb,), wr=(vst,))
            if part == 1:
                store_kt(ktst)
            if part == 2:
                store_v(vst)

    AX = mybir.AxisListType.X

    def group_store(dsts_srcs, src_buf, dst_bufs):
        ev = None
        for (dst, src) in dsts_srcs:
            ev = k.dma(ACT, dst, src, rd=(src_buf,), wr=(), sbuf=src_buf)
        for b in dst_bufs:
            for sem, val in list(b.r.items()):
                pass
            b.w = ev
            b.r = {}

    def rstd_from(ss_ap, buf, scale, n):
        k.ts(DVE, ss_ap, ss_ap, scale, EPS, ALU.mult, ALU.add, rd=(buf,), wr=(buf,))
        k.tt(POOL, ss_ap, ss_ap, m05.t[0:n, :], ALU.pow, rd=(buf, m05), wr=(buf,))

    def phase_z(ws, ntok):
        for half in range(2):
            ws.prefetch()
            slot = ws.get()
            ws.prefetch()
            for cb in range(4):
                bk = getbank()
                proj_fm(slot, cb, ntok, bk)
                k.actf(zaTh[0].t[:, half * 4 + cb, 0:ntok], bk.t[:, 0:ntok], AF.Silu, rd=(bk,), wr=(zaTh[0],))

    def attn_post(O, nq, m, h, qcol0, o1b, o2b, onb, stb):
        if m == 0:
            k.op(DVE, lambda: nc.vector.reciprocal(out=stb.t[0:nq, 0:1], in_=O.t[0:nq, 128:129]), rd=(O,), wr=(stb,))
            k.ts(DVE, o1b.t[0:nq, :], O.t[0:nq, 0:128], stb.t[0:nq, 0:1], None, ALU.mult, None, rd=(O, stb), wr=(o1b,))
            return
        k.op(DVE, lambda: nc.vector.reciprocal(out=stb.t[0:nq, 1:2], in_=O.t[0:nq, 128:129]), rd=(O,), wr=(stb,))
        k.tt(DVE, stb.t[0:nq, 1:2], stb.t[0:nq, 1:2], neg_lam[0:nq, :], ALU.mult, rd=(stb, lamv), wr=(stb,))
        k.stt(o2b.t[0:nq, :], O.t[0:nq, 0:128], stb.t[0:nq, 1:2], o1b.t[0:nq, :], ALU.mult, ALU.add,
              rd=(O, stb, o1b), wr=(o2b,))
        k.actf(onb.t[0:nq, :], o2b.t[0:nq, :], AF.Square, rd=(o2b,), wr=(onb, stb), accum=stb.t[0:nq, 2:3])
        rstd_from(stb.t[0:nq, 2:3], stb, 1.0 / 128.0, nq)
        k.stt(onb.t[0:nq, :], o2b.t[0:nq, :], stb.t[0:nq, 2:3], sn.t[0:nq, :], ALU.mult, ALU.mult,
              rd=(o2b, stb, sn), wr=(onb,))
        tb = getbank()
        k.tr(tb.t[:, 0:nq], onb.t[0:nq, :], ident.t[0:nq, 0:nq], rd=(onb, ident), wr=(tb,))
        k.tt(DVE, yaT.t[:, h, qcol0:qcol0 + nq], tb.t[:, 0:nq], zaTh[0].t[:, h, qcol0:qcol0 + nq], ALU.mult,
             rd=(tb, zaTh[0]), wr=(yaT,))

    def attn_prompt(j, pes):
        nk = 4 * (j + 1)
        nfull = 4 * j
        KTh = [k.sb("KTh%d" % i, [128, SEQ], BF16, pes) for i in range(2)]
        Vh = [k.sb("Vh%d" % i, [128, 32, 130], BF16, pes) for i in range(2)]
        PT = [k.sb("PT%d" % i, [128, 2, 512], BF16, pes) for i in range(3)]
        o1b = [k.sb("o1b%d" % i, [128, 128], F32, pes) for i in range(4)]
        o2b = [k.sb("o2b%d" % i, [128, 128], F32, pes) for i in range(2)]
        onb = [k.sb("onb%d" % i, [128, 128], F32, pes) for i in range(2)]
        stb = [k.sb("stb%d" % i, [128, 4], F32, pes) for i in range(4)]
        st["brot"] = [0, 1, 2, 3]
        ob = banks[4:8]

        def load(h):
            k.dma(SP, KTh[h % 2].t[:, 0:nk * 128], KTP_v[h, :, 0:nk * 128], rd=(ktp_b[h],), wr=(KTh[h % 2],), sbuf=KTh[h % 2])
            k.dma(SP, Vh[h % 2].t[:, 0:nk, :], VP_v[h, :, 0:nk, :], rd=(vp_b[h],), wr=(Vh[h % 2],), sbuf=Vh[h % 2])

        items = [(kt, kt + 1) for kt in range(0, nfull, 2)] + [(kt,) for kt in range(nfull, nk)]
        flat = [(h, m, it, idx == len(items) - 1) for h in range(8) for m in range(2) for idx, it in enumerate(items)]
        load(0)
        load(1)
        pend = []
        pi = 0

        def emit_pv(ent):
            h, m, item_, last, d_, pt_ = ent
            V = Vh[h % 2]
            for ii, kt_ in enumerate(item_):
                for qs in range(d_, 4):
                    k.mm(ob[qs].t[:, 0:130], pt_.t[:, ii, (qs - d_) * 128:(qs - d_ + 1) * 128], V.t[:, kt_, :],
                         kt_ == 0, kt_ == nfull + qs, rd=(pt_, V), wr=(ob[qs],),
                         sig=(qs == 3 and ii == len(item_) - 1))
            if last:
                for qs in range(4):
                    attn_post(ob[qs], 128, m, h, qs * 128, o1b[qs], o2b[qs % 2], onb[qs % 2], stb[qs])
                if m == 1 and h + 2 < 8:
                    load(h + 2)

        for (h, m, item, last) in flat:
            KT = KTh[h % 2]
            b0 = 2 * (pi % 2)
            pt = PT[pi % 3]
            pi += 1
            d = max(0, item[0] - nfull)
            q0 = 128 * d
            N = 512 - q0
            bks = [banks[b0 + ii] for ii in range(len(item))]
            for ii, kt in enumerate(item):
                k.mm(bks[ii].t[:, 0:N], KT.t[:, kt * 128:(kt + 1) * 128], qTz[m].t[:, h, q0:512], True, True,
                     rd=(KT, qTz[m]), wr=(bks[ii],), sig=True)
            if len(item) == 2:
                k.actf(pt.t[:, :, :], PS3[:, b0:b0 + 2, :], AF.Exp, rd=tuple(bks), wr=(pt,), scale=0.125)
            else:
                k.actf(pt.t[:, 0, 0:N], bks[0].t[:, 0:N], AF.Exp, rd=tuple(bks), wr=(pt,), scale=0.125)
                k.memset(POOL, pt.t[64:128, 0, 0:64], 0.0, wr=(pt,))
            pend.append((h, m, item, last, d, pt))
            if len(pend) > 2:
                emit_pv(pend.pop(0))
        while pend:
            emit_pv(pend.pop(0))
        st["brot"] = list(range(8))

    def hgrn_prep(h, hh, ntok, csz, qh, ff, tmp, rm, qdT, kdT, qs_writer, ksf, decay):
        nch = ntok // csz
        mid = (csz - 1) // 2
        tg, tk, tb_, t1, e1, e3 = [t_.t[:, 0:ntok] for t_ in tmp[0:6]]
        Tg, Tk, Tb, T1, E1b, E3b = tmp[0:6]
        k.actf(tg, ff.t[:, hh, 0:ntok], AF.Ln, rd=(ff,), wr=(Tg,))
        k.ts(DVE, tk, ff.t[:, hh, 0:ntok], -1.0, 1.0, ALU.mult, ALU.add, rd=(ff,), wr=(Tk,))
        k.op(DVE, lambda: nc.vector.tensor_tensor_scan(out=tb_, data0=rm.t[:, 0:ntok], data1=tg, initial=0.0,
                                                       op0=ALU.mult, op1=ALU.add), rd=(rm, Tg), wr=(Tb,))
        b3 = tb_.rearrange("p (c t) -> p c t", t=csz)
        k.tt(DVE, t1.rearrange("p (c t) -> p c t", t=csz), b3, b3[:, :, mid:mid + 1].to_broadcast([128, nch, csz]),
             ALU.subtract, rd=(Tb,), wr=(T1,))
        k.actf(e1, t1, AF.Exp, rd=(T1,), wr=(E1b,))
        k.tt(POOL, qdT.t[:, hh, 0:ntok], qh.t[:, hh, 0:ntok], e1, ALU.mult, rd=(qh, E1b), wr=(qdT,))
        k.actf(e1, t1, AF.Exp, rd=(T1,), wr=(E1b,), scale=-1.0)
        k.tt(POOL, kdT.t[:, hh, 0:ntok], tk, e1, ALU.mult, rd=(Tk, E1b), wr=(kdT,))
        k.actf(e3, tb_, AF.Exp, rd=(Tb,), wr=(E3b,))
        qs_writer(hh, qh.t[:, hh, 0:ntok], e3, qh, E3b)
        k.copy(POOL, decay.t[:, hh, 0:nch], e3.rearrange("p (c t) -> p c t", t=csz)[:, :, csz - 1], rd=(E3b,), wr=(decay,))
        k.tt(DVE, t1.rearrange("p (c t) -> p c t", t=csz), b3[:, :, csz - 1:csz].to_broadcast([128, nch, csz]), b3,
             ALU.subtract, rd=(Tb,), wr=(T1,))
        k.actf(e1, t1, AF.Exp, rd=(T1,), wr=(E1b,))
        k.tt(DVE, ksf.t[:, 0:ntok], tk, e1, ALU.mult, rd=(Tk, E1b), wr=(ksf,))

    def ob_post1(obank, c0, nq, h, onb, stb, si):
        k.actf(onb.t[0:nq, :], obank.t[0:nq, c0:c0 + 128], AF.Square, rd=(obank,), wr=(onb, stb), accum=stb.t[0:nq, si:si + 1])
        rstd_from(stb.t[0:nq, si:si + 1], stb, 1.0 / 128.0, nq)
        k.stt(onb.t[0:nq, :], obank.t[0:nq, c0:c0 + 128], stb.t[0:nq, si:si + 1], gain.t[0:nq, h * 128:(h + 1) * 128],
              ALU.mult, ALU.mult, rd=(obank, stb, gain), wr=(onb,))

    def ob_post2(nq, h, gz_ap, gz_buf, qcol0, onb):
        tb = getbank()
        k.tr(tb.t[:, 0:nq], onb.t[0:nq, :], ident.t[0:nq, 0:nq], rd=(onb, ident), wr=(tb,))
        k.tt(DVE, ybT.t[:, h, qcol0:qcol0 + nq], tb.t[:, 0:nq], gz_ap, ALU.mult, rd=(tb, gz_buf), wr=(ybT,))

    def ob_post(obank, c0, nq, h, gz_ap, gz_buf, qcol0, onb, stb, si):
        ob_post1(obank, c0, nq, h, onb, stb, si)
        ob_post2(nq, h, gz_ap, gz_buf, qcol0, onb)

    def hgrn_proj(ws, g, ntok, subs, qh, ff, vB, gz, og, tmp, preps):
        preps = list(preps)
        for which in range(5):
            ws.prefetch()
            slot = ws.get()
            ws.prefetch()
            if which == 2:
                for si_, sub in enumerate(subs):
                    bk = getbank()
                    proj_tm(slot, sub, bk)
                    evac(rr_eng([ACT, DVE]), vB.t[0:sub[1], si_, :], bk.t[0:sub[1], :], rd=(bk,), wr=(vB,))
                    if preps:
                        preps.pop(0)()
                while preps:
                    preps.pop(0)()
                continue
            for hh in range(4):
                h = 4 * g + hh
                bk = getbank()
                proj_fm(slot, hh, ntok, bk)
                src = bk.t[:, 0:ntok]
                if which == 0:
                    k.actf(qh.t[:, hh, 0:ntok], src, AF.Silu, rd=(bk,), wr=(qh,))
                elif which == 1:
                    k.actf(ff.t[:, hh, 0:ntok], src, AF.Sigmoid, rd=(bk,), wr=(ff,))
                    k.ts(DVE, ff.t[:, hh, 0:ntok], ff.t[:, hh, 0:ntok], lbv.t[:, 1, h:h + 1], lbv.t[:, 0, h:h + 1],
                         ALU.mult, ALU.add, rd=(ff, lbv), wr=(ff,))
                elif which == 3:
                    k.actf(og.t[:, hh, 0:ntok], src, AF.Sigmoid, rd=(bk,), wr=(og,))
                else:
                    k.actf(tmp[5].t[:, 0:ntok], src, AF.Silu, rd=(bk,), wr=(tmp[5],))
                    k.tt(DVE, og.t[:, hh, 0:ntok], tmp[5].t[:, 0:ntok], og.t[:, hh, 0:ntok], ALU.mult,
                         rd=(tmp[5], og), wr=(og,))

    def phase_b_prompt(ws, j):
        for g in range(2):
            with ExitStack() as pes:
                qh = k.sb("qh", [128, 4, T], F32, pes)
                ff = k.sb("ff", [128, 4, T], F32, pes)
                vB = k.sb("vB", [128, 4, 512], BF16, pes)
                gz = None
                og = k.sb("og", [128, 4, T], BF16, pes)
                qdT = k.sb("qdT", [128, 4, T], BF16, pes)
                kdT = k.sb("kdT", [128, 4, T], BF16, pes)
                qsE = k.sb("qsE", [128, 4, 4, 128], BF16, pes)
                qsO = k.sb("qsO", [128, 4, 4, 128], BF16, pes)
                ks_tm = k.sb("ks_tm", [128, 4, 4, 128], BF16, pes)
                decay = k.sb("decay", [128, 4, 8], F32, pes)
                tmp = [k.sb("htmp%d" % i, [128, T], F32, pes) for i in range(6)]
                scTs = [k.sb("scT%d" % i, [128, 4, 128], BF16, pes) for i in range(2)]
                onb = [k.sb("onbB%d" % i, [128, 128], F32, pes) for i in range(8)]
                stbs = [k.sb("stbB%d" % i, [128, 2], F32, pes) for i in range(8)]
                for s_ in scTs:
                    k.memset(POOL, s_.t[:], 0.0, wr=(s_,))
                k.memset(POOL, qsE.t[:], 0.0, wr=(qsE,))
                k.memset(POOL, qsO.t[:], 0.0, wr=(qsO,))
                ksf = [k.sb("ksf%d" % i, [128, T], F32, pes) for i in range(4)]

                def qs_writer(hh, q_ap, e3_ap, qbuf, ebuf):
                    qv = q_ap.rearrange("p (s two t) -> p s two t", two=2, t=64)
                    ev_ = e3_ap.rearrange("p (s two t) -> p s two t", two=2, t=64)
                    k.tt(DVE, qsE.t[:, hh, :, 0:64], qv[:, :, 0, :], ev_[:, :, 0, :], ALU.mult, rd=(qbuf, ebuf), wr=(qsE,))
                    k.tt(DVE, qsO.t[:, hh, :, 64:128], qv[:, :, 1, :], ev_[:, :, 1, :], ALU.mult, rd=(qbuf, ebuf), wr=(qsO,))

                def ks_writer(hh, Tg):
                    bk = getbank()
                    for s_ in range(4):
                        k.tr(bk.t[:, s_ * 128:(s_ + 1) * 128], Tg.t[:, s_ * 128:(s_ + 1) * 128], ident.t[:, :],
                             rd=(Tg, ident), wr=(bk,), sig=(s_ == 3))
                    evac(rr_eng([ACT, DVE]), ks_tm.t[:, hh, :, :], bk.t[:, :].rearrange("p (s d) -> p s d", d=128), rd=(bk,), wr=(ks_tm,))

                preps = [lambda hh=hh: hgrn_prep(4 * g + hh, hh, T, 64, qh, ff, tmp, rmask, qdT, kdT, qs_writer, ksf[hh], decay)
                         for hh in range(4)]
                hgrn_proj(ws, g, T, subs4, qh, ff, vB, gz, og, tmp, preps)
                for hh in range(4):
                    ks_writer(hh, ksf[hh])

                def post2(cp_):
                    cs2 = slice(cp_ * 128, (cp_ + 1) * 128)
                    for hh in range(4):
                        ob_post2(128, 4 * g + hh, og.t[:, hh, cs2], og, cp_ * 128, onb[(cp_ % 2) * 4 + hh])

                for cp in range(4):
                    cs_ = slice(cp * 128, (cp + 1) * 128)
                    scb = getbank()
                    for hh in range(4):
                        k.mm(scb.t[:, hh * 128:(hh + 1) * 128], kdT.t[:, hh, cs_], qdT.t[:, hh, cs_], True, True,
                             rd=(kdT, qdT), wr=(scb,), sig=(hh == 3))
                    scT = scTs[cp % 2]
                    scv = scb.t[:, :].rearrange("p (h t) -> p h t", t=128)
                    k.tt(DVE, scT.t[0:64, :, 0:64], scv[0:64, :, 0:64], tri.t[0:64, :].unsqueeze(1).to_broadcast([64, 4, 64]),
                         ALU.mult, rd=(scb, tri), wr=(scT,))
                    k.tt(DVE, scT.t[64:128, :, 64:128], scv[64:128, :, 64:128],
                         tri.t[64:128, :].unsqueeze(1).to_broadcast([64, 4, 64]), ALU.mult, rd=(scb, tri), wr=(scT,))
                    for par in range(2):
                        ps_ = slice(par * 64, (par + 1) * 64)
                        dsb = getbank()
                        for hh in range(4):
                            k.mm(dsb.t[:, hh * 128:(hh + 1) * 128], ks_tm.t[ps_, hh, cp, :], vB.t[ps_, cp, hh * 128:(hh + 1) * 128],
                                 True, True, rd=(ks_tm, vB), wr=(dsb,), sig=(hh == 3))
                        if par == 0 and cp > 0:
                            post2(cp - 1)
                        if par == 1:
                            obk = getbank()
                            for hh in range(4):
                                h = 4 * g + hh
                                oc_ = slice(hh * 128, (hh + 1) * 128)
                                k.mm(obk.t[:, oc_], scT.t[:, hh, :], vB.t[:, cp, oc_], True, False, rd=(scT, vB), wr=(obk,), sig=False)
                                k.mm(obk.t[:, oc_], qsE.t[:, hh, cp, :], Sbf[h][1].t[:], False, False, rd=(qsE, Sbf[h][1]), wr=(obk,), sig=False)
                                k.mm(obk.t[:, oc_], qsO.t[:, hh, cp, :], Sbf[h][0].t[:], False, True, rd=(qsO, Sbf[h][0]), wr=(obk,), sig=True)
                        for hh in range(4):
                            h = 4 * g + hh
                            k.stt(Sst[h].t[:], Sst[h].t[:], decay.t[:, hh, 2 * cp + par:2 * cp + par + 1],
                                  dsb.t[:, hh * 128:(hh + 1) * 128], ALU.mult, ALU.add, rd=(Sst[h], decay, dsb), wr=(Sst[h],))
                            k.copy(rr_eng([ACT, POOL]), Sbf[h][par].t[:], Sst[h].t[:], rd=(Sst[h],), wr=(Sbf[h][par],))
                    for hh in range(4):
                        ob_post1(obk, hh * 128, 128, 4 * g + hh, onb[(cp % 2) * 4 + hh], stbs[(cp % 2) * 4 + hh], 0)
                post2(3)
                k.barrier()

    def phase_c(ws, ntok, groups, pes):
        qcT = k.sb("qcT", [128, 8, ntok], BF16, pes)
        zcs = k.sb("zcs", [128, 8, ntok], BF16, pes)
        PTm = [k.sb("PTm%d" % i, [128, 2, ntok], BF16, pes) for i in range(2)]
        rs = [k.sb("rsC%d" % i, [128, ntok], F32, pes) for i in range(2)]
        tc_ = [k.sb("tmpC%d" % i, [128, ntok], F32, pes) for i in range(2)]
        for which in range(2):
            for half in range(2):
                ws.prefetch()
                slot = ws.get()
                ws.prefetch()
                for cb in range(4):
                    bk = getbank()
                    proj_fm(slot, cb, ntok, bk)
                    if which == 0:
                        evac(rr_eng([ACT, DVE]), qcT.t[:, half * 4 + cb, 0:ntok], bk.t[:, 0:ntok], rd=(bk,), wr=(qcT,))
                    else:
                        k.actf(zcs.t[:, half * 4 + cb, 0:ntok], bk.t[:, 0:ntok], AF.Silu, rd=(bk,), wr=(zcs,))
        it = 0
        for (c0, n, mkb, mvb_) in groups:
            cs_ = slice(c0, c0 + n)
            for h in range(4):
                pt = PTm[it % 2]
                r_ = rs[it % 2]
                it += 1
                for mt in range(2):
                    sbk = getbank()
                    for half in range(2):
                        k.mm(sbk.t[:, 0:n], mkb.t[:, 2 * h + half, mt * 128:(mt + 1) * 128], qcT.t[:, 2 * h + half, cs_],
                             half == 0, half == 1, rd=(mkb, qcT), wr=(sbk,), sig=(half == 1))
                    k.actf(pt.t[:, mt, 0:n], sbk.t[:, 0:n], AF.Exp, rd=(sbk,), wr=(pt,), scale=1.0 / 16.0)
                smb = getbank()
                for mt in range(2):
                    k.mm(smb.t[:, 0:n], ones_bf.t[:, :], pt.t[:, mt, 0:n], mt == 0, mt == 1, rd=(ones_bf, pt), wr=(smb,), sig=(mt == 1))
                k.op(DVE, lambda: nc.vector.reciprocal(out=r_.t[:, 0:n], in_=smb.t[:, 0:n]), rd=(smb,), wr=(r_,))
                for eh in range(2):
                    obk = getbank()
                    ch = 2 * h + eh
                    for mt in range(2):
                        k.mm(obk.t[:, 0:n], mvb_.t[:, mt, ch * 128:(ch + 1) * 128], pt.t[:, mt, 0:n], mt == 0, mt == 1,
                             rd=(mvb_, pt), wr=(obk,), sig=(mt == 1))
                    t_ = tc_[eh]
                    k.tt(DVE, t_.t[:, 0:n], obk.t[:, 0:n], r_.t[:, 0:n], ALU.mult, rd=(obk, r_), wr=(t_,))
                    k.tt(POOL, ycT.t[:, ch, cs_], t_.t[:, 0:n], zcs.t[:, ch, cs_], ALU.mult, rd=(t_, zcs), wr=(ycT,))

    def phase_merge_out(ws, ntok, subs, x_src, y_dst, pes):
        mergedT = k.sb("mergedT", [128, 16, T], BF16, pes)
        sg = [k.sb("sgM%d" % i, [128, T], F32, pes) for i in range(3)]
        acc = [k.sb("accM%d" % i, [128, T], F32, pes) for i in range(2)]
        tmpm = [k.sb("tmpM%d" % i, [128, T], F32, pes) for i in range(2)]
        xr = [k.sb("xr%d" % i, [128, D], F32, pes) for i in range(len(subs))]
        stats = [k.sb("lnstat%d" % i, [128, 4, 6], F32, pes) for i in range(len(subs))]
        mvs = [k.sb("lnmv%d" % i, [128, 4], F32, pes) for i in range(len(subs))]
        gam = k.sb("gam", [128, D], F32, pes)
        bet = k.sb("bet", [128, D], F32, pes)
        k.dma(SP, gam.t[:], lng[0:1, :].to_broadcast([128, D]), wr=(gam,), sbuf=gam)
        k.dma(SP, bet.t[:], lnb[0:1, :].to_broadcast([128, D]), wr=(bet,), sbuf=bet)
        for si_, (t0, n) in enumerate(subs):
            k.dma(SP, xr[si_].t[0:n, :], x_src[t0:t0 + n, :], wr=(xr[si_],), sbuf=xr[si_])
        yTs = (yaT, ybT, ycT)
        for oc in range(16):
            ws.prefetch()
            slot = ws.get()
            ws.prefetch()
            a_ = acc[oc % 2]
            for br in range(3):
                gb = getbank()
                gv = slot.t[:, br * 2048:(br + 1) * 2048].rearrange("p (kc c) -> p kc c", c=128)
                for kc in range(16):
                    k.mm(gb.t[:, 0:ntok], gv[:, kc, :], xT.t[:, kc, 0:ntok], kc == 0, kc == 15, rd=(slot, xT), wr=(gb,), sig=(kc == 15))
                yb_ = getbank()
                wv = slot.t[:, 6144 + br * 1024:6144 + (br + 1) * 1024].rearrange("p (kc c) -> p kc c", c=128)
                for kc in range(8):
                    k.mm(yb_.t[:, 0:ntok], wv[:, kc, :], yTs[br].t[:, kc, 0:ntok], kc == 0, kc == 7, rd=(slot, yTs[br]), wr=(yb_,), sig=(kc == 7))
                s_ = sg[br]
                k.actf(s_.t[:, 0:ntok], gb.t[:, 0:ntok], AF.Sigmoid, rd=(gb,), wr=(s_,))
                if br == 0:
                    k.tt(DVE, a_.t[:, 0:ntok], s_.t[:, 0:ntok], yb_.t[:, 0:ntok], ALU.mult, rd=(s_, yb_), wr=(a_,))
                else:
                    t_ = tmpm[br - 1]
                    k.tt(DVE, t_.t[:, 0:ntok], s_.t[:, 0:ntok], yb_.t[:, 0:ntok], ALU.mult, rd=(s_, yb_), wr=(t_,))
                    if br == 1:
                        k.tt(POOL, a_.t[:, 0:ntok], a_.t[:, 0:ntok], t_.t[:, 0:ntok], ALU.add, rd=(a_, t_), wr=(a_,))
                    else:
                        k.tt(POOL, mergedT.t[:, oc, 0:ntok], a_.t[:, 0:ntok], t_.t[:, 0:ntok], ALU.add, rd=(a_, t_), wr=(mergedT,))
        for cb in range(4):
            ws.prefetch()
            slot = ws.get()
            ws.prefetch()
            wv = slot.t[:, 0:8192].rearrange("p (kc c) -> p kc c", c=512)
            for si_, (t0, n) in enumerate(subs):
                bk = getbank()
                for kc in range(16):
                    k.mm(bk.t[0:n, :], mergedT.t[:, kc, t0:t0 + n], wv[:, kc, :], kc == 0, kc == 15, rd=(mergedT, slot), wr=(bk,), sig=(kc == 15))
                xc = xr[si_].t[0:n, cb * 512:(cb + 1) * 512]
                k.stt(xc, xc, ALPHA, bk.t[0:n, :], ALU.mult, ALU.add, rd=(xr[si_], bk), wr=(xr[si_],))
        for si_, (t0, n) in enumerate(subs):
            xb = xr[si_]
            stat, mv_ = stats[si_], mvs[si_]
            for c in range(4):
                k.op(DVE, lambda c=c: nc.vector.bn_stats(out=stat.t[0:n, c, :], in_=xb.t[0:n, c * 512:(c + 1) * 512]), rd=(xb,), wr=(stat,))
            k.op(DVE, lambda: nc.vector.bn_aggr(out=mv_.t[0:n, 0:2], in_=stat.t[0:n, :, :].rearrange("p a b -> p (a b)")), rd=(stat,), wr=(mv_,))
            k.ts(DVE, mv_.t[0:n, 2:3], mv_.t[0:n, 1:2], EPS, None, ALU.add, None, rd=(mv_,), wr=(mv_,))
            k.tt(POOL, mv_.t[0:n, 2:3], mv_.t[0:n, 2:3], m05.t[0:n, :], ALU.pow, rd=(mv_, m05), wr=(mv_,))
            k.ts(DVE, mv_.t[0:n, 3:4], mv_.t[0:n, 0:1], mv_.t[0:n, 2:3], -1.0, ALU.mult, ALU.mult, rd=(mv_,), wr=(mv_,))
            k.actf(xb.t[0:n, :], xb.t[0:n, :], AF.Identity, rd=(xb, mv_), wr=(xb,), scale=mv_.t[0:n, 2:3], bias=mv_.t[0:n, 3:4])
            k.tt(DVE, xb.t[0:n, :], xb.t[0:n, :], gam.t[0:n, :], ALU.mult, rd=(xb, gam), wr=(xb,))
            k.tt(POOL, xb.t[0:n, :], xb.t[0:n, :], bet.t[0:n, :], ALU.add, rd=(xb, bet), wr=(xb,))
            k.dma(ACT, y_dst[t0:t0 + n, :], xb.t[0:n, :], rd=(xb,), sbuf=xb, is_out=True)

    def mem_transposes(src_bufs, n_sub, dstT):
        for s_ in range(n_sub):
            for g in range(2):
                bk = getbank()
                for i in range(4):
                    ch = 4 * g + i
                    k.tr(bk.t[:, i * 128:(i + 1) * 128], src_bufs[s_].t[:, ch * 128:(ch + 1) * 128], ident.t[:, :],
                         rd=(src_bufs[s_], ident), wr=(bk,), sig=(i == 3))
                evac(rr_eng([ACT, DVE]), dstT.t[:, 4 * g:4 * g + 4, s_ * 128:(s_ + 1) * 128],
                     bk.t[:, :].rearrange("p (a b) -> p a b", b=128), rd=(bk,), wr=(dstT,))

    def phase_mem_prompt():
        with ExitStack() as pes:
            subs2 = [(0, 128), (128, 128)]
            build_xT(mem, subs2, pes)
            ws = WStream([U_MK, U_MK + 1, U_MV, U_MV + 1], jit=True)
            mtm = [k.sb("mtm%d" % i, [128, 1024], F32, pes) for i in range(2)]
            for which in range(2):
                for half in range(2):
                    ws.prefetch()
                    slot = ws.get()
                    ws.prefetch()
                    for si_, sub in enumerate(subs2):
                        bk = getbank()
                        proj_tm(slot, sub, bk)
                        evac(rr_eng([ACT, DVE]), mtm[si_].t[:, half * 512:(half + 1) * 512], bk.t[:, :], rd=(bk,), wr=(mtm[si_],))
                for si_, (t0, n) in enumerate(subs2):
                    dst = mk_p if which == 0 else mv_p
                    k.dma(ACT, dst[t0:t0 + n, :], mtm[si_].t[:, :], rd=(mtm[si_],), sbuf=mtm[si_], is_out=True)
                if which == 0:
                    mem_transposes(mtm, 2, mkT)
                else:
                    for si_ in range(2):
                        evac(rr_eng([ACT, DVE]), mvb.t[:, si_, :], mtm[si_].t[:, :], rd=(mtm[si_],), wr=(mvb,))
            k.barrier()

    KTS_v = KTS
    VS_v = VS.rearrange("i p (s e) -> i p s e", e=130)
    subs1 = [(0, 64)]

    def phase_s_cache():
        with ExitStack() as pes:
            ckst = [[k.sb("ckst%d_%d" % (S, i), [128, 1024], F32, pes) for i in range(2)] for S in range(2)]
            cvst = [[k.sb("cvst%d_%d" % (S, i), [128, 1024], F32, pes) for i in range(2)] for S in range(2)]
            ktst = [k.sb("ktstS%d" % S, [128, 8, 256], BF16, pes) for S in range(2)]
            vst = [k.sb("vstS%d" % S, [128, 2, 8, 130], BF16, pes) for S in range(2)]
            for S in range(2):
                k.memset(POOL, vst[S].t[:, :, :, 128:129], 1.0, wr=(vst[S],))
                k.memset(POOL, vst[S].t[:, :, :, 129:130], 0.0, wr=(vst[S],))
            its = [(b, jj) for b in range(4) for jj in range(16)]

            def loads(it):
                b, jj = its[it]
                S = it % 2
                for s_ in range(2):
                    r0 = jj * 256 + s_ * 128
                    k.dma(SP, ckst[S][s_].t[:, :], ck[b, r0:r0 + 128, :], wr=(ckst[S][s_],), sbuf=ckst[S][s_])
                    k.dma(SP, cvst[S][s_].t[:, :], cv[b, r0:r0 + 128, :], wr=(cvst[S][s_],), sbuf=cvst[S][s_])

            loads(0)
            for it, (b, jj) in enumerate(its):
                if it + 1 < len(its):
                    loads(it + 1)
                S = it % 2
                for s_ in range(2):
                    for g in range(2):
                        bk = getbank()
                        for i in range(4):
                            h = 4 * g + i
                            k.tr(bk.t[:, i * 128:(i + 1) * 128], ckst[S][s_].t[:, h * 128:(h + 1) * 128], ident.t[:, :],
                                 rd=(ckst[S][s_], ident), wr=(bk,), sig=(i == 3))
                        evac(rr_eng([ACT, DVE]), ktst[S].t[:, 4 * g:4 * g + 4, s_ * 128:(s_ + 1) * 128],
                             bk.t[:, :].rearrange("p (a b) -> p a b", b=128), rd=(bk,), wr=(ktst[S],))
                    evac(rr_eng([POOL, DVE]), vst[S].t[:, s_, :, 0:128], cvst[S][s_].t[:, :].rearrange("p (h e) -> p h e", e=128),
                         rd=(cvst[S][s_],), wr=(vst[S],))
                group_store([(KTS_v[b * 8 + h, :, jj * 256:(jj + 1) * 256], ktst[S].t[:, h, :]) for h in range(8)],
                            ktst[S], [kts_b[b * 8 + h] for h in range(8)])
                group_store([(VS_v[b * 8 + h, :, 2 * jj:2 * jj + 2, :], vst[S].t[:, :, h, :]) for h in range(8)],
                            vst[S], [vs_b[b * 8 + h] for h in range(8)])
            k.barrier()

    cache = {"step": 0, "sets": None}
    NSTEP = 4 * 32

    def cache_alloc(es_):
        sets = []
        for S in range(2):
            d = dict(ck=k.sb("cck%d" % S, [128, 1024], F32, es_), cv=k.sb("ccv%d" % S, [128, 1024], F32, es_),
                     kt=k.sb("ckt%d" % S, [128, 8, 128], BF16, es_), v=k.sb("cvv%d" % S, [128, 8, 130], BF16, es_))
            k.memset(POOL, d["v"].t[:, :, 128:129], 1.0, wr=(d["v"],))
            k.memset(POOL, d["v"].t[:, :, 129:130], 0.0, wr=(d["v"],))
            sets.append(d)
        cache["sets"] = sets
        k.local_bufs = [b for b in k.local_bufs if all(b is not x for d in sets for x in d.values())]

    def cache_loads(s_):
        if s_ >= NSTEP:
            return
        b, r = divmod(s_, 32)
        d = cache["sets"][s_ % 2]
        k.dma(SP, d["ck"].t[:, :], ck[b, r * 128:(r + 1) * 128, :], wr=(d["ck"],), sbuf=d["ck"])
        k.dma(SP, d["cv"].t[:, :], cv[b, r * 128:(r + 1) * 128, :], wr=(d["cv"],), sbuf=d["cv"])

    def cache_step():
        s_ = cache["step"]
        if s_ >= NSTEP or cache["sets"] is None:
            return
        if s_ == 0:
            cache_loads(0)
        cache_loads(s_ + 1)
        b, r = divmod(s_, 32)
        d = cache["sets"][s_ % 2]
        for g in range(2):
            bk = getbank()
            for i in range(4):
                h = 4 * g + i
                k.tr(bk.t[:, i * 128:(i + 1) * 128], d["ck"].t[:, h * 128:(h + 1) * 128], ident.t[:, :],
                     rd=(d["ck"], ident), wr=(bk,), sig=(i == 3))
            evac(DVE, d["kt"].t[:, 4 * g:4 * g + 4, :], bk.t[:, :].rearrange("p (a b) -> p a b", b=128), rd=(bk,), wr=(d["kt"],))
        evac(POOL, d["v"].t[:, :, 0:128], d["cv"].t[:, :].rearrange("p (h e) -> p h e", e=128), rd=(d["cv"],), wr=(d["v"],))
        hb = [kts_b[b * 8 + h] for h in range(8)]
        ev = k.dma(SP, KTS_v[b * 8:(b + 1) * 8, :, r * 128:(r + 1) * 128].rearrange("h p t -> p h t"), d["kt"].t[:, :, :],
                   rd=(d["kt"],), wr=(), sbuf=d["kt"])
        for x in hb:
            x.w = ev
            x.r = {}
        vb_ = [vs_b[b * 8 + h] for h in range(8)]
        ev = k.dma(SP, VS_v[b * 8:(b + 1) * 8, :, r, :].rearrange("h p e -> p h e"), d["v"].t[:, :, :],
                   rd=(d["v"],), wr=(), sbuf=d["v"])
        for x in vb_:
            x.w = ev
            x.r = {}
        cache["step"] = s_ + 1

    def attn_sample(pes):
        KTh = [k.sb("KThS%d" % i, [128, SEQ + 16], BF16, pes) for i in range(2)]
        Vh = [k.sb("VhS%d" % i, [128, 33, 130], BF16, pes) for i in range(2)]
        PT = [k.sb("PTS%d" % i, [128, 512], BF16, pes) for i in range(2)]
        PTt = [k.sb("PTtS%d" % i, [16, 16], BF16, pes) for i in range(2)]
        o1b = [k.sb("o1bS%d" % i, [128, 128], F32, pes) for i in range(2)]
        o2b = [k.sb("o2bS%d" % i, [128, 128], F32, pes) for i in range(2)]
        onb = [k.sb("onbS%d" % i, [128, 128], F32, pes) for i in range(2)]
        stb = [k.sb("stbS%d" % i, [128, 4], F32, pes) for i in range(2)]

        def load(i):
            k.dma(SP, KTh[i % 2].t[:, :], KTS_v[i, :, :], rd=(kts_b[i],), wr=(KTh[i % 2],), sbuf=KTh[i % 2])
            k.dma(SP, Vh[i % 2].t[:, :, :], VS_v[i, :, :, :], rd=(vs_b[i],), wr=(Vh[i % 2],), sbuf=Vh[i % 2])

        load(0)
        pi = 0
        for i in range(32):
            b, h = i // 8, i % 8
            if i + 1 < 32:
                load(i + 1)
            KT, V = KTh[i % 2], Vh[i % 2]
            qc = slice(b * 16, (b + 1) * 16)
            for m in range(2):
                ms = slice(m * 64, (m + 1) * 64)
                pt, ptt = PT[pi % 2], PTt[pi % 2]
                pi += 1
                sbk = getbank()
                for kt in range(32):
                    k.mm(sbk.t[:, kt * 16:(kt + 1) * 16], KT.t[:, kt * 128:(kt + 1) * 128], qTz[m].t[:, h, qc], True, True,
                         rd=(KT, qTz[m]), wr=(sbk,), sig=(kt == 31))
                k.actf(pt.t[:, :], sbk.t[:, :], AF.Exp, rd=(sbk,), wr=(pt,), scale=0.125)
                sbt = getbank()
                k.mm(sbt.t[0:16, 0:16], KT.t[:, SEQ:SEQ + 16], qTz[m].t[:, h, qc], True, True, rd=(KT, qTz[m]), wr=(sbt,), sig=True)
                k.actf(ptt.t[:, :], sbt.t[0:16, 0:16], AF.Exp, rd=(sbt,), wr=(ptt,), scale=0.125)
                O = getbank()
                for kt in range(32):
                    k.mm(O.t[0:16, 0:130], pt.t[:, kt * 16:(kt + 1) * 16], V.t[:, kt, :], kt == 0, False, rd=(pt, V), wr=(O,), sig=False)
                k.mm(O.t[0:16, 0:130], ptt.t[:, :], V.t[0:16, 32, :], False, True, rd=(ptt, V), wr=(O,), sig=True)
                attn_post(O, 16, m, h, b * 16, o1b[i % 2], o2b[i % 2], onb[i % 2], stb[i % 2])

    def phase_b_sample(ws):
        for g in range(2):
            with ExitStack() as pes:
                qh = k.sb("qhS", [128, 4, 64], F32, pes)
                ff = k.sb("ffS", [128, 4, 64], F32, pes)
                vB = k.sb("vBS", [128, 1, 512], BF16, pes)
                vBm = k.sb("vBmS", [64, 4, 512], BF16, pes)
                gz = None
                og = k.sb("ogS", [128, 4, 64], BF16, pes)
                qdT = k.sb("qdTS", [128, 4, 64], BF16, pes)
                kdT = k.sb("kdTS", [128, 4, 64], BF16, pes)
                qsZ = k.sb("qsZS", [128, 4, 4, 64], BF16, pes)
                ks_tm = k.sb("ks_tmS", [64, 4, 128], BF16, pes)
                decay = k.sb("decayS", [128, 4, 4], F32, pes)
                tmp = [k.sb("htmpS%d" % i, [128, 64], F32, pes) for i in range(6)]
                scT = [k.sb("scTS%d" % i, [64, 64], BF16, pes) for i in range(2)]
                onb = [k.sb("onbBS%d" % i, [128, 128], F32, pes) for i in range(2)]
                stb = k.sb("stbBS", [128, 8], F32, pes)
                S0f = [k.sb("S0f%d" % i, [128, 128], F32, pes) for i in range(8)]
                S0b = [k.sb("S0b%d" % i, [128, 128], BF16, pes) for i in range(8)]
                k.memset(POOL, qsZ.t[:], 0.0, wr=(qsZ,))
                ksf = [k.sb("ksfS%d" % i, [128, 64], F32, pes) for i in range(4)]
                for b in []:
                    k.ts(DVE, vBm.t[0:64, b, :], vB.t[0:64, 0, :], rowmask.t[0:64, b:b + 1], None, ALU.mult, None,
                         rd=(vB, rowmask), wr=(vBm,))

                def qs_writer(hh, q_ap, e3_ap, qbuf, ebuf):
                    for b in range(4):
                        c_ = slice(b * 16, (b + 1) * 16)
                        k.tt(DVE, qsZ.t[:, hh, b, c_], q_ap[:, c_], e3_ap[:, c_], ALU.mult, rd=(qbuf, ebuf), wr=(qsZ,))

                def ks_writer(hh, Tg):
                    bk = getbank()
                    k.tr(bk.t[0:64, 0:128], Tg.t[:, 0:64], ident.t[:, :], rd=(Tg, ident), wr=(bk,))
                    evac(rr_eng([ACT, DVE]), ks_tm.t[0:64, hh, :], bk.t[0:64, 0:128], rd=(bk,), wr=(ks_tm,))

                preps = [lambda hh=hh: hgrn_prep(4 * g + hh, hh, 64, 16, qh, ff, tmp, rmask16, qdT, kdT, qs_writer, ksf[hh], decay)
                         for hh in range(4)]
                hgrn_proj(ws, g, 64, subs1, qh, ff, vB, gz, og, tmp, preps)
                for b in range(4):
                    k.ts(DVE, vBm.t[0:64, b, :], vB.t[0:64, 0, :], rowmask.t[0:64, b:b + 1], None, ALU.mult, None,
                         rd=(vB, rowmask), wr=(vBm,))
                for hh in range(4):
                    ks_writer(hh, ksf[hh])
                for hh in range(4):
                    h = 4 * g + hh
                    hc = slice(hh * 128, (hh + 1) * 128)
                    for b in range(4):
                        sf, sbb = S0f[(hh % 2) * 4 + b], S0b[(hh % 2) * 4 + b]
                        k.dma(SP, sf.t[:, :], s0[b * 8 + h], wr=(sf,), sbuf=sf)
                        evac(rr_eng([ACT, POOL]), sbb.t[:, :], sf.t[:, :], rd=(sf,), wr=(sbb,))
                    scb = getbank()
                    k.mm(scb.t[0:64, 0:64], kdT.t[:, hh, 0:64], qdT.t[:, hh, 0:64], True, True, rd=(kdT, qdT), wr=(scb,), sig=True)
                    sc_ = scT[hh % 2]
                    k.tt(DVE, sc_.t[:, :], scb.t[0:64, 0:64], tri16.t[:, :], ALU.mult, rd=(scb, tri16), wr=(sc_,))
                    obk = getbank()
                    k.mm(obk.t[0:64, 0:128], sc_.t[:, :], vB.t[0:64, 0, hc], True, False, rd=(sc_, vB), wr=(obk,), sig=False)
                    for b in range(4):
                        sbb = S0b[(hh % 2) * 4 + b]
                        k.mm(obk.t[0:64, 0:128], qsZ.t[:, hh, b, :], sbb.t[:, :], False, b == 3, rd=(qsZ, sbb), wr=(obk,), sig=(b == 3))
                    dsb = getbank()
                    for b in range(4):
                        k.mm(dsb.t[:, b * 128:(b + 1) * 128], ks_tm.t[0:64, hh, :], vBm.t[0:64, b, hc], True, True,
                             rd=(ks_tm, vBm), wr=(dsb,), sig=(b == 3))
                    for b in range(4):
                        sf = S0f[(hh % 2) * 4 + b]
                        k.stt(sf.t[:, :], sf.t[:, :], decay.t[:, hh, b:b + 1], dsb.t[:, b * 128:(b + 1) * 128], ALU.mult, ALU.add,
                              rd=(sf, decay, dsb), wr=(sf,))
                        k.dma(ACT, hg_s[b * 8 + h], sf.t[:, :], rd=(sf,), sbuf=sf, is_out=True)
                    ob_post(obk, 0, 64, h, og.t[:, hh, 0:64], og, 0, onb[hh % 2], stb, hh)
                k.barrier()

    def run_sample():
        ws = WStream(TILE_UNITS)
        outer = ExitStack()
        alloc_q_za(outer)
        with ExitStack() as pes:
            build_xT(xs, subs1, pes)

            def store_kt(ktst):
                group_store([(KTS_v[b * 8 + h, :, SEQ:SEQ + 16], ktst.t[:, h, b * 16:(b + 1) * 16])
                             for b in range(4) for h in range(8)], ktst, kts_b)

            def store_v(vst):
                group_store([(VS_v[b * 8 + h, 0:16, 32, :], vst.t[b * 16:(b + 1) * 16, 0, h, :])
                             for b in range(4) for h in range(8)], vst, vs_b)

            with nc.allow_non_contiguous_dma(reason="small per-sequence K^T column writes"):
                phase_a(ws, subs1, 64, [32], k_s, v_s, store_kt, store_v, pes)
            phase_z(ws, 64)
            k.barrier()
        with ExitStack() as pes:
            attn_sample(pes)
            k.barrier()
        outer.close()
        phase_b_sample(ws)
        with ExitStack() as pes:
            mkTs = [k.sb("mkTs%d" % b, [128, 8, 256], BF16, pes) for b in range(4)]
            mvbs = [k.sb("mvbs%d" % b, [128, 2, 1024], BF16, pes) for b in range(4)]
            mst = [k.sb("mst%d" % i, [128, 1024], F32, pes) for i in range(4)]
            for b in range(4):
                for s_ in range(2):
                    k.dma(SP, mst[s_].t[:, :], cmk[b, s_ * 128:(s_ + 1) * 128, :], wr=(mst[s_],), sbuf=mst[s_])
                    k.dma(SP, mst[2 + s_].t[:, :], cmv[b, s_ * 128:(s_ + 1) * 128, :], wr=(mst[2 + s_],), sbuf=mst[2 + s_])
                mem_transposes(mst[0:2], 2, mkTs[b])
                for s_ in range(2):
                    evac(rr_eng([ACT, DVE]), mvbs[b].t[:, s_, :], mst[2 + s_].t[:, :], rd=(mst[2 + s_],), wr=(mvbs[b],))
            phase_c(ws, 64, [(b * 16, 16, mkTs[b], mvbs[b]) for b in range(4)], pes)
            k.barrier()
        with ExitStack() as pes:
            phase_merge_out(ws, 64, subs1, xs, y_s, pes)
            k.barrier()

    def alloc_q_za(pes):
        for m in range(2):
            qTz[m] = k.sb("qTz%d" % m, [128, 8, T], BF16, pes)
        zaTh[0] = k.sb("zaT", [128, 8, T], BF16, pes)
        k.memset(POOL, qTz[0].t[64:128, :, :], 0.0, wr=(qTz[0],))
        k.memset(POOL, qTz[1].t[0:64, :, :], 0.0, wr=(qTz[1],))

    KTP_v = KTP
    VP_v = VP.rearrange("h p (s e) -> h p s e", e=130)
    subs4 = [(i * 128, 128) for i in range(4)]
    TILE_UNITS = list(range(0, 42))

    for h in range(8):
        k.memset(POOL, Sst[h].t[:], 0.0, wr=(Sst[h],))
        for p_ in range(2):
            k.memset(POOL, Sbf[h][p_].t[:], 0.0, wr=(Sbf[h][p_],))

    es_stg = ExitStack()
    for i in range(4):
        stg.append(k.sb("stg%d" % i, [128, 1024], F32, es_stg))
    k.local_bufs = [b for b in k.local_bufs if all(b is not x for x in stg)]
    es_cache = ExitStack()
    phase_mem_prompt()
    if not JIT_TILE0:
        for u in range(0, 42):
            wconvert(u)
        k.barrier()

    NT_RUN = NT
    for j in range(NT_RUN):
        if j == 1:
            es_stg.close()
            with nc.allow_non_contiguous_dma(reason="per-head scatter of converted cache tiles"):
                cache_alloc(es_cache)
        ws = WStream(TILE_UNITS, jit=(j == 0 and JIT_TILE0))
        if j >= 1:
            ws.bg = cache_step
        outer = ExitStack()
        alloc_q_za(outer)
        with ExitStack() as pes:
            build_xT(xp[j * T:(j + 1) * T, :], subs4, pes)

            def store_kt(ktst, j=j):
                group_store([(KTP_v[h, :, j * T:(j + 1) * T], ktst.t[:, h, 0:T]) for h in range(8)], ktst, ktp_b)

            def store_v(vst, j=j):
                group_store([(VP_v[h, :, 4 * j:4 * j + 4, :], vst.t[:, :, h, :]) for h in range(8)], vst, vp_b)

            phase_a(ws, subs4, T, [4 * j + i for i in range(4)],
                    k_p[j * T:(j + 1) * T, :], v_p[j * T:(j + 1) * T, :], store_kt, store_v, pes)
            phase_z(ws, T)
            k.barrier()
        with ExitStack() as pes:
            attn_prompt(j, pes)
            k.barrier()
        outer.close()
        phase_b_prompt(ws, j)
        with ExitStack() as pes:
            phase_c(ws, T, [(0, T, mkT, mvb)], pes)
            k.barrier()
        with ExitStack() as pes:
            phase_merge_out(ws, T, subs4, xp[j * T:(j + 1) * T, :], y_p[j * T:(j + 1) * T, :], pes)
            k.barrier()
    for h in range(8):
        k.dma(ACT, hg_p[h], Sst[h].t[:], rd=(Sst[h],), sbuf=Sst[h], is_out=True)
    while cache["step"] < NSTEP:
        cache_step()
    k.barrier()
    es_cache.close()
    if RUN_SAMPLE:
        run_sample()

    k.finish()
    es.close()
    return nc


_CACHE = {}


def _consts():
    p = np.arange(128)
    inv = np.power(500000.0, -np.arange(0, 16, 2, dtype=np.float32) / 16.0).astype(np.float32)
    rope = np.zeros((128, 33, 16), np.float32)
    for S in range(33):
        pos = (128 * S + p) if S < 32 else (4096 + (p % 16))
        ang = pos.astype(np.float32)[:, None] * inv[None, :]
        rope[:, S, 0:8] = np.cos(ang)
        rope[:, S, 8:16] = np.sin(ang)
    t = np.arange(64)
    tri = ((p[:, None] % 64) <= t[None, :]).astype(np.float32)
    rmask = (np.arange(512) % 64 != 0).astype(np.float32)[None, :].repeat(128, 0)
    rmask16 = (np.arange(64) % 16 != 0).astype(np.float32)[None, :].repeat(128, 0)
    s = np.arange(64)
    tri16 = ((s[:, None] // 16 == s[None, :] // 16) & (s[:, None] <= s[None, :])).astype(np.float32)
    rowmask = (s[:, None] // 16 == np.arange(4)[None, :]).astype(np.float32)
    return {
        "c_ident": np.eye(128, dtype=np.float32),
        "c_rope": np.ascontiguousarray(rope.reshape(128, 33 * 16)),
        "c_tri": np.ascontiguousarray(tri),
        "c_rmask": np.ascontiguousarray(rmask),
        "c_rmask16": np.ascontiguousarray(rmask16),
        "c_tri16": np.ascontiguousarray(tri16),
        "c_rowmask": np.ascontiguousarray(rowmask),
    }


def kernel(x_prompt, x_sample, cache_attn_k, cache_attn_v, state_hgrn, cache_mem_k, cache_mem_v,
           mem_prompt, w_in, lambda_q1, lambda_k1, lambda_q2, lambda_k2, attn_sub_norm,
           hgrn_lb_logits, hgrn_norm, w_mem_k, w_mem_v, w_branch_a, w_branch_b, w_branch_c,
           w_out, ln_gamma, ln_beta):
    f = lambda a: np.ascontiguousarray(np.asarray(a, dtype=np.float32))
    if "nc" not in _CACHE:
        _CACHE["nc"] = build_program()
    nc = _CACHE["nc"]
    consts = _consts()
    shared = {
        "w_in": f(w_in)[0], "w_mk": f(w_mem_k)[0], "w_mv": f(w_mem_v)[0],
        "w_a": f(w_branch_a)[0], "w_b": f(w_branch_b)[0], "w_c": f(w_branch_c)[0], "w_o": f(w_out)[0],
        "lam4": np.ascontiguousarray(np.concatenate([f(lambda_q1), f(lambda_k1), f(lambda_q2), f(lambda_k2)], 0)),
        "subn": f(attn_sub_norm), "lbl": f(hgrn_lb_logits), "hgn": f(hgrn_norm),
        "lng": f(ln_gamma), "lnb": f(ln_beta),
    }
    shared.update(consts)
    xp_, xs_ = f(x_prompt), f(x_sample)
    ck_, cv_ = f(cache_attn_k)[0], f(cache_attn_v)[0]
    s0_, cmk_, cmv_, mem_ = f(state_hgrn)[0], f(cache_mem_k)[0], f(cache_mem_v)[0], f(mem_prompt)
    in_maps = []
    for c in range(8):
        m = dict(shared)
        m["xp"] = xp_[c]
        m["xs"] = xs_[4 * c:4 * c + 4].reshape(64, D)
        m["ck"] = ck_[4 * c:4 * c + 4].reshape(4, SEQ, 1024)
        m["cv"] = cv_[4 * c:4 * c + 4].reshape(4, SEQ, 1024)
        m["s0"] = s0_[4 * c:4 * c + 4].reshape(32, 128, 128)
        m["cmk"] = cmk_[4 * c:4 * c + 4].reshape(4, 256, 1024)
        m["cmv"] = cmv_[4 * c:4 * c + 4].reshape(4, 256, 1024)
        m["mem"] = mem_[c]
        in_maps.append(m)
    res = run_bass_kernel_spmd(nc, in_maps, core_ids=list(range(8)))
    R = res.results
    cat = lambda name: np.stack([np.asarray(R[c][name]) for c in range(8)], 0)
    y_prompt = cat("y_p")
    y_sample = cat("y_s").reshape(32, 16, D)
    k_prompt = cat("k_p").reshape(1, 8, SEQ, 8, 2, 64)
    v_prompt = cat("v_p").reshape(1, 8, SEQ, 8, 128)
    hgrn_prompt = cat("hg_p").reshape(1, 8, 8, 128, 128)
    mem_k_prompt = cat("mk_p").reshape(1, 8, 256, 4, 256)
    mem_v_prompt = cat("mv_p").reshape(1, 8, 256, 4, 256)
    k_sample = cat("k_s").reshape(1, 32, 16, 8, 2, 64)
    v_sample = cat("v_s").reshape(1, 32, 16, 8, 128)
    hgrn_sample = cat("hg_s").reshape(1, 32, 8, 128, 128)
    return (y_prompt, y_sample, k_prompt, v_prompt, hgrn_prompt, mem_k_prompt, mem_v_prompt,
            k_sample, v_sample, hgrn_sample)
```

```python
import numpy as np
from contextlib import ExitStack
import concourse.bass as bass
import concourse.mybir as mybir
from concourse.bass_utils import run_bass_kernel_spmd

F32 = mybir.dt.float32
BF16 = mybir.dt.bfloat16
AF = mybir.ActivationFunctionType
ALU = mybir.AluOpType

D = 2048
SEQ = 4096
NIN = 17408
T = 512
NT = SEQ // T
ALPHA = 2.0 ** 0.25
EPS = 1e-5
LAM_INIT = 0.8 - 0.6 * 1.0
NU_BIG = 22
U_MERGE = 22
U_OUT = 38
U_MK = 42
U_MV = 44
NUNITS = 46
RUN_SAMPLE = True
JIT_TILE0 = True
USZ = 9216
BIG_ORDER = [0, 1, 2, 3, 4, 5, 6, 7, 8, 10, 12, 14, 16, 9, 11, 13, 15, 17, 18, 19, 20, 21]


class Sem:
    def __init__(self, h):
        self.h = h
        self.n = 0


class Buf:
    def __init__(self, name, t=None):
        self.name = name
        self.t = t
        self.w = None
        self.r = {}
        self.sem = None


class Eng:
    def __init__(self, name, eng, sem):
        self.name = name
        self.eng = eng
        self.sem = sem
        self.seen = {}


class K:
    def __init__(self, nc, es):
        self.nc = nc
        self.es = es
        self.nsem = 0
        self.pe = Eng("pe", nc.tensor, self.new_sem("pe"))
        self.act = Eng("act", nc.scalar, self.new_sem("act"))
        self.dve = Eng("dve", nc.vector, self.new_sem("dve"))
        self.pool = Eng("pool", nc.gpsimd, self.new_sem("pool"))
        self.sp = Eng("sp", nc.sync, self.new_sem("sp"))
        self.engines = [self.pe, self.act, self.dve, self.pool, self.sp]
        self.dma_sems = []
        self.out_evs = {}
        self.rr = 0
        self.cast_rr = 0
        self.local_bufs = []
        self.sem_pool = []

    def new_sem(self, name):
        self.nsem += 1
        return Sem(self.es.enter_context(self.nc.semaphore("s%d_%s" % (self.nsem, name))))

    def sb(self, name, shape, dt, es=None):
        self.nbuf = getattr(self, "nbuf", 0) + 1
        name = "%s_%d" % (name, self.nbuf)
        t = (es or self.es).enter_context(self.nc.sbuf_tensor(name, list(shape), dt))
        b = Buf(name, t)
        if es is not None:
            self.local_bufs.append(b)
        return b

    def wait(self, E, ev):
        sem, val = ev
        if E is self.pe and sem is E.sem:
            return
        if E.seen.get(sem, 0) >= val:
            return
        E.eng.wait_ge(sem.h, val)
        E.seen[sem] = val

    def _deps(self, E, rd, wr):
        for b in rd:
            if b.w is not None:
                for ev in (b.w if isinstance(b.w, list) else [b.w]):
                    self.wait(E, ev)
        for b in wr:
            if b.w is not None:
                for ev in (b.w if isinstance(b.w, list) else [b.w]):
                    self.wait(E, ev)
            for sem, val in b.r.items():
                self.wait(E, (sem, val))

    def _record(self, ev, rd, wr):
        sem, val = ev
        for b in rd:
            if b.r.get(sem, 0) < val:
                b.r[sem] = val
        for b in wr:
            b.w = ev
            b.r = {}

    def op(self, E, fn, rd=(), wr=(), sig=True):
        self._deps(E, rd, wr)
        inst = fn()
        if sig:
            E.sem.n += 1
            inst.then_inc(E.sem.h, 1)
            ev = (E.sem, E.sem.n)
        else:
            ev = (E.sem, E.sem.n + 1)
        self._record(ev, rd, wr)
        return ev

    def dma(self, Q, out, in_, rd=(), wr=(), sbuf=None, is_out=False):
        self._deps(Q, rd, wr)
        inst = Q.eng.dma_start(out=out, in_=in_)
        if sbuf.sem is None:
            if self.sem_pool:
                sbuf.sem = self.sem_pool.pop()
            else:
                sbuf.sem = self.new_sem("d%d" % len(self.dma_sems))
                self.dma_sems.append(sbuf.sem)
        sbuf.sem.n += 16
        inst.then_inc(sbuf.sem.h, 16)
        ev = (sbuf.sem, sbuf.sem.n)
        self._record(ev, rd, wr)
        if is_out:
            self.out_evs[sbuf.sem] = sbuf.sem.n
        return ev

    def barrier(self):
        self._barrier()
        for b in self.local_bufs:
            if b.sem is not None:
                self.sem_pool.append(b.sem)
                b.sem = None
        self.local_bufs = []

    def _barrier(self):
        evs = [(e.sem, e.sem.n) for e in self.engines if e.sem.n > 0]
        evs += [(s, s.n) for s in self.dma_sems if s.n > 0]
        for E in self.engines:
            for ev in evs:
                if ev[0] is E.sem:
                    continue
                self.wait(E, ev)

    def finish(self):
        for sem, val in self.out_evs.items():
            self.wait(self.sp, (sem, val))
        for e in self.engines:
            if e is not self.sp and e.sem.n > 0:
                self.wait(self.sp, (e.sem, e.sem.n))

    def mm(self, out_ap, lhsT, rhs, start, stop, rd, wr, sig):
        nc = self.nc
        return self.op(self.pe, lambda: nc.tensor.matmul(out_ap, lhsT=lhsT, rhs=rhs, start=start, stop=stop),
                       rd=rd, wr=wr, sig=sig)

    def tr(self, out_ap, in_ap, ident_ap, rd, wr, sig=True):
        nc = self.nc
        return self.op(self.pe, lambda: nc.tensor.transpose(out_ap, in_ap, ident_ap), rd=rd, wr=wr, sig=sig)

    def actf(self, out_ap, in_ap, func, rd, wr, scale=None, bias=None, accum=None):
        nc = self.nc
        kw = {}
        if scale is not None:
            kw["scale"] = scale
        if bias is not None:
            kw["bias"] = bias
        if accum is not None:
            kw["accum_out"] = accum
        return self.op(self.act, lambda: nc.scalar.activation(out=out_ap, in_=in_ap, func=func, **kw), rd=rd, wr=wr)

    def tt(self, E, out_ap, in0, in1, op, rd, wr):
        return self.op(E, lambda: E.eng.tensor_tensor(out=out_ap, in0=in0, in1=in1, op=op), rd=rd, wr=wr)

    def ts(self, E, out_ap, in0, s1, s2, op0, op1, rd, wr):
        if op1 is None:
            return self.op(E, lambda: E.eng.tensor_scalar(out=out_ap, in0=in0, scalar1=s1, scalar2=None, op0=op0),
                           rd=rd, wr=wr)
        return self.op(E, lambda: E.eng.tensor_scalar(out=out_ap, in0=in0, scalar1=s1, scalar2=s2, op0=op0, op1=op1),
                       rd=rd, wr=wr)

    def stt(self, out_ap, in0, scalar, in1, op0, op1, rd, wr):
        nc = self.nc
        return self.op(self.dve, lambda: nc.vector.scalar_tensor_tensor(out=out_ap, in0=in0, scalar=scalar, in1=in1,
                                                                       op0=op0, op1=op1), rd=rd, wr=wr)

    def copy(self, E, out_ap, in_ap, rd, wr):
        if E is self.act:
            return self.op(E, lambda: self.nc.scalar.copy(out=out_ap, in_=in_ap), rd=rd, wr=wr)
        return self.op(E, lambda: E.eng.tensor_copy(out=out_ap, in_=in_ap), rd=rd, wr=wr)

    def memset(self, E, ap, val, wr):
        return self.op(E, lambda: E.eng.memset(ap, val), rd=(), wr=wr)


def build_program():
    nc = bass.Bass("TRN2", target_bir_lowering=False)
    es = ExitStack()
    k = K(nc, es)

    def din(name, shape):
        return nc.dram_tensor(name, list(shape), F32, kind="ExternalInput").ap()

    def dout(name, shape):
        return nc.dram_tensor(name, list(shape), F32, kind="ExternalOutput").ap()

    xp = din("xp", [SEQ, D])
    xs = din("xs", [64, D])
    ck = din("ck", [4, SEQ, 1024])
    cv = din("cv", [4, SEQ, 1024])
    s0 = din("s0", [32, 128, 128])
    cmk = din("cmk", [4, 256, 1024])
    cmv = din("cmv", [4, 256, 1024])
    mem = din("mem", [256, D])
    w_in = din("w_in", [D, NIN])
    w_mk = din("w_mk", [D, 1024])
    w_mv = din("w_mv", [D, 1024])
    w_a = din("w_a", [1024, D])
    w_b = din("w_b", [1024, D])
    w_c = din("w_c", [1024, D])
    w_o = din("w_o", [D, D])
    lam4 = din("lam4", [4, 64])
    subn = din("subn", [1, 128])
    lbl = din("lbl", [2, 1024])
    hgn = din("hgn", [1, 1024])
    lng = din("lng", [1, D])
    lnb = din("lnb", [1, D])
    c_ident = din("c_ident", [128, 128])
    c_rope = din("c_rope", [128, 33 * 16])
    c_tri = din("c_tri", [128, 64])
    c_rmask = din("c_rmask", [128, 512])
    c_rmask16 = din("c_rmask16", [128, 64])
    c_tri16 = din("c_tri16", [64, 64])
    c_rowmask = din("c_rowmask", [64, 4])

    y_p = dout("y_p", [SEQ, D])
    y_s = dout("y_s", [64, D])
    k_p = dout("k_p", [SEQ, 1024])
    v_p = dout("v_p", [SEQ, 1024])
    hg_p = dout("hg_p", [8, 128, 128])
    mk_p = dout("mk_p", [256, 1024])
    mv_p = dout("mv_p", [256, 1024])
    k_s = dout("k_s", [64, 1024])
    v_s = dout("v_s", [64, 1024])
    hg_s = dout("hg_s", [32, 128, 128])

    WS = nc.dram_tensor("ws_scratch", [NUNITS, 128, USZ], BF16, kind="Internal").ap()
    KTP = nc.dram_tensor("ktp_scratch", [8, 128, SEQ], BF16, kind="Internal").ap()
    VP = nc.dram_tensor("vp_scratch", [8, 128, 32 * 130], BF16, kind="Internal").ap()
    KTS = nc.dram_tensor("kts_scratch", [32, 128, SEQ + 16], BF16, kind="Internal").ap()
    VS = nc.dram_tensor("vs_scratch", [32, 128, 33 * 130], BF16, kind="Internal").ap()
    ws_b = [Buf("ws%d" % u) for u in range(NUNITS)]
    ktp_b = [Buf("ktp%d" % h) for h in range(8)]
    vp_b = [Buf("vp%d" % h) for h in range(8)]
    kts_b = [Buf("kts%d" % i) for i in range(32)]
    vs_b = [Buf("vs%d" % i) for i in range(32)]

    ident = k.sb("ident", [128, 128], F32)
    rope = k.sb("rope", [128, 33, 16], F32)
    tri = k.sb("tri", [128, 64], F32)
    rmask = k.sb("rmask", [128, 512], F32)
    rmask16 = k.sb("rmask16", [128, 64], F32)
    tri16 = k.sb("tri16", [64, 64], F32)
    rowmask = k.sb("rowmask", [64, 4], F32)
    gain = k.sb("gain", [128, 1024], F32)
    sn = k.sb("sn", [128, 128], F32)
    ones_bf = k.sb("ones_bf", [128, 128], BF16)
    lamw = k.sb("lamw", [128, 4, 64], F32)
    lamv = k.sb("lamv", [128, 8], F32)
    lbt = k.sb("lbt", [128, 2, 8], F32)
    lbv = k.sb("lbv", [128, 2, 8], F32)
    m05 = k.sb("m05", [128, 1], F32)
    W = [k.sb("wslot%d" % i, [128, USZ], BF16) for i in range(2)]
    xT = k.sb("xT", [128, 16, T], BF16)
    yaT = k.sb("yaT", [128, 8, T], BF16)
    ybT = k.sb("ybT", [128, 8, T], BF16)
    ycT = k.sb("ycT", [128, 8, T], BF16)
    qTz = [None, None]
    zaTh = [None]
    Sst = [k.sb("S%d" % h, [128, 128], F32) for h in range(8)]
    Sbf = [[k.sb("Sbf%d_%d" % (h, p), [128, 128], BF16) for p in range(2)] for h in range(8)]
    mkT = k.sb("mkT", [128, 8, 256], BF16)
    mvb = k.sb("mvb", [128, 2, 1024], BF16)

    PS = es.enter_context(nc.psum_tensor("psum_all", [128, 4096], F32))
    PS3 = PS[:, :].rearrange("p (b n) -> p b n", n=512)
    banks = [Buf("bank%d" % i, PS[:, i * 512:(i + 1) * 512]) for i in range(8)]
    st = {"brot": list(range(8)), "bi": 0, "wi": 0, "si": 0}

    def getbank():
        b = banks[st["brot"][st["bi"] % len(st["brot"])]]
        st["bi"] += 1
        return b

    def rr_eng(engs):
        k.rr += 1
        return engs[k.rr % len(engs)]

    SP, PE, ACT, DVE, POOL = k.sp, k.pe, k.act, k.dve, k.pool

    def load_const(buf, src, shape_ap=None):
        k.dma(SP, buf.t[:] if shape_ap is None else shape_ap, src, rd=(), wr=(buf,), sbuf=buf)

    load_const(ident, c_ident)
    k.dma(SP, rope.t[:].rearrange("p a b -> p (a b)"), c_rope, wr=(rope,), sbuf=rope)
    load_const(tri, c_tri)
    load_const(rmask, c_rmask)
    load_const(rmask16, c_rmask16)
    load_const(tri16, c_tri16)
    load_const(rowmask, c_rowmask)
    k.dma(SP, gain.t[:], hgn[0:1, :].to_broadcast([128, 1024]), wr=(gain,), sbuf=gain)
    k.dma(SP, sn.t[:], subn[0:1, :].to_broadcast([128, 128]), wr=(sn,), sbuf=sn)
    for i in range(4):
        k.dma(SP, lamw.t[:, i, :], lam4[i:i + 1, :].to_broadcast([128, 64]), wr=(lamw,), sbuf=lamw)
    with nc.allow_non_contiguous_dma(reason="tiny lb logits transpose load"):
        for l in range(2):
            k.dma(SP, lbt.t[:, l, :], lbl[l:l + 1, :].rearrange("o (h d) -> d (o h)", d=128), wr=(lbt,), sbuf=lbt)
    k.memset(DVE, ones_bf.t[:], 1.0, wr=(ones_bf,))
    k.memset(DVE, m05.t[:], -0.5, wr=(m05,))
    k.ts(DVE, sn.t[:], sn.t[:], 1.0 - LAM_INIT, None, ALU.mult, None, rd=(sn,), wr=(sn,))
    with ExitStack() as les:
        lt = k.sb("lam_tmp", [128, 64], F32, les)
        for j in range(2):
            k.tt(DVE, lt.t[:], lamw.t[:, 2 * j, :], lamw.t[:, 2 * j + 1, :], ALU.mult, rd=(lamw,), wr=(lt,))
            k.op(DVE, lambda j=j: nc.vector.reduce_sum(out=lamv.t[:, j:j + 1], in_=lt.t[:],
                                                       axis=mybir.AxisListType.X), rd=(lt,), wr=(lamv,))
        k.actf(lamv.t[:, 2:4], lamv.t[:, 0:2], AF.Exp, rd=(lamv,), wr=(lamv,))
        k.tt(DVE, lamv.t[:, 4:5], lamv.t[:, 3:4], lamv.t[:, 2:3], ALU.subtract, rd=(lamv,), wr=(lamv,))
        k.ts(DVE, lamv.t[:, 4:5], lamv.t[:, 4:5], -LAM_INIT, None, ALU.add, None, rd=(lamv,), wr=(lamv,))
        k.tt(DVE, lbv.t[:, 0, :], lbt.t[:, 0, :], lbt.t[:, 1, :], ALU.subtract, rd=(lbt,), wr=(lbv,))
        k.actf(lbv.t[:, 0, :], lbv.t[:, 0, :], AF.Sigmoid, rd=(lbv,), wr=(lbv,))
        k.ts(DVE, lbv.t[:, 1, :], lbv.t[:, 0, :], -1.0, 1.0, ALU.mult, ALU.add, rd=(lbv,), wr=(lbv,))
        k.barrier()
    neg_lam = lamv.t[:, 4:5]

    w_in_v = w_in.rearrange("(kc p) n -> p kc n", p=128)
    w_o_v = w_o.rearrange("(kc p) n -> p kc n", p=128)
    w_mk_v = w_mk.rearrange("(kc p) n -> p kc n", p=128)
    w_mv_v = w_mv.rearrange("(kc p) n -> p kc n", p=128)
    w_abc_v = [w.rearrange("(kc p) n -> p kc n", p=128) for w in (w_a, w_b, w_c)]

    def unit_pieces(u):
        if u < NU_BIG or u >= U_OUT:
            if u < NU_BIG:
                src, c0 = w_in_v, 512 * BIG_ORDER[u]
            elif u < U_MK:
                src, c0 = w_o_v, 512 * (u - U_OUT)
            elif u < U_MV:
                src, c0 = w_mk_v, 512 * (u - U_MK)
            else:
                src, c0 = w_mv_v, 512 * (u - U_MV)
            return [(1024 * q, [(src[:, 2 * q:2 * q + 2, c0:c0 + 512], 2, 512)]) for q in range(8)]
        oc = u - U_MERGE
        g = [w_in_v[:, :, 11264 + 2048 * b + 128 * oc: 11264 + 2048 * b + 128 * oc + 128] for b in range(3)]
        wv = [w_abc_v[b][:, :, 128 * oc:128 * oc + 128] for b in range(3)]
        pcs = []
        for b in range(3):
            for q in range(2):
                pcs.append((2048 * b + 1024 * q, [(g[b][:, 8 * q:8 * q + 8, :], 8, 128)]))
        for b in range(3):
            pcs.append((6144 + 1024 * b, [(wv[b], 8, 128)]))
        return pcs

    stg = []
    cast_seq = [DVE, ACT, DVE, ACT, POOL]

    def wconvert(u):
        slot = W[st["wi"] % 2]
        st["wi"] += 1
        last = {}
        for (off, parts) in unit_pieces(u):
            sg = stg[st["si"] % len(stg)]
            E = cast_seq[st["si"] % len(cast_seq)]
            st["si"] += 1
            o = 0
            for (src, a, b) in parts:
                k.dma(SP, sg.t[:, o:o + a * b].rearrange("p (a b) -> p a b", b=b), src, wr=(sg,), sbuf=sg)
                o += a * b
            k._deps(E, (), (slot,))
            last[E] = k.op(E, (lambda E=E, off=off, o=o, sg=sg: (nc.scalar.copy(out=slot.t[:, off:off + o], in_=sg.t[:, 0:o])
                                                               if E is ACT else
                                                               E.eng.tensor_copy(out=slot.t[:, off:off + o], in_=sg.t[:, 0:o]))),
                           rd=(sg,), wr=())
        slot.w = list(last.values())
        slot.r = {}
        nel = USZ if U_MERGE <= u < U_OUT else 8192
        k.dma(ACT, WS[u, :, 0:nel], slot.t[:, 0:nel], rd=(slot,), wr=(ws_b[u],), sbuf=slot)
        return slot

    def wload(u):
        slot = W[st["wi"] % 2]
        st["wi"] += 1
        nel = USZ if U_MERGE <= u < U_OUT else 8192
        k.dma(SP, slot.t[:, 0:nel], WS[u, :, 0:nel], rd=(ws_b[u],), wr=(slot,), sbuf=slot)
        return slot

    class WStream:
        def __init__(self, order, jit=False):
            self.order = list(order)
            self.pos = 0
            self.loaded = []
            self.jit = jit
            self.bg = None
            self.ncall = 0

        def _load(self):
            u = self.order[self.pos]
            self.loaded.append(wconvert(u) if self.jit else wload(u))
            self.pos += 1

        def prefetch(self):
            pass

        def get(self):
            if not self.loaded:
                self._load()
            s = self.loaded.pop(0)
            if self.pos < len(self.order):
                self._load()
            if self.bg is not None:
                self.ncall += 1
                if self.ncall % 2 == 0:
                    self.bg()
            return s

    def evac(E, out_ap, in_ap, rd, wr):
        k.copy(E, out_ap, in_ap, rd=rd, wr=wr)

    def build_xT(x_src, subs, pes, nbuf=2):
        xst = [k.sb("xst%d" % i, [128, D], F32, pes) for i in range(nbuf)]
        for si_, (t0, n) in enumerate(subs):
            xb = xst[si_ % nbuf]
            k.dma(SP, xb.t[0:n, :], x_src[t0:t0 + n, :], wr=(xb,), sbuf=xb)
            for g in range(4):
                bk = getbank()
                for i in range(4):
                    kc = 4 * g + i
                    k.tr(bk.t[:, i * 128:i * 128 + n], xb.t[0:n, kc * 128:(kc + 1) * 128], ident.t[0:n, 0:n],
                         rd=(xb, ident), wr=(bk,), sig=(i == 3))
                E = rr_eng([ACT, DVE])
                evac(E, xT.t[:, 4 * g:4 * g + 4, t0:t0 + n],
                     bk.t[:, :].rearrange("p (a b) -> p a b", b=128)[:, :, 0:n], rd=(bk,), wr=(xT,))

    def proj_tm(slot, sub, dst_bank):
        t0, n = sub
        wv = slot.t[:, 0:8192].rearrange("p (kc c) -> p kc c", c=512)
        for kc in range(16):
            k.mm(dst_bank.t[0:n, :], xT.t[:, kc, t0:t0 + n], wv[:, kc, :], kc == 0, kc == 15,
                 rd=(xT, slot), wr=(dst_bank,), sig=(kc == 15))

    def proj_fm(slot, cb, ntok, dst_bank):
        wv = slot.t[:, 0:8192].rearrange("p (kc c) -> p kc c", c=512)
        for kc in range(16):
            k.mm(dst_bank.t[:, 0:ntok], wv[:, kc, cb * 128:(cb + 1) * 128], xT.t[:, kc, 0:ntok], kc == 0, kc == 15,
                 rd=(xT, slot), wr=(dst_bank,), sig=(kc == 15))

    def rope_apply(tm, n, S):
        v = tm.t[0:n, :].rearrange("p (m d) -> p m d", d=64)
        x1, x2 = v[:, :, 0:8], v[:, :, 8:16]
        cs = rope.t[0:n, S, 0:8].unsqueeze(1).to_broadcast([n, 16, 8])
        sn_ = rope.t[0:n, S, 8:16].unsqueeze(1).to_broadcast([n, 16, 8])
        return x1, x2, cs, sn_

    def phase_a(ws, subs, ntok, rope_slots, k_out, v_out, store_kt, store_v, pes):
        tm = [k.sb("tm%d" % i, [128, 1024], F32, pes) for i in range(len(subs))]
        rt = [k.sb("ropet%d" % i, [128, 16, 8], F32, pes) for i in range(4)]
        ktst = k.sb("ktst", [128, 8, T], BF16, pes)
        vst = k.sb("vst", [128, 4, 8, 130], BF16, pes)
        k.memset(POOL, vst.t[:, :, :, 128:129], 1.0, wr=(vst,))
        k.memset(POOL, vst.t[:, :, :, 129:130], 0.0, wr=(vst,))
        for part in range(3):
            for half in range(2):
                ws.prefetch()
                slot = ws.get()
                ws.prefetch()
                for si_, sub in enumerate(subs):
                    bk = getbank()
                    proj_tm(slot, sub, bk)
                    n = sub[1]
                    E = rr_eng([ACT, DVE])
                    evac(E, tm[si_].t[0:n, half * 512:(half + 1) * 512], bk.t[0:n, :], rd=(bk,), wr=(tm[si_],))
            for si_, (t0, n) in enumerate(subs):
                tmb = tm[si_]
                if part < 2:
                    x1, x2, cs, sn_ = rope_apply(tmb, n, rope_slots[si_])
                    a, b_, c, d_ = [r.t[0:n] for r in rt]
                    rtb = tuple(rt)
                    k.tt(DVE, a, x1, cs, ALU.mult, rd=(tmb, rope), wr=(rt[0],))
                    k.tt(DVE, b_, x2, sn_, ALU.mult, rd=(tmb, rope), wr=(rt[1],))
                    k.tt(DVE, c, x2, cs, ALU.mult, rd=(tmb, rope), wr=(rt[2],))
                    k.tt(DVE, d_, x1, sn_, ALU.mult, rd=(tmb, rope), wr=(rt[3],))
                    k.tt(DVE, x1, a, b_, ALU.subtract, rd=(rt[0], rt[1]), wr=(tmb,))
                    k.tt(DVE, x2, c, d_, ALU.add, rd=(rt[2], rt[3]), wr=(tmb,))
                    if part == 1:
                        k.dma(ACT, k_out[t0:t0 + n, :], tmb.t[0:n, :], rd=(tmb,), sbuf=tmb, is_out=True)
                    for g in range(2):
                        bk = getbank()
                        for i in range(4):
                            h = 4 * g + i
                            k.tr(bk.t[:, i * 128:i * 128 + n], tmb.t[0:n, h * 128:(h + 1) * 128], ident.t[0:n, 0:n],
                                 rd=(tmb, ident), wr=(bk,), sig=(i == 3))
                        bv = bk.t[:, :].rearrange("p (a b) -> p a b", b=128)
                        if part == 0:
                            evac(ACT, qTz[0].t[0:64, 4 * g:4 * g + 4, t0:t0 + n], bv[0:64, :, 0:n], rd=(bk,), wr=(qTz[0],))
                            evac(DVE, qTz[1].t[64:128, 4 * g:4 * g + 4, t0:t0 + n], bv[64:128, :, 0:n], rd=(bk,), wr=(qTz[1],))
                        else:
                            E = rr_eng([ACT, DVE])
                            evac(E, ktst.t[:, 4 * g:4 * g + 4, t0:t0 + n], bv[:, :, 0:n], rd=(bk,), wr=(ktst,))
                else:
                    k.dma(ACT, v_out[t0:t0 + n, :], tmb.t[0:n, :], rd=(tmb,), sbuf=tmb, is_out=True)
                    E = rr_eng([ACT, DVE])
                    evac(E, vst.t[0:n, si_, :, 0:128], tmb.t[0:n, :].rearrange("p (h e) -> p h e", e=128),
                         rd=(tmb,), wr=(vst,))
            if part == 1:
                store_kt(ktst)
            if part == 2:
                store_v(vst)

    AX = mybir.AxisListType.X

    def group_store(dsts_srcs, src_buf, dst_bufs):
        ev = None
        for (dst, src) in dsts_srcs:
            ev = k.dma(ACT, dst, src, rd=(src_buf,), wr=(), sbuf=src_buf)
        for b in dst_bufs:
            for sem, val in list(b.r.items()):
                pass
            b.w = ev
            b.r = {}

    def rstd_from(ss_ap, buf, scale, n):
        k.ts(DVE, ss_ap, ss_ap, scale, EPS, ALU.mult, ALU.add, rd=(buf,), wr=(buf,))
        k.tt(POOL, ss_ap, ss_ap, m05.t[0:n, :], ALU.pow, rd=(buf, m05), wr=(buf,))

    def phase_z(ws, ntok):
        for half in range(2):
            ws.prefetch()
            slot = ws.get()
            ws.prefetch()
            for cb in range(4):
                bk = getbank()
                proj_fm(slot, cb, ntok, bk)
                k.actf(zaTh[0].t[:, half * 4 + cb, 0:ntok], bk.t[:, 0:ntok], AF.Silu, rd=(bk,), wr=(zaTh[0],))

    def attn_post(O, nq, m, h, qcol0, o1b, o2b, onb, stb):
        if m == 0:
            k.op(DVE, lambda: nc.vector.reciprocal(out=stb.t[0:nq, 0:1], in_=O.t[0:nq, 128:129]), rd=(O,), wr=(stb,))
            k.ts(DVE, o1b.t[0:nq, :], O.t[0:nq, 0:128], stb.t[0:nq, 0:1], None, ALU.mult, None, rd=(O, stb), wr=(o1b,))
            return
        k.op(DVE, lambda: nc.vector.reciprocal(out=stb.t[0:nq, 1:2], in_=O.t[0:nq, 128:129]), rd=(O,), wr=(stb,))
        k.tt(DVE, stb.t[0:nq, 1:2], stb.t[0:nq, 1:2], neg_lam[0:nq, :], ALU.mult, rd=(stb, lamv), wr=(stb,))
        k.stt(o2b.t[0:nq, :], O.t[0:nq, 0:128], stb.t[0:nq, 1:2], o1b.t[0:nq, :], ALU.mult, ALU.add,
              rd=(O, stb, o1b), wr=(o2b,))
        k.actf(onb.t[0:nq, :], o2b.t[0:nq, :], AF.Square, rd=(o2b,), wr=(onb, stb), accum=stb.t[0:nq, 2:3])
        rstd_from(stb.t[0:nq, 2:3], stb, 1.0 / 128.0, nq)
        k.stt(onb.t[0:nq, :], o2b.t[0:nq, :], stb.t[0:nq, 2:3], sn.t[0:nq, :], ALU.mult, ALU.mult,
              rd=(o2b, stb, sn), wr=(onb,))
        tb = getbank()
        k.tr(tb.t[:, 0:nq], onb.t[0:nq, :], ident.t[0:nq, 0:nq], rd=(onb, ident), wr=(tb,))
        k.tt(DVE, yaT.t[:, h, qcol0:qcol0 + nq], tb.t[:, 0:nq], zaTh[0].t[:, h, qcol0:qcol0 + nq], ALU.mult,
             rd=(tb, zaTh[0]), wr=(yaT,))

    def attn_prompt(j, pes):
        nk = 4 * (j + 1)
        nfull = 4 * j
        KTh = [k.sb("KTh%d" % i, [128, SEQ], BF16, pes) for i in range(2)]
        Vh = [k.sb("Vh%d" % i, [128, 32, 130], BF16, pes) for i in range(2)]
        PT = [k.sb("PT%d" % i, [128, 2, 512], BF16, pes) for i in range(3)]
        o1b = [k.sb("o1b%d" % i, [128, 128], F32, pes) for i in range(4)]
        o2b = [k.sb("o2b%d" % i, [128, 128], F32, pes) for i in range(2)]
        onb = [k.sb("onb%d" % i, [128, 128], F32, pes) for i in range(2)]
        stb = [k.sb("stb%d" % i, [128, 4], F32, pes) for i in range(4)]
        st["brot"] = [0, 1, 2, 3]
        ob = banks[4:8]

        def load(h):
            k.dma(SP, KTh[h % 2].t[:, 0:nk * 128], KTP_v[h, :, 0:nk * 128], rd=(ktp_b[h],), wr=(KTh[h % 2],), sbuf=KTh[h % 2])
            k.dma(SP, Vh[h % 2].t[:, 0:nk, :], VP_v[h, :, 0:nk, :], rd=(vp_b[h],), wr=(Vh[h % 2],), sbuf=Vh[h % 2])

        items = [(kt, kt + 1) for kt in range(0, nfull, 2)] + [(kt,) for kt in range(nfull, nk)]
        flat = [(h, m, it, idx == len(items) - 1) for h in range(8) for m in range(2) for idx, it in enumerate(items)]
        load(0)
        load(1)
        pend = []
        pi = 0

        def emit_pv(ent):
            h, m, item_, last, d_, pt_ = ent
            V = Vh[h % 2]
            for ii, kt_ in enumerate(item_):
                for qs in range(d_, 4):
                    k.mm(ob[qs].t[:, 0:130], pt_.t[:, ii, (qs - d_) * 128:(qs - d_ + 1) * 128], V.t[:, kt_, :],
                         kt_ == 0, kt_ == nfull + qs, rd=(pt_, V), wr=(ob[qs],),
                         sig=(qs == 3 and ii == len(item_) - 1))
            if last:
                for qs in range(4):
                    attn_post(ob[qs], 128, m, h, qs * 128, o1b[qs], o2b[qs % 2], onb[qs % 2], stb[qs])
                if m == 1 and h + 2 < 8:
                    load(h + 2)

        for (h, m, item, last) in flat:
            KT = KTh[h % 2]
            b0 = 2 * (pi % 2)
            pt = PT[pi % 3]
            pi += 1
            d = max(0, item[0] - nfull)
            q0 = 128 * d
            N = 512 - q0
            bks = [banks[b0 + ii] for ii in range(len(item))]
            for ii, kt in enumerate(item):
                k.mm(bks[ii].t[:, 0:N], KT.t[:, kt * 128:(kt + 1) * 128], qTz[m].t[:, h, q0:512], True, True,
                     rd=(KT, qTz[m]), wr=(bks[ii],), sig=True)
            if len(item) == 2:
                k.actf(pt.t[:, :, :], PS3[:, b0:b0 + 2, :], AF.Exp, rd=tuple(bks), wr=(pt,), scale=0.125)
            else:
                k.actf(pt.t[:, 0, 0:N], bks[0].t[:, 0:N], AF.Exp, rd=tuple(bks), wr=(pt,), scale=0.125)
                k.memset(POOL, pt.t[64:128, 0, 0:64], 0.0, wr=(pt,))
            pend.append((h, m, item, last, d, pt))
            if len(pend) > 2:
                emit_pv(pend.pop(0))
        while pend:
            emit_pv(pend.pop(0))
        st["brot"] = list(range(8))

    def hgrn_prep(h, hh, ntok, csz, qh, ff, tmp, rm, qdT, kdT, qs_writer, ksf, decay):
        nch = ntok // csz
        mid = (csz - 1) // 2
        tg, tk, tb_, t1, e1, e3 = [t_.t[:, 0:ntok] for t_ in tmp[0:6]]
        Tg, Tk, Tb, T1, E1b, E3b = tmp[0:6]
        k.actf(tg, ff.t[:, hh, 0:ntok], AF.Ln, rd=(ff,), wr=(Tg,))
        k.ts(DVE, tk, ff.t[:, hh, 0:ntok], -1.0, 1.0, ALU.mult, ALU.add, rd=(ff,), wr=(Tk,))
        k.op(DVE, lambda: nc.vector.tensor_tensor_scan(out=tb_, data0=rm.t[:, 0:ntok], data1=tg, initial=0.0,
                                                       op0=ALU.mult, op1=ALU.add), rd=(rm, Tg), wr=(Tb,))
        b3 = tb_.rearrange("p (c t) -> p c t", t=csz)
        k.tt(DVE, t1.rearrange("p (c t) -> p c t", t=csz), b3, b3[:, :, mid:mid + 1].to_broadcast([128, nch, csz]),
             ALU.subtract, rd=(Tb,), wr=(T1,))
        k.actf(e1, t1, AF.Exp, rd=(T1,), wr=(E1b,))
        k.tt(POOL, qdT.t[:, hh, 0:ntok], qh.t[:, hh, 0:ntok], e1, ALU.mult, rd=(qh, E1b), wr=(qdT,))
        k.actf(e1, t1, AF.Exp, rd=(T1,), wr=(E1b,), scale=-1.0)
        k.tt(POOL, kdT.t[:, hh, 0:ntok], tk, e1, ALU.mult, rd=(Tk, E1b), wr=(kdT,))
        k.actf(e3, tb_, AF.Exp, rd=(Tb,), wr=(E3b,))
        qs_writer(hh, qh.t[:, hh, 0:ntok], e3, qh, E3b)
        k.copy(POOL, decay.t[:, hh, 0:nch], e3.rearrange("p (c t) -> p c t", t=csz)[:, :, csz - 1], rd=(E3b,), wr=(decay,))
        k.tt(DVE, t1.rearrange("p (c t) -> p c t", t=csz), b3[:, :, csz - 1:csz].to_broadcast([128, nch, csz]), b3,
             ALU.subtract, rd=(Tb,), wr=(T1,))
        k.actf(e1, t1, AF.Exp, rd=(T1,), wr=(E1b,))
        k.tt(DVE, ksf.t[:, 0:ntok], tk, e1, ALU.mult, rd=(Tk, E1b), wr=(ksf,))

    def ob_post1(obank, c0, nq, h, onb, stb, si):
        k.actf(onb.t[0:nq, :], obank.t[0:nq, c0:c0 + 128], AF.Square, rd=(obank,), wr=(onb, stb), accum=stb.t[0:nq, si:si + 1])
        rstd_from(stb.t[0:nq, si:si + 1], stb, 1.0 / 128.0, nq)
        k.stt(onb.t[0:nq, :], obank.t[0:nq, c0:c0 + 128], stb.t[0:nq, si:si + 1], gain.t[0:nq, h * 128:(h + 1) * 128],
              ALU.mult, ALU.mult, rd=(obank, stb, gain), wr=(onb,))

    def ob_post2(nq, h, gz_ap, gz_buf, qcol0, onb):
        tb = getbank()
        k.tr(tb.t[:, 0:nq], onb.t[0:nq, :], ident.t[0:nq, 0:nq], rd=(onb, ident), wr=(tb,))
        k.tt(DVE, ybT.t[:, h, qcol0:qcol0 + nq], tb.t[:, 0:nq], gz_ap, ALU.mult, rd=(tb, gz_buf), wr=(ybT,))

    def ob_post(obank, c0, nq, h, gz_ap, gz_buf, qcol0, onb, stb, si):
        ob_post1(obank, c0, nq, h, onb, stb, si)
        ob_post2(nq, h, gz_ap, gz_buf, qcol0, onb)

    def hgrn_proj(ws, g, ntok, subs, qh, ff, vB, gz, og, tmp, preps):
        preps = list(preps)
        for which in range(5):
            ws.prefetch()
            slot = ws.get()
            ws.prefetch()
            if which == 2:
                for si_, sub in enumerate(subs):
                    bk = getbank()
                    proj_tm(slot, sub, bk)
                    evac(rr_eng([ACT, DVE]), vB.t[0:sub[1], si_, :], bk.t[0:sub[1], :], rd=(bk,), wr=(vB,))
                    if preps:
                        preps.pop(0)()
                while preps:
                    preps.pop(0)()
                continue
            for hh in range(4):
                h = 4 * g + hh
                bk = getbank()
                proj_fm(slot, hh, ntok, bk)
                src = bk.t[:, 0:ntok]
                if which == 0:
                    k.actf(qh.t[:, hh, 0:ntok], src, AF.Silu, rd=(bk,), wr=(qh,))
                elif which == 1:
                    k.actf(ff.t[:, hh, 0:ntok], src, AF.Sigmoid, rd=(bk,), wr=(ff,))
                    k.ts(DVE, ff.t[:, hh, 0:ntok], ff.t[:, hh, 0:ntok], lbv.t[:, 1, h:h + 1], lbv.t[:, 0, h:h + 1],
                         ALU.mult, ALU.add, rd=(ff, lbv), wr=(ff,))
                elif which == 3:
                    k.actf(og.t[:, hh, 0:ntok], src, AF.Sigmoid, rd=(bk,), wr=(og,))
                else:
                    k.actf(tmp[5].t[:, 0:ntok], src, AF.Silu, rd=(bk,), wr=(tmp[5],))
                    k.tt(DVE, og.t[:, hh, 0:ntok], tmp[5].t[:, 0:ntok], og.t[:, hh, 0:ntok], ALU.mult,
                         rd=(tmp[5], og), wr=(og,))

    def phase_b_prompt(ws, j):
        for g in range(2):
            with ExitStack() as pes:
                qh = k.sb("qh", [128, 4, T], F32, pes)
                ff = k.sb("ff", [128, 4, T], F32, pes)
                vB = k.sb("vB", [128, 4, 512], BF16, pes)
                gz = None
                og = k.sb("og", [128, 4, T], BF16, pes)
                qdT = k.sb("qdT", [128, 4, T], BF16, pes)
                kdT = k.sb("kdT", [128, 4, T], BF16, pes)
                qsE = k.sb("qsE", [128, 4, 4, 128], BF16, pes)
                qsO = k.sb("qsO", [128, 4, 4, 128], BF16, pes)
                ks_tm = k.sb("ks_tm", [128, 4, 4, 128], BF16, pes)
                decay = k.sb("decay", [128, 4, 8], F32, pes)
                tmp = [k.sb("htmp%d" % i, [128, T], F32, pes) for i in range(6)]
                scTs = [k.sb("scT%d" % i, [128, 4, 128], BF16, pes) for i in range(2)]
                onb = [k.sb("onbB%d" % i, [128, 128], F32, pes) for i in range(8)]
                stbs = [k.sb("stbB%d" % i, [128, 2], F32, pes) for i in range(8)]
                for s_ in scTs:
                    k.memset(POOL, s_.t[:], 0.0, wr=(s_,))
                k.memset(POOL, qsE.t[:], 0.0, wr=(qsE,))
                k.memset(POOL, qsO.t[:], 0.0, wr=(qsO,))
                ksf = [k.sb("ksf%d" % i, [128, T], F32, pes) for i in range(4)]

                def qs_writer(hh, q_ap, e3_ap, qbuf, ebuf):
                    qv = q_ap.rearrange("p (s two t) -> p s two t", two=2, t=64)
                    ev_ = e3_ap.rearrange("p (s two t) -> p s two t", two=2, t=64)
                    k.tt(DVE, qsE.t[:, hh, :, 0:64], qv[:, :, 0, :], ev_[:, :, 0, :], ALU.mult, rd=(qbuf, ebuf), wr=(qsE,))
                    k.tt(DVE, qsO.t[:, hh, :, 64:128], qv[:, :, 1, :], ev_[:, :, 1, :], ALU.mult, rd=(qbuf, ebuf), wr=(qsO,))

                def ks_writer(hh, Tg):
                    bk = getbank()
                    for s_ in range(4):
                        k.tr(bk.t[:, s_ * 128:(s_ + 1) * 128], Tg.t[:, s_ * 128:(s_ + 1) * 128], ident.t[:, :],
                             rd=(Tg, ident), wr=(bk,), sig=(s_ == 3))
                    evac(rr_eng([ACT, DVE]), ks_tm.t[:, hh, :, :], bk.t[:, :].rearrange("p (s d) -> p s d", d=128), rd=(bk,), wr=(ks_tm,))

                preps = [lambda hh=hh: hgrn_prep(4 * g + hh, hh, T, 64, qh, ff, tmp, rmask, qdT, kdT, qs_writer, ksf[hh], decay)
                         for hh in range(4)]
                hgrn_proj(ws, g, T, subs4, qh, ff, vB, gz, og, tmp, preps)
                for hh in range(4):
                    ks_writer(hh, ksf[hh])

                def post2(cp_):
                    cs2 = slice(cp_ * 128, (cp_ + 1) * 128)
                    for hh in range(4):
                        ob_post2(128, 4 * g + hh, og.t[:, hh, cs2], og, cp_ * 128, onb[(cp_ % 2) * 4 + hh])

                for cp in range(4):
                    cs_ = slice(cp * 128, (cp + 1) * 128)
                    scb = getbank()
                    for hh in range(4):
                        k.mm(scb.t[:, hh * 128:(hh + 1) * 128], kdT.t[:, hh, cs_], qdT.t[:, hh, cs_], True, True,
                             rd=(kdT, qdT), wr=(scb,), sig=(hh == 3))
                    scT = scTs[cp % 2]
                    scv = scb.t[:, :].rearrange("p (h t) -> p h t", t=128)
                    k.tt(DVE, scT.t[0:64, :, 0:64], scv[0:64, :, 0:64], tri.t[0:64, :].unsqueeze(1).to_broadcast([64, 4, 64]),
                         ALU.mult, rd=(scb, tri), wr=(scT,))
                    k.tt(DVE, scT.t[64:128, :, 64:128], scv[64:128, :, 64:128],
                         tri.t[64:128, :].unsqueeze(1).to_broadcast([64, 4, 64]), ALU.mult, rd=(scb, tri), wr=(scT,))
                    for par in range(2):
                        ps_ = slice(par * 64, (par + 1) * 64)
                        dsb = getbank()
                        for hh in range(4):
                            k.mm(dsb.t[:, hh * 128:(hh + 1) * 128], ks_tm.t[ps_, hh, cp, :], vB.t[ps_, cp, hh * 128:(hh + 1) * 128],
                                 True, True, rd=(ks_tm, vB), wr=(dsb,), sig=(hh == 3))
                        if par == 0 and cp > 0:
                            post2(cp - 1)
                        if par == 1:
                            obk = getbank()
                            for hh in range(4):
                                h = 4 * g + hh
                                oc_ = slice(hh * 128, (hh + 1) * 128)
                                k.mm(obk.t[:, oc_], scT.t[:, hh, :], vB.t[:, cp, oc_], True, False, rd=(scT, vB), wr=(obk,), sig=False)
                                k.mm(obk.t[:, oc_], qsE.t[:, hh, cp, :], Sbf[h][1].t[:], False, False, rd=(qsE, Sbf[h][1]), wr=(obk,), sig=False)
                                k.mm(obk.t[:, oc_], qsO.t[:, hh, cp, :], Sbf[h][0].t[:], False, True, rd=(qsO, Sbf[h][0]), wr=(obk,), sig=True)
                        for hh in range(4):
                            h = 4 * g + hh
                            k.stt(Sst[h].t[:], Sst[h].t[:], decay.t[:, hh, 2 * cp + par:2 * cp + par + 1],
                                  dsb.t[:, hh * 128:(hh + 1) * 128], ALU.mult, ALU.add, rd=(Sst[h], decay, dsb), wr=(Sst[h],))
                            k.copy(rr_eng([ACT, POOL]), Sbf[h][par].t[:], Sst[h].t[:], rd=(Sst[h],), wr=(Sbf[h][par],))
                    for hh in range(4):
                        ob_post1(obk, hh * 128, 128, 4 * g + hh, onb[(cp % 2) * 4 + hh], stbs[(cp % 2) * 4 + hh], 0)
                post2(3)
                k.barrier()

    def phase_c(ws, ntok, groups, pes):
        qcT = k.sb("qcT", [128, 8, ntok], BF16, pes)
        zcs = k.sb("zcs", [128, 8, ntok], BF16, pes)
        PTm = [k.sb("PTm%d" % i, [128, 2, ntok], BF16, pes) for i in range(2)]
        rs = [k.sb("rsC%d" % i, [128, ntok], F32, pes) for i in range(2)]
        tc_ = [k.sb("tmpC%d" % i, [128, ntok], F32, pes) for i in range(2)]
        for which in range(2):
            for half in range(2):
                ws.prefetch()
                slot = ws.get()
                ws.prefetch()
                for cb in range(4):
                    bk = getbank()
                    proj_fm(slot, cb, ntok, bk)
                    if which == 0:
                        evac(rr_eng([ACT, DVE]), qcT.t[:, half * 4 + cb, 0:ntok], bk.t[:, 0:ntok], rd=(bk,), wr=(qcT,))
                    else:
                        k.actf(zcs.t[:, half * 4 + cb, 0:ntok], bk.t[:, 0:ntok], AF.Silu, rd=(bk,), wr=(zcs,))
        it = 0
        for (c0, n, mkb, mvb_) in groups:
            cs_ = slice(c0, c0 + n)
            for h in range(4):
                pt = PTm[it % 2]
                r_ = rs[it % 2]
                it += 1
                for mt in range(2):
                    sbk = getbank()
                    for half in range(2):
                        k.mm(sbk.t[:, 0:n], mkb.t[:, 2 * h + half, mt * 128:(mt + 1) * 128], qcT.t[:, 2 * h + half, cs_],
                             half == 0, half == 1, rd=(mkb, qcT), wr=(sbk,), sig=(half == 1))
                    k.actf(pt.t[:, mt, 0:n], sbk.t[:, 0:n], AF.Exp, rd=(sbk,), wr=(pt,), scale=1.0 / 16.0)
                smb = getbank()
                for mt in range(2):
                    k.mm(smb.t[:, 0:n], ones_bf.t[:, :], pt.t[:, mt, 0:n], mt == 0, mt == 1, rd=(ones_bf, pt), wr=(smb,), sig=(mt == 1))
                k.op(DVE, lambda: nc.vector.reciprocal(out=r_.t[:, 0:n], in_=smb.t[:, 0:n]), rd=(smb,), wr=(r_,))
                for eh in range(2):
                    obk = getbank()
                    ch = 2 * h + eh
                    for mt in range(2):
                        k.mm(obk.t[:, 0:n], mvb_.t[:, mt, ch * 128:(ch + 1) * 128], pt.t[:, mt, 0:n], mt == 0, mt == 1,
                             rd=(mvb_, pt), wr=(obk,), sig=(mt == 1))
                    t_ = tc_[eh]
                    k.tt(DVE, t_.t[:, 0:n], obk.t[:, 0:n], r_.t[:, 0:n], ALU.mult, rd=(obk, r_), wr=(t_,))
                    k.tt(POOL, ycT.t[:, ch, cs_], t_.t[:, 0:n], zcs.t[:, ch, cs_], ALU.mult, rd=(t_, zcs), wr=(ycT,))

    def phase_merge_out(ws, ntok, subs, x_src, y_dst, pes, next_x=None):
        mergedT = k.sb("mergedT", [128, 16, T], BF16, pes)
        mes = ExitStack()
        xr = [k.sb("xr%d" % i, [128, D], F32, pes) for i in range(len(subs))]
        stats = [k.sb("lnstat%d" % i, [128, 4, 6], F32, pes) for i in range(len(subs))]
        mvs = [k.sb("lnmv%d" % i, [128, 4], F32, pes) for i in range(len(subs))]
        gam = k.sb("gam", [128, D], F32, pes)
        bet = k.sb("bet", [128, D], F32, pes)
        k.dma(SP, gam.t[:], lng[0:1, :].to_broadcast([128, D]), wr=(gam,), sbuf=gam)
        k.dma(SP, bet.t[:], lnb[0:1, :].to_broadcast([128, D]), wr=(bet,), sbuf=bet)
        sg = [k.sb("sgM%d" % i, [128, T], F32, mes) for i in range(2)]
        acc = [k.sb("accM%d" % i, [128, T], F32, mes) for i in range(2)]
        tmpm = [k.sb("tmpM%d" % i, [128, T], F32, mes) for i in range(2)]
        for si_, (t0, n) in enumerate(subs):
            k.dma(SP, xr[si_].t[0:n, :], x_src[t0:t0 + n, :], wr=(xr[si_],), sbuf=xr[si_])
        yTs = (yaT, ybT, ycT)
        for oc in range(16):
            ws.prefetch()
            slot = ws.get()
            ws.prefetch()
            a_ = acc[oc % 2]
            for br in range(3):
                gb = getbank()
                gv = slot.t[:, br * 2048:(br + 1) * 2048].rearrange("p (kc c) -> p kc c", c=128)
                for kc in range(16):
                    k.mm(gb.t[:, 0:ntok], gv[:, kc, :], xT.t[:, kc, 0:ntok], kc == 0, kc == 15, rd=(slot, xT), wr=(gb,), sig=(kc == 15))
                yb_ = getbank()
                wv = slot.t[:, 6144 + br * 1024:6144 + (br + 1) * 1024].rearrange("p (kc c) -> p kc c", c=128)
                for kc in range(8):
                    k.mm(yb_.t[:, 0:ntok], wv[:, kc, :], yTs[br].t[:, kc, 0:ntok], kc == 0, kc == 7, rd=(slot, yTs[br]), wr=(yb_,), sig=(kc == 7))
                s_ = sg[(3 * oc + br) % 2]
                k.actf(s_.t[:, 0:ntok], gb.t[:, 0:ntok], AF.Sigmoid, rd=(gb,), wr=(s_,))
                if br == 0:
                    k.tt(DVE, a_.t[:, 0:ntok], s_.t[:, 0:ntok], yb_.t[:, 0:ntok], ALU.mult, rd=(s_, yb_), wr=(a_,))
                else:
                    t_ = tmpm[br - 1]
                    k.tt(DVE, t_.t[:, 0:ntok], s_.t[:, 0:ntok], yb_.t[:, 0:ntok], ALU.mult, rd=(s_, yb_), wr=(t_,))
                    if br == 1:
                        k.tt(POOL, a_.t[:, 0:ntok], a_.t[:, 0:ntok], t_.t[:, 0:ntok], ALU.add, rd=(a_, t_), wr=(a_,))
                    else:
                        k.tt(POOL, mergedT.t[:, oc, 0:ntok], a_.t[:, 0:ntok], t_.t[:, 0:ntok], ALU.add, rd=(a_, t_), wr=(mergedT,))
        k._deps(SP, (), tuple(sg + acc + tmpm))
        mes.close()
        for cb in range(4):
            ws.prefetch()
            slot = ws.get()
            ws.prefetch()
            wv = slot.t[:, 0:8192].rearrange("p (kc c) -> p kc c", c=512)
            for si_, (t0, n) in enumerate(subs):
                bk = getbank()
                for kc in range(16):
                    k.mm(bk.t[0:n, :], mergedT.t[:, kc, t0:t0 + n], wv[:, kc, :], kc == 0, kc == 15, rd=(mergedT, slot), wr=(bk,), sig=(kc == 15))
                xc = xr[si_].t[0:n, cb * 512:(cb + 1) * 512]
                k.stt(xc, xc, ALPHA, bk.t[0:n, :], ALU.mult, ALU.add, rd=(xr[si_], bk), wr=(xr[si_],))
        if next_x is not None:
            build_xT(next_x, subs4, pes, nbuf=1)
        for si_, (t0, n) in enumerate(subs):
            xb = xr[si_]
            stat, mv_ = stats[si_], mvs[si_]
            for c in range(4):
                k.op(DVE, lambda c=c: nc.vector.bn_stats(out=stat.t[0:n, c, :], in_=xb.t[0:n, c * 512:(c + 1) * 512]), rd=(xb,), wr=(stat,))
            k.op(DVE, lambda: nc.vector.bn_aggr(out=mv_.t[0:n, 0:2], in_=stat.t[0:n, :, :].rearrange("p a b -> p (a b)")), rd=(stat,), wr=(mv_,))
            k.ts(DVE, mv_.t[0:n, 2:3], mv_.t[0:n, 1:2], EPS, None, ALU.add, None, rd=(mv_,), wr=(mv_,))
            k.tt(POOL, mv_.t[0:n, 2:3], mv_.t[0:n, 2:3], m05.t[0:n, :], ALU.pow, rd=(mv_, m05), wr=(mv_,))
            k.ts(DVE, mv_.t[0:n, 3:4], mv_.t[0:n, 0:1], mv_.t[0:n, 2:3], -1.0, ALU.mult, ALU.mult, rd=(mv_,), wr=(mv_,))
            k.actf(xb.t[0:n, :], xb.t[0:n, :], AF.Identity, rd=(xb, mv_), wr=(xb,), scale=mv_.t[0:n, 2:3], bias=mv_.t[0:n, 3:4])
            k.tt(DVE, xb.t[0:n, :], xb.t[0:n, :], gam.t[0:n, :], ALU.mult, rd=(xb, gam), wr=(xb,))
            k.tt(POOL, xb.t[0:n, :], xb.t[0:n, :], bet.t[0:n, :], ALU.add, rd=(xb, bet), wr=(xb,))
            k.dma(ACT, y_dst[t0:t0 + n, :], xb.t[0:n, :], rd=(xb,), sbuf=xb, is_out=True)

    def mem_transposes(src_bufs, n_sub, dstT):
        for s_ in range(n_sub):
            for g in range(2):
                bk = getbank()
                for i in range(4):
                    ch = 4 * g + i
                    k.tr(bk.t[:, i * 128:(i + 1) * 128], src_bufs[s_].t[:, ch * 128:(ch + 1) * 128], ident.t[:, :],
                         rd=(src_bufs[s_], ident), wr=(bk,), sig=(i == 3))
                evac(rr_eng([ACT, DVE]), dstT.t[:, 4 * g:4 * g + 4, s_ * 128:(s_ + 1) * 128],
                     bk.t[:, :].rearrange("p (a b) -> p a b", b=128), rd=(bk,), wr=(dstT,))

    def phase_mem_prompt():
        with ExitStack() as pes:
            subs2 = [(0, 128), (128, 128)]
            build_xT(mem, subs2, pes)
            ws = WStream([U_MK, U_MK + 1, U_MV, U_MV + 1], jit=True)
            mtm = [k.sb("mtm%d" % i, [128, 1024], F32, pes) for i in range(2)]
            for which in range(2):
                for half in range(2):
                    ws.prefetch()
                    slot = ws.get()
                    ws.prefetch()
                    for si_, sub in enumerate(subs2):
                        bk = getbank()
                        proj_tm(slot, sub, bk)
                        evac(rr_eng([ACT, DVE]), mtm[si_].t[:, half * 512:(half + 1) * 512], bk.t[:, :], rd=(bk,), wr=(mtm[si_],))
                for si_, (t0, n) in enumerate(subs2):
                    dst = mk_p if which == 0 else mv_p
                    k.dma(ACT, dst[t0:t0 + n, :], mtm[si_].t[:, :], rd=(mtm[si_],), sbuf=mtm[si_], is_out=True)
                if which == 0:
                    mem_transposes(mtm, 2, mkT)
                else:
                    for si_ in range(2):
                        evac(rr_eng([ACT, DVE]), mvb.t[:, si_, :], mtm[si_].t[:, :], rd=(mtm[si_],), wr=(mvb,))
            k.barrier()

    KTS_v = KTS
    VS_v = VS.rearrange("i p (s e) -> i p s e", e=130)
    subs1 = [(0, 64)]

    def phase_s_cache():
        with ExitStack() as pes:
            ckst = [[k.sb("ckst%d_%d" % (S, i), [128, 1024], F32, pes) for i in range(2)] for S in range(2)]
            cvst = [[k.sb("cvst%d_%d" % (S, i), [128, 1024], F32, pes) for i in range(2)] for S in range(2)]
            ktst = [k.sb("ktstS%d" % S, [128, 8, 256], BF16, pes) for S in range(2)]
            vst = [k.sb("vstS%d" % S, [128, 2, 8, 130], BF16, pes) for S in range(2)]
            for S in range(2):
                k.memset(POOL, vst[S].t[:, :, :, 128:129], 1.0, wr=(vst[S],))
                k.memset(POOL, vst[S].t[:, :, :, 129:130], 0.0, wr=(vst[S],))
            its = [(b, jj) for b in range(4) for jj in range(16)]

            def loads(it):
                b, jj = its[it]
                S = it % 2
                for s_ in range(2):
                    r0 = jj * 256 + s_ * 128
                    k.dma(SP, ckst[S][s_].t[:, :], ck[b, r0:r0 + 128, :], wr=(ckst[S][s_],), sbuf=ckst[S][s_])
                    k.dma(SP, cvst[S][s_].t[:, :], cv[b, r0:r0 + 128, :], wr=(cvst[S][s_],), sbuf=cvst[S][s_])

            loads(0)
            for it, (b, jj) in enumerate(its):
                if it + 1 < len(its):
                    loads(it + 1)
                S = it % 2
                for s_ in range(2):
                    for g in range(2):
                        bk = getbank()
                        for i in range(4):
                            h = 4 * g + i
                            k.tr(bk.t[:, i * 128:(i + 1) * 128], ckst[S][s_].t[:, h * 128:(h + 1) * 128], ident.t[:, :],
                                 rd=(ckst[S][s_], ident), wr=(bk,), sig=(i == 3))
                        evac(rr_eng([ACT, DVE]), ktst[S].t[:, 4 * g:4 * g + 4, s_ * 128:(s_ + 1) * 128],
                             bk.t[:, :].rearrange("p (a b) -> p a b", b=128), rd=(bk,), wr=(ktst[S],))
                    evac(rr_eng([POOL, DVE]), vst[S].t[:, s_, :, 0:128], cvst[S][s_].t[:, :].rearrange("p (h e) -> p h e", e=128),
                         rd=(cvst[S][s_],), wr=(vst[S],))
                group_store([(KTS_v[b * 8 + h, :, jj * 256:(jj + 1) * 256], ktst[S].t[:, h, :]) for h in range(8)],
                            ktst[S], [kts_b[b * 8 + h] for h in range(8)])
                group_store([(VS_v[b * 8 + h, :, 2 * jj:2 * jj + 2, :], vst[S].t[:, :, h, :]) for h in range(8)],
                            vst[S], [vs_b[b * 8 + h] for h in range(8)])
            k.barrier()

    cache = {"step": 0, "sets": None}
    NSTEP = 4 * 32

    def cache_alloc(es_):
        sets = []
        for S in range(2):
            d = dict(ck=k.sb("cck%d" % S, [128, 1024], F32, es_), cv=k.sb("ccv%d" % S, [128, 1024], F32, es_),
                     kt=k.sb("ckt%d" % S, [128, 8, 128], BF16, es_), v=k.sb("cvv%d" % S, [128, 8, 130], BF16, es_))
            k.memset(POOL, d["v"].t[:, :, 128:129], 1.0, wr=(d["v"],))
            k.memset(POOL, d["v"].t[:, :, 129:130], 0.0, wr=(d["v"],))
            sets.append(d)
        cache["sets"] = sets
        k.local_bufs = [b for b in k.local_bufs if all(b is not x for d in sets for x in d.values())]

    def cache_loads(s_):
        if s_ >= NSTEP:
            return
        b, r = divmod(s_, 32)
        d = cache["sets"][s_ % 2]
        k.dma(SP, d["ck"].t[:, :], ck[b, r * 128:(r + 1) * 128, :], wr=(d["ck"],), sbuf=d["ck"])
        k.dma(SP, d["cv"].t[:, :], cv[b, r * 128:(r + 1) * 128, :], wr=(d["cv"],), sbuf=d["cv"])

    def cache_step():
        s_ = cache["step"]
        if s_ >= NSTEP or cache["sets"] is None:
            return
        if s_ == 0:
            cache_loads(0)
        cache_loads(s_ + 1)
        b, r = divmod(s_, 32)
        d = cache["sets"][s_ % 2]
        for g in range(2):
            bk = getbank()
            for i in range(4):
                h = 4 * g + i
                k.tr(bk.t[:, i * 128:(i + 1) * 128], d["ck"].t[:, h * 128:(h + 1) * 128], ident.t[:, :],
                     rd=(d["ck"], ident), wr=(bk,), sig=(i == 3))
            evac(DVE, d["kt"].t[:, 4 * g:4 * g + 4, :], bk.t[:, :].rearrange("p (a b) -> p a b", b=128), rd=(bk,), wr=(d["kt"],))
        evac(POOL, d["v"].t[:, :, 0:128], d["cv"].t[:, :].rearrange("p (h e) -> p h e", e=128), rd=(d["cv"],), wr=(d["v"],))
        hb = [kts_b[b * 8 + h] for h in range(8)]
        ev = k.dma(SP, KTS_v[b * 8:(b + 1) * 8, :, r * 128:(r + 1) * 128].rearrange("h p t -> p h t"), d["kt"].t[:, :, :],
                   rd=(d["kt"],), wr=(), sbuf=d["kt"])
        for x in hb:
            x.w = ev
            x.r = {}
        vb_ = [vs_b[b * 8 + h] for h in range(8)]
        ev = k.dma(SP, VS_v[b * 8:(b + 1) * 8, :, r, :].rearrange("h p e -> p h e"), d["v"].t[:, :, :],
                   rd=(d["v"],), wr=(), sbuf=d["v"])
        for x in vb_:
            x.w = ev
            x.r = {}
        cache["step"] = s_ + 1

    def attn_sample(pes):
        KTh = [k.sb("KThS%d" % i, [128, SEQ + 16], BF16, pes) for i in range(2)]
        Vh = [k.sb("VhS%d" % i, [128, 33, 130], BF16, pes) for i in range(2)]
        PT = [k.sb("PTS%d" % i, [128, 512], BF16, pes) for i in range(2)]
        PTt = [k.sb("PTtS%d" % i, [16, 16], BF16, pes) for i in range(2)]
        o1b = [k.sb("o1bS%d" % i, [128, 128], F32, pes) for i in range(2)]
        o2b = [k.sb("o2bS%d" % i, [128, 128], F32, pes) for i in range(2)]
        onb = [k.sb("onbS%d" % i, [128, 128], F32, pes) for i in range(2)]
        stb = [k.sb("stbS%d" % i, [128, 4], F32, pes) for i in range(2)]

        def load(i):
            k.dma(SP, KTh[i % 2].t[:, :], KTS_v[i, :, :], rd=(kts_b[i],), wr=(KTh[i % 2],), sbuf=KTh[i % 2])
            k.dma(SP, Vh[i % 2].t[:, :, :], VS_v[i, :, :, :], rd=(vs_b[i],), wr=(Vh[i % 2],), sbuf=Vh[i % 2])

        load(0)
        pi = 0
        for i in range(32):
            b, h = i // 8, i % 8
            if i + 1 < 32:
                load(i + 1)
            KT, V = KTh[i % 2], Vh[i % 2]
            qc = slice(b * 16, (b + 1) * 16)
            for m in range(2):
                ms = slice(m * 64, (m + 1) * 64)
                pt, ptt = PT[pi % 2], PTt[pi % 2]
                pi += 1
                sbk = getbank()
                for kt in range(32):
                    k.mm(sbk.t[:, kt * 16:(kt + 1) * 16], KT.t[:, kt * 128:(kt + 1) * 128], qTz[m].t[:, h, qc], True, True,
                         rd=(KT, qTz[m]), wr=(sbk,), sig=(kt == 31))
                k.actf(pt.t[:, :], sbk.t[:, :], AF.Exp, rd=(sbk,), wr=(pt,), scale=0.125)
                sbt = getbank()
                k.mm(sbt.t[0:16, 0:16], KT.t[:, SEQ:SEQ + 16], qTz[m].t[:, h, qc], True, True, rd=(KT, qTz[m]), wr=(sbt,), sig=True)
                k.actf(ptt.t[:, :], sbt.t[0:16, 0:16], AF.Exp, rd=(sbt,), wr=(ptt,), scale=0.125)
                O = getbank()
                for kt in range(32):
                    k.mm(O.t[0:16, 0:130], pt.t[:, kt * 16:(kt + 1) * 16], V.t[:, kt, :], kt == 0, False, rd=(pt, V), wr=(O,), sig=False)
                k.mm(O.t[0:16, 0:130], ptt.t[:, :], V.t[0:16, 32, :], False, True, rd=(ptt, V), wr=(O,), sig=True)
                attn_post(O, 16, m, h, b * 16, o1b[i % 2], o2b[i % 2], onb[i % 2], stb[i % 2])

    def phase_b_sample(ws):
        for g in range(2):
            with ExitStack() as pes:
                qh = k.sb("qhS", [128, 4, 64], F32, pes)
                ff = k.sb("ffS", [128, 4, 64], F32, pes)
                vB = k.sb("vBS", [128, 1, 512], BF16, pes)
                vBm = k.sb("vBmS", [64, 4, 512], BF16, pes)
                gz = None
                og = k.sb("ogS", [128, 4, 64], BF16, pes)
                qdT = k.sb("qdTS", [128, 4, 64], BF16, pes)
                kdT = k.sb("kdTS", [128, 4, 64], BF16, pes)
                qsZ = k.sb("qsZS", [128, 4, 4, 64], BF16, pes)
                ks_tm = k.sb("ks_tmS", [64, 4, 128], BF16, pes)
                decay = k.sb("decayS", [128, 4, 4], F32, pes)
                tmp = [k.sb("htmpS%d" % i, [128, 64], F32, pes) for i in range(6)]
                scT = [k.sb("scTS%d" % i, [64, 64], BF16, pes) for i in range(2)]
                onb = [k.sb("onbBS%d" % i, [128, 128], F32, pes) for i in range(2)]
                stb = k.sb("stbBS", [128, 8], F32, pes)
                S0f = [k.sb("S0f%d" % i, [128, 128], F32, pes) for i in range(8)]
                S0b = [k.sb("S0b%d" % i, [128, 128], BF16, pes) for i in range(8)]
                k.memset(POOL, qsZ.t[:], 0.0, wr=(qsZ,))
                ksf = [k.sb("ksfS%d" % i, [128, 64], F32, pes) for i in range(4)]
                for b in []:
                    k.ts(DVE, vBm.t[0:64, b, :], vB.t[0:64, 0, :], rowmask.t[0:64, b:b + 1], None, ALU.mult, None,
                         rd=(vB, rowmask), wr=(vBm,))

                def qs_writer(hh, q_ap, e3_ap, qbuf, ebuf):
                    for b in range(4):
                        c_ = slice(b * 16, (b + 1) * 16)
                        k.tt(DVE, qsZ.t[:, hh, b, c_], q_ap[:, c_], e3_ap[:, c_], ALU.mult, rd=(qbuf, ebuf), wr=(qsZ,))

                def ks_writer(hh, Tg):
                    bk = getbank()
                    k.tr(bk.t[0:64, 0:128], Tg.t[:, 0:64], ident.t[:, :], rd=(Tg, ident), wr=(bk,))
                    evac(rr_eng([ACT, DVE]), ks_tm.t[0:64, hh, :], bk.t[0:64, 0:128], rd=(bk,), wr=(ks_tm,))

                preps = [lambda hh=hh: hgrn_prep(4 * g + hh, hh, 64, 16, qh, ff, tmp, rmask16, qdT, kdT, qs_writer, ksf[hh], decay)
                         for hh in range(4)]
                hgrn_proj(ws, g, 64, subs1, qh, ff, vB, gz, og, tmp, preps)
                for b in range(4):
                    k.ts(DVE, vBm.t[0:64, b, :], vB.t[0:64, 0, :], rowmask.t[0:64, b:b + 1], None, ALU.mult, None,
                         rd=(vB, rowmask), wr=(vBm,))
                for hh in range(4):
                    ks_writer(hh, ksf[hh])
                for hh in range(4):
                    h = 4 * g + hh
                    hc = slice(hh * 128, (hh + 1) * 128)
                    for b in range(4):
                        sf, sbb = S0f[(hh % 2) * 4 + b], S0b[(hh % 2) * 4 + b]
                        k.dma(SP, sf.t[:, :], s0[b * 8 + h], wr=(sf,), sbuf=sf)
                        evac(rr_eng([ACT, POOL]), sbb.t[:, :], sf.t[:, :], rd=(sf,), wr=(sbb,))
                    scb = getbank()
                    k.mm(scb.t[0:64, 0:64], kdT.t[:, hh, 0:64], qdT.t[:, hh, 0:64], True, True, rd=(kdT, qdT), wr=(scb,), sig=True)
                    sc_ = scT[hh % 2]
                    k.tt(DVE, sc_.t[:, :], scb.t[0:64, 0:64], tri16.t[:, :], ALU.mult, rd=(scb, tri16), wr=(sc_,))
                    obk = getbank()
                    k.mm(obk.t[0:64, 0:128], sc_.t[:, :], vB.t[0:64, 0, hc], True, False, rd=(sc_, vB), wr=(obk,), sig=False)
                    for b in range(4):
                        sbb = S0b[(hh % 2) * 4 + b]
                        k.mm(obk.t[0:64, 0:128], qsZ.t[:, hh, b, :], sbb.t[:, :], False, b == 3, rd=(qsZ, sbb), wr=(obk,), sig=(b == 3))
                    dsb = getbank()
                    for b in range(4):
                        k.mm(dsb.t[:, b * 128:(b + 1) * 128], ks_tm.t[0:64, hh, :], vBm.t[0:64, b, hc], True, True,
                             rd=(ks_tm, vBm), wr=(dsb,), sig=(b == 3))
                    for b in range(4):
                        sf = S0f[(hh % 2) * 4 + b]
                        k.stt(sf.t[:, :], sf.t[:, :], decay.t[:, hh, b:b + 1], dsb.t[:, b * 128:(b + 1) * 128], ALU.mult, ALU.add,
                              rd=(sf, decay, dsb), wr=(sf,))
                        k.dma(ACT, hg_s[b * 8 + h], sf.t[:, :], rd=(sf,), sbuf=sf, is_out=True)
                    ob_post(obk, 0, 64, h, og.t[:, hh, 0:64], og, 0, onb[hh % 2], stb, hh)
                k.barrier()

    def run_sample():
        ws = WStream(TILE_UNITS)
        outer = ExitStack()
        alloc_q_za(outer)
        with ExitStack() as pes:
            build_xT(xs, subs1, pes)

            def store_kt(ktst):
                group_store([(KTS_v[b * 8 + h, :, SEQ:SEQ + 16], ktst.t[:, h, b * 16:(b + 1) * 16])
                             for b in range(4) for h in range(8)], ktst, kts_b)

            def store_v(vst):
                group_store([(VS_v[b * 8 + h, 0:16, 32, :], vst.t[b * 16:(b + 1) * 16, 0, h, :])
                             for b in range(4) for h in range(8)], vst, vs_b)

            with nc.allow_non_contiguous_dma(reason="small per-sequence K^T column writes"):
                phase_a(ws, subs1, 64, [32], k_s, v_s, store_kt, store_v, pes)
            phase_z(ws, 64)
            k.barrier()
        with ExitStack() as pes:
            attn_sample(pes)
            k.barrier()
        outer.close()
        phase_b_sample(ws)
        with ExitStack() as pes:
            mkTs = [k.sb("mkTs%d" % b, [128, 8, 256], BF16, pes) for b in range(4)]
            mvbs = [k.sb("mvbs%d" % b, [128, 2, 1024], BF16, pes) for b in range(4)]
            mst = [k.sb("mst%d" % i, [128, 1024], F32, pes) for i in range(4)]
            for b in range(4):
                for s_ in range(2):
                    k.dma(SP, mst[s_].t[:, :], cmk[b, s_ * 128:(s_ + 1) * 128, :], wr=(mst[s_],), sbuf=mst[s_])
                    k.dma(SP, mst[2 + s_].t[:, :], cmv[b, s_ * 128:(s_ + 1) * 128, :], wr=(mst[2 + s_],), sbuf=mst[2 + s_])
                mem_transposes(mst[0:2], 2, mkTs[b])
                for s_ in range(2):
                    evac(rr_eng([ACT, DVE]), mvbs[b].t[:, s_, :], mst[2 + s_].t[:, :], rd=(mst[2 + s_],), wr=(mvbs[b],))
            phase_c(ws, 64, [(b * 16, 16, mkTs[b], mvbs[b]) for b in range(4)], pes)
            k.barrier()
        with ExitStack() as pes:
            phase_merge_out(ws, 64, subs1, xs, y_s, pes)
            k.barrier()

    def alloc_q_za(pes):
        for m in range(2):
            qTz[m] = k.sb("qTz%d" % m, [128, 8, T], BF16, pes)
        zaTh[0] = k.sb("zaT", [128, 8, T], BF16, pes)
        k.memset(POOL, qTz[0].t[64:128, :, :], 0.0, wr=(qTz[0],))
        k.memset(POOL, qTz[1].t[0:64, :, :], 0.0, wr=(qTz[1],))

    KTP_v = KTP
    VP_v = VP.rearrange("h p (s e) -> h p s e", e=130)
    subs4 = [(i * 128, 128) for i in range(4)]
    TILE_UNITS = list(range(0, 42))

    for h in range(8):
        k.memset(POOL, Sst[h].t[:], 0.0, wr=(Sst[h],))
        for p_ in range(2):
            k.memset(POOL, Sbf[h][p_].t[:], 0.0, wr=(Sbf[h][p_],))

    es_stg = ExitStack()
    for i in range(4):
        stg.append(k.sb("stg%d" % i, [128, 1024], F32, es_stg))
    k.local_bufs = [b for b in k.local_bufs if all(b is not x for x in stg)]
    es_cache = ExitStack()
    phase_mem_prompt()
    if not JIT_TILE0:
        for u in range(0, 42):
            wconvert(u)
        k.barrier()

    NT_RUN = NT
    for j in range(NT_RUN):
        if j == 1:
            es_stg.close()
            with nc.allow_non_contiguous_dma(reason="per-head scatter of converted cache tiles"):
                cache_alloc(es_cache)
        ws = WStream(TILE_UNITS, jit=(j == 0 and JIT_TILE0))
        if j >= 1:
            ws.bg = cache_step
        outer = ExitStack()
        alloc_q_za(outer)
        with ExitStack() as pes:
            if j == 0:
                build_xT(xp[j * T:(j + 1) * T, :], subs4, pes)

            def store_kt(ktst, j=j):
                group_store([(KTP_v[h, :, j * T:(j + 1) * T], ktst.t[:, h, 0:T]) for h in range(8)], ktst, ktp_b)

            def store_v(vst, j=j):
                group_store([(VP_v[h, :, 4 * j:4 * j + 4, :], vst.t[:, :, h, :]) for h in range(8)], vst, vp_b)

            phase_a(ws, subs4, T, [4 * j + i for i in range(4)],
                    k_p[j * T:(j + 1) * T, :], v_p[j * T:(j + 1) * T, :], store_kt, store_v, pes)
            phase_z(ws, T)
            k.barrier()
        with ExitStack() as pes:
            attn_prompt(j, pes)
            k.barrier()
        outer.close()
        phase_b_prompt(ws, j)
        with ExitStack() as pes:
            phase_c(ws, T, [(0, T, mkT, mvb)], pes)
            k.barrier()
        with ExitStack() as pes:
            phase_merge_out(ws, T, subs4, xp[j * T:(j + 1) * T, :], y_p[j * T:(j + 1) * T, :], pes,
                            next_x=(xp[(j + 1) * T:(j + 2) * T, :] if j + 1 < NT_RUN else None))
            k.barrier()
    for h in range(8):
        k.dma(ACT, hg_p[h], Sst[h].t[:], rd=(Sst[h],), sbuf=Sst[h], is_out=True)
    while cache["step"] < NSTEP:
        cache_step()
    k.barrier()
    es_cache.close()
    if RUN_SAMPLE:
        run_sample()

    k.finish()
    es.close()
    return nc


_CACHE = {}


def _consts():
    p = np.arange(128)
    inv = np.power(500000.0, -np.arange(0, 16, 2, dtype=np.float32) / 16.0).astype(np.float32)
    rope = np.zeros((128, 33, 16), np.float32)
    for S in range(33):
        pos = (128 * S + p) if S < 32 else (4096 + (p % 16))
        ang = pos.astype(np.float32)[:, None] * inv[None, :]
        rope[:, S, 0:8] = np.cos(ang)
        rope[:, S, 8:16] = np.sin(ang)
    t = np.arange(64)
    tri = ((p[:, None] % 64) <= t[None, :]).astype(np.float32)
    rmask = (np.arange(512) % 64 != 0).astype(np.float32)[None, :].repeat(128, 0)
    rmask16 = (np.arange(64) % 16 != 0).astype(np.float32)[None, :].repeat(128, 0)
    s = np.arange(64)
    tri16 = ((s[:, None] // 16 == s[None, :] // 16) & (s[:, None] <= s[None, :])).astype(np.float32)
    rowmask = (s[:, None] // 16 == np.arange(4)[None, :]).astype(np.float32)
    return {
        "c_ident": np.eye(128, dtype=np.float32),
        "c_rope": np.ascontiguousarray(rope.reshape(128, 33 * 16)),
        "c_tri": np.ascontiguousarray(tri),
        "c_rmask": np.ascontiguousarray(rmask),
        "c_rmask16": np.ascontiguousarray(rmask16),
        "c_tri16": np.ascontiguousarray(tri16),
        "c_rowmask": np.ascontiguousarray(rowmask),
    }


def kernel(x_prompt, x_sample, cache_attn_k, cache_attn_v, state_hgrn, cache_mem_k, cache_mem_v,
           mem_prompt, w_in, lambda_q1, lambda_k1, lambda_q2, lambda_k2, attn_sub_norm,
           hgrn_lb_logits, hgrn_norm, w_mem_k, w_mem_v, w_branch_a, w_branch_b, w_branch_c,
           w_out, ln_gamma, ln_beta):
    f = lambda a: np.ascontiguousarray(np.asarray(a, dtype=np.float32))
    if "nc" not in _CACHE:
        _CACHE["nc"] = build_program()
    nc = _CACHE["nc"]
    consts = _consts()
    shared = {
        "w_in": f(w_in)[0], "w_mk": f(w_mem_k)[0], "w_mv": f(w_mem_v)[0],
        "w_a": f(w_branch_a)[0], "w_b": f(w_branch_b)[0], "w_c": f(w_branch_c)[0], "w_o": f(w_out)[0],
        "lam4": np.ascontiguousarray(np.concatenate([f(lambda_q1), f(lambda_k1), f(lambda_q2), f(lambda_k2)], 0)),
        "subn": f(attn_sub_norm), "lbl": f(hgrn_lb_logits), "hgn": f(hgrn_norm),
        "lng": f(ln_gamma), "lnb": f(ln_beta),
    }
    shared.update(consts)
    xp_, xs_ = f(x_prompt), f(x_sample)
    ck_, cv_ = f(cache_attn_k)[0], f(cache_attn_v)[0]
    s0_, cmk_, cmv_, mem_ = f(state_hgrn)[0], f(cache_mem_k)[0], f(cache_mem_v)[0], f(mem_prompt)
    in_maps = []
    for c in range(8):
        m = dict(shared)
        m["xp"] = xp_[c]
        m["xs"] = xs_[4 * c:4 * c + 4].reshape(64, D)
        m["ck"] = ck_[4 * c:4 * c + 4].reshape(4, SEQ, 1024)
        m["cv"] = cv_[4 * c:4 * c + 4].reshape(4, SEQ, 1024)
        m["s0"] = s0_[4 * c:4 * c + 4].reshape(32, 128, 128)
        m["cmk"] = cmk_[4 * c:4 * c + 4].reshape(4, 256, 1024)
        m["cmv"] = cmv_[4 * c:4 * c + 4].reshape(4, 256, 1024)
        m["mem"] = mem_[c]
        in_maps.append(m)
    res = run_bass_kernel_spmd(nc, in_maps, core_ids=list(range(8)))
    R = res.results
    cat = lambda name: np.stack([np.asarray(R[c][name]) for c in range(8)], 0)
    y_prompt = cat("y_p")
    y_sample = cat("y_s").reshape(32, 16, D)
    k_prompt = cat("k_p").reshape(1, 8, SEQ, 8, 2, 64)
    v_prompt = cat("v_p").reshape(1, 8, SEQ, 8, 128)
    hgrn_prompt = cat("hg_p").reshape(1, 8, 8, 128, 128)
    mem_k_prompt = cat("mk_p").reshape(1, 8, 256, 4, 256)
    mem_v_prompt = cat("mv_p").reshape(1, 8, 256, 4, 256)
    k_sample = cat("k_s").reshape(1, 32, 16, 8, 2, 64)
    v_sample = cat("v_s").reshape(1, 32, 16, 8, 128)
    hgrn_sample = cat("hg_s").reshape(1, 32, 8, 128, 128)
    return (y_prompt, y_sample, k_prompt, v_prompt, hgrn_prompt, mem_k_prompt, mem_v_prompt,
            k_sample, v_sample, hgrn_sample)
```

```python
import numpy as np
from contextlib import ExitStack
import concourse.bass as bass
import concourse.mybir as mybir
from concourse.bass_utils import run_bass_kernel_spmd

F32 = mybir.dt.float32
BF16 = mybir.dt.bfloat16
AF = mybir.ActivationFunctionType
ALU = mybir.AluOpType

D = 2048
SEQ = 4096
NIN = 17408
T = 512
NT = SEQ // T
ALPHA = 2.0 ** 0.25
EPS = 1e-5
LAM_INIT = 0.8 - 0.6 * 1.0
NU_BIG = 22
U_MERGE = 22
U_OUT = 38
U_MK = 42
U_MV = 44
NUNITS = 46
RUN_SAMPLE = True
JIT_TILE0 = True
USZ = 9216
BIG_ORDER = [0, 1, 2, 3, 4, 5, 6, 7, 8, 10, 12, 14, 16, 9, 11, 13, 15, 17, 18, 19, 20, 21]


class Sem:
    def __init__(self, h):
        self.h = h
        self.n = 0


class Buf:
    def __init__(self, name, t=None):
        self.name = name
        self.t = t
        self.w = None
        self.r = {}
        self.sem = None


class Eng:
    def __init__(self, name, eng, sem):
        self.name = name
        self.eng = eng
        self.sem = sem
        self.seen = {}


class K:
    def __init__(self, nc, es):
        self.nc = nc
        self.es = es
        self.nsem = 0
        self.pe = Eng("pe", nc.tensor, self.new_sem("pe"))
        self.act = Eng("act", nc.scalar, self.new_sem("act"))
        self.dve = Eng("dve", nc.vector, self.new_sem("dve"))
        self.pool = Eng("pool", nc.gpsimd, self.new_sem("pool"))
        self.sp = Eng("sp", nc.sync, self.new_sem("sp"))
        self.engines = [self.pe, self.act, self.dve, self.pool, self.sp]
        self.dma_sems = []
        self.out_evs = {}
        self.rr = 0
        self.cast_rr = 0
        self.local_bufs = []
        self.sem_pool = []

    def new_sem(self, name):
        self.nsem += 1
        return Sem(self.es.enter_context(self.nc.semaphore("s%d_%s" % (self.nsem, name))))

    def sb(self, name, shape, dt, es=None):
        self.nbuf = getattr(self, "nbuf", 0) + 1
        name = "%s_%d" % (name, self.nbuf)
        t = (es or self.es).enter_context(self.nc.sbuf_tensor(name, list(shape), dt))
        b = Buf(name, t)
        if es is not None:
            self.local_bufs.append(b)
        return b

    def wait(self, E, ev):
        sem, val = ev
        if E is self.pe and sem is E.sem:
            return
        if E.seen.get(sem, 0) >= val:
            return
        E.eng.wait_ge(sem.h, val)
        E.seen[sem] = val

    def _deps(self, E, rd, wr):
        for b in rd:
            if b.w is not None:
                for ev in (b.w if isinstance(b.w, list) else [b.w]):
                    self.wait(E, ev)
        for b in wr:
            if b.w is not None:
                for ev in (b.w if isinstance(b.w, list) else [b.w]):
                    self.wait(E, ev)
            for sem, val in b.r.items():
                self.wait(E, (sem, val))

    def _record(self, ev, rd, wr):
        sem, val = ev
        for b in rd:
            if b.r.get(sem, 0) < val:
                b.r[sem] = val
        for b in wr:
            b.w = ev
            b.r = {}

    def op(self, E, fn, rd=(), wr=(), sig=True):
        self._deps(E, rd, wr)
        inst = fn()
        if sig:
            E.sem.n += 1
            inst.then_inc(E.sem.h, 1)
            ev = (E.sem, E.sem.n)
        else:
            ev = (E.sem, E.sem.n + 1)
        self._record(ev, rd, wr)
        return ev

    def dma(self, Q, out, in_, rd=(), wr=(), sbuf=None, is_out=False):
        self._deps(Q, rd, wr)
        inst = Q.eng.dma_start(out=out, in_=in_)
        if sbuf.sem is None:
            if self.sem_pool:
                sbuf.sem = self.sem_pool.pop()
            else:
                sbuf.sem = self.new_sem("d%d" % len(self.dma_sems))
                self.dma_sems.append(sbuf.sem)
        sbuf.sem.n += 16
        inst.then_inc(sbuf.sem.h, 16)
        ev = (sbuf.sem, sbuf.sem.n)
        self._record(ev, rd, wr)
        if is_out:
            self.out_evs[sbuf.sem] = sbuf.sem.n
        return ev

    def barrier(self):
        self._barrier()
        for b in self.local_bufs:
            if b.sem is not None:
                self.sem_pool.append(b.sem)
                b.sem = None
        self.local_bufs = []

    def _barrier(self):
        evs = [(e.sem, e.sem.n) for e in self.engines if e.sem.n > 0]
        evs += [(s, s.n) for s in self.dma_sems if s.n > 0]
        for E in self.engines:
            for ev in evs:
                if ev[0] is E.sem:
                    continue
                self.wait(E, ev)

    def finish(self):
        for sem, val in self.out_evs.items():
            self.wait(self.sp, (sem, val))
        for e in self.engines:
            if e is not self.sp and e.sem.n > 0:
                self.wait(self.sp, (e.sem, e.sem.n))

    def mm(self, out_ap, lhsT, rhs, start, stop, rd, wr, sig):
        nc = self.nc
        return self.op(self.pe, lambda: nc.tensor.matmul(out_ap, lhsT=lhsT, rhs=rhs, start=start, stop=stop),
                       rd=rd, wr=wr, sig=sig)

    def tr(self, out_ap, in_ap, ident_ap, rd, wr, sig=True):
        nc = self.nc
        return self.op(self.pe, lambda: nc.tensor.transpose(out_ap, in_ap, ident_ap), rd=rd, wr=wr, sig=sig)

    def actf(self, out_ap, in_ap, func, rd, wr, scale=None, bias=None, accum=None):
        nc = self.nc
        kw = {}
        if scale is not None:
            kw["scale"] = scale
        if bias is not None:
            kw["bias"] = bias
        if accum is not None:
            kw["accum_out"] = accum
        return self.op(self.act, lambda: nc.scalar.activation(out=out_ap, in_=in_ap, func=func, **kw), rd=rd, wr=wr)

    def tt(self, E, out_ap, in0, in1, op, rd, wr):
        return self.op(E, lambda: E.eng.tensor_tensor(out=out_ap, in0=in0, in1=in1, op=op), rd=rd, wr=wr)

    def ts(self, E, out_ap, in0, s1, s2, op0, op1, rd, wr):
        if op1 is None:
            return self.op(E, lambda: E.eng.tensor_scalar(out=out_ap, in0=in0, scalar1=s1, scalar2=None, op0=op0),
                           rd=rd, wr=wr)
        return self.op(E, lambda: E.eng.tensor_scalar(out=out_ap, in0=in0, scalar1=s1, scalar2=s2, op0=op0, op1=op1),
                       rd=rd, wr=wr)

    def stt(self, out_ap, in0, scalar, in1, op0, op1, rd, wr):
        nc = self.nc
        return self.op(self.dve, lambda: nc.vector.scalar_tensor_tensor(out=out_ap, in0=in0, scalar=scalar, in1=in1,
                                                                       op0=op0, op1=op1), rd=rd, wr=wr)

    def copy(self, E, out_ap, in_ap, rd, wr):
        if E is self.act:
            return self.op(E, lambda: self.nc.scalar.copy(out=out_ap, in_=in_ap), rd=rd, wr=wr)
        return self.op(E, lambda: E.eng.tensor_copy(out=out_ap, in_=in_ap), rd=rd, wr=wr)

    def memset(self, E, ap, val, wr):
        return self.op(E, lambda: E.eng.memset(ap, val), rd=(), wr=wr)


def build_program():
    nc = bass.Bass("TRN2", target_bir_lowering=False)
    es = ExitStack()
    k = K(nc, es)

    def din(name, shape):
        return nc.dram_tensor(name, list(shape), F32, kind="ExternalInput").ap()

    def dout(name, shape):
        return nc.dram_tensor(name, list(shape), F32, kind="ExternalOutput").ap()

    xp = din("xp", [SEQ, D])
    xs = din("xs", [64, D])
    ck = din("ck", [4, SEQ, 1024])
    cv = din("cv", [4, SEQ, 1024])
    s0 = din("s0", [32, 128, 128])
    cmk = din("cmk", [4, 256, 1024])
    cmv = din("cmv", [4, 256, 1024])
    mem = din("mem", [256, D])
    w_in = din("w_in", [D, NIN])
    w_mk = din("w_mk", [D, 1024])
    w_mv = din("w_mv", [D, 1024])
    w_a = din("w_a", [1024, D])
    w_b = din("w_b", [1024, D])
    w_c = din("w_c", [1024, D])
    w_o = din("w_o", [D, D])
    lam4 = din("lam4", [4, 64])
    subn = din("subn", [1, 128])
    lbl = din("lbl", [2, 1024])
    hgn = din("hgn", [1, 1024])
    lng = din("lng", [1, D])
    lnb = din("lnb", [1, D])
    c_ident = din("c_ident", [128, 128])
    c_rope = din("c_rope", [128, 33 * 16])
    c_tri = din("c_tri", [128, 64])
    c_rmask = din("c_rmask", [128, 512])
    c_rmask16 = din("c_rmask16", [128, 64])
    c_tri16 = din("c_tri16", [64, 64])
    c_rowmask = din("c_rowmask", [64, 4])

    y_p = dout("y_p", [SEQ, D])
    y_s = dout("y_s", [64, D])
    k_p = dout("k_p", [SEQ, 1024])
    v_p = dout("v_p", [SEQ, 1024])
    hg_p = dout("hg_p", [8, 128, 128])
    mk_p = dout("mk_p", [256, 1024])
    mv_p = dout("mv_p", [256, 1024])
    k_s = dout("k_s", [64, 1024])
    v_s = dout("v_s", [64, 1024])
    hg_s = dout("hg_s", [32, 128, 128])

    WS = nc.dram_tensor("ws_scratch", [NUNITS, 128, USZ], BF16, kind="Internal").ap()
    KTP = nc.dram_tensor("ktp_scratch", [8, 128, SEQ], BF16, kind="Internal").ap()
    VP = nc.dram_tensor("vp_scratch", [8, 128, 32 * 130], BF16, kind="Internal").ap()
    KTS = nc.dram_tensor("kts_scratch", [32, 128, SEQ + 16], BF16, kind="Internal").ap()
    VS = nc.dram_tensor("vs_scratch", [32, 128, 33 * 130], BF16, kind="Internal").ap()
    ws_b = [Buf("ws%d" % u) for u in range(NUNITS)]
    ktp_b = [Buf("ktp%d" % h) for h in range(8)]
    vp_b = [Buf("vp%d" % h) for h in range(8)]
    kts_b = [Buf("kts%d" % i) for i in range(32)]
    vs_b = [Buf("vs%d" % i) for i in range(32)]

    ident = k.sb("ident", [128, 128], F32)
    rope = k.sb("rope", [128, 33, 16], F32)
    tri = k.sb("tri", [128, 64], F32)
    rmask = k.sb("rmask", [128, 512], F32)
    rmask16 = k.sb("rmask16", [128, 64], F32)
    tri16 = k.sb("tri16", [64, 64], F32)
    rowmask = k.sb("rowmask", [64, 4], F32)
    gain = k.sb("gain", [128, 1024], F32)
    sn = k.sb("sn", [128, 128], F32)
    ones_bf = k.sb("ones_bf", [128, 128], BF16)
    lamw = k.sb("lamw", [128, 4, 64], F32)
    lamv = k.sb("lamv", [128, 8], F32)
    lbt = k.sb("lbt", [128, 2, 8], F32)
    lbv = k.sb("lbv", [128, 2, 8], F32)
    m05 = k.sb("m05", [128, 1], F32)
    W = [k.sb("wslot%d" % i, [128, USZ], BF16) for i in range(2)]
    xT = k.sb("xT", [128, 16, T], BF16)
    yaT = k.sb("yaT", [128, 8, T], BF16)
    ybT = k.sb("ybT", [128, 8, T], BF16)
    ycT = k.sb("ycT", [128, 8, T], BF16)
    qTz = [None, None]
    zaTh = [None]
    Sst = [k.sb("S%d" % h, [128, 128], F32) for h in range(8)]
    Sbf = [[k.sb("Sbf%d_%d" % (h, p), [128, 128], BF16) for p in range(2)] for h in range(8)]
    mkT = k.sb("mkT", [128, 8, 256], BF16)
    mvb = k.sb("mvb", [128, 2, 1024], BF16)

    PS = es.enter_context(nc.psum_tensor("psum_all", [128, 4096], F32))
    PS3 = PS[:, :].rearrange("p (b n) -> p b n", n=512)
    banks = [Buf("bank%d" % i, PS[:, i * 512:(i + 1) * 512]) for i in range(8)]
    st = {"brot": list(range(8)), "bi": 0, "wi": 0, "si": 0}

    def getbank():
        b = banks[st["brot"][st["bi"] % len(st["brot"])]]
        st["bi"] += 1
        return b

    def rr_eng(engs):
        k.rr += 1
        return engs[k.rr % len(engs)]

    SP, PE, ACT, DVE, POOL = k.sp, k.pe, k.act, k.dve, k.pool

    def load_const(buf, src, shape_ap=None):
        k.dma(SP, buf.t[:] if shape_ap is None else shape_ap, src, rd=(), wr=(buf,), sbuf=buf)

    load_const(ident, c_ident)
    k.dma(SP, rope.t[:].rearrange("p a b -> p (a b)"), c_rope, wr=(rope,), sbuf=rope)
    load_const(tri, c_tri)
    load_const(rmask, c_rmask)
    load_const(rmask16, c_rmask16)
    load_const(tri16, c_tri16)
    load_const(rowmask, c_rowmask)
    k.dma(SP, gain.t[:], hgn[0:1, :].to_broadcast([128, 1024]), wr=(gain,), sbuf=gain)
    k.dma(SP, sn.t[:], subn[0:1, :].to_broadcast([128, 128]), wr=(sn,), sbuf=sn)
    for i in range(4):
        k.dma(SP, lamw.t[:, i, :], lam4[i:i + 1, :].to_broadcast([128, 64]), wr=(lamw,), sbuf=lamw)
    with nc.allow_non_contiguous_dma(reason="tiny lb logits transpose load"):
        for l in range(2):
            k.dma(SP, lbt.t[:, l, :], lbl[l:l + 1, :].rearrange("o (h d) -> d (o h)", d=128), wr=(lbt,), sbuf=lbt)
    k.memset(DVE, ones_bf.t[:], 1.0, wr=(ones_bf,))
    k.memset(DVE, m05.t[:], -0.5, wr=(m05,))
    k.ts(DVE, sn.t[:], sn.t[:], 1.0 - LAM_INIT, None, ALU.mult, None, rd=(sn,), wr=(sn,))
    with ExitStack() as les:
        lt = k.sb("lam_tmp", [128, 64], F32, les)
        for j in range(2):
            k.tt(DVE, lt.t[:], lamw.t[:, 2 * j, :], lamw.t[:, 2 * j + 1, :], ALU.mult, rd=(lamw,), wr=(lt,))
            k.op(DVE, lambda j=j: nc.vector.reduce_sum(out=lamv.t[:, j:j + 1], in_=lt.t[:],
                                                       axis=mybir.AxisListType.X), rd=(lt,), wr=(lamv,))
        k.actf(lamv.t[:, 2:4], lamv.t[:, 0:2], AF.Exp, rd=(lamv,), wr=(lamv,))
        k.tt(DVE, lamv.t[:, 4:5], lamv.t[:, 3:4], lamv.t[:, 2:3], ALU.subtract, rd=(lamv,), wr=(lamv,))
        k.ts(DVE, lamv.t[:, 4:5], lamv.t[:, 4:5], -LAM_INIT, None, ALU.add, None, rd=(lamv,), wr=(lamv,))
        k.tt(DVE, lbv.t[:, 0, :], lbt.t[:, 0, :], lbt.t[:, 1, :], ALU.subtract, rd=(lbt,), wr=(lbv,))
        k.actf(lbv.t[:, 0, :], lbv.t[:, 0, :], AF.Sigmoid, rd=(lbv,), wr=(lbv,))
        k.ts(DVE, lbv.t[:, 1, :], lbv.t[:, 0, :], -1.0, 1.0, ALU.mult, ALU.add, rd=(lbv,), wr=(lbv,))
        k.barrier()
    neg_lam = lamv.t[:, 4:5]

    w_in_v = w_in.rearrange("(kc p) n -> p kc n", p=128)
    w_o_v = w_o.rearrange("(kc p) n -> p kc n", p=128)
    w_mk_v = w_mk.rearrange("(kc p) n -> p kc n", p=128)
    w_mv_v = w_mv.rearrange("(kc p) n -> p kc n", p=128)
    w_abc_v = [w.rearrange("(kc p) n -> p kc n", p=128) for w in (w_a, w_b, w_c)]

    def unit_pieces(u):
        if u < NU_BIG or u >= U_OUT:
            if u < NU_BIG:
                src, c0 = w_in_v, 512 * BIG_ORDER[u]
            elif u < U_MK:
                src, c0 = w_o_v, 512 * (u - U_OUT)
            elif u < U_MV:
                src, c0 = w_mk_v, 512 * (u - U_MK)
            else:
                src, c0 = w_mv_v, 512 * (u - U_MV)
            return [(1024 * q, [(src[:, 2 * q:2 * q + 2, c0:c0 + 512], 2, 512)]) for q in range(8)]
        oc = u - U_MERGE
        g = [w_in_v[:, :, 11264 + 2048 * b + 128 * oc: 11264 + 2048 * b + 128 * oc + 128] for b in range(3)]
        wv = [w_abc_v[b][:, :, 128 * oc:128 * oc + 128] for b in range(3)]
        pcs = []
        for b in range(3):
            for q in range(2):
                pcs.append((2048 * b + 1024 * q, [(g[b][:, 8 * q:8 * q + 8, :], 8, 128)]))
        for b in range(3):
            pcs.append((6144 + 1024 * b, [(wv[b], 8, 128)]))
        return pcs

    stg = []
    cast_seq = [DVE, ACT, DVE, ACT, POOL]

    def wconvert(u):
        slot = W[st["wi"] % 2]
        st["wi"] += 1
        last = {}
        for (off, parts) in unit_pieces(u):
            sg = stg[st["si"] % len(stg)]
            E = cast_seq[st["si"] % len(cast_seq)]
            st["si"] += 1
            o = 0
            for (src, a, b) in parts:
                k.dma(SP, sg.t[:, o:o + a * b].rearrange("p (a b) -> p a b", b=b), src, wr=(sg,), sbuf=sg)
                o += a * b
            k._deps(E, (), (slot,))
            last[E] = k.op(E, (lambda E=E, off=off, o=o, sg=sg: (nc.scalar.copy(out=slot.t[:, off:off + o], in_=sg.t[:, 0:o])
                                                               if E is ACT else
                                                               E.eng.tensor_copy(out=slot.t[:, off:off + o], in_=sg.t[:, 0:o]))),
                           rd=(sg,), wr=())
        slot.w = list(last.values())
        slot.r = {}
        nel = USZ if U_MERGE <= u < U_OUT else 8192
        k.dma(ACT, WS[u, :, 0:nel], slot.t[:, 0:nel], rd=(slot,), wr=(ws_b[u],), sbuf=slot)
        return slot

    def wload(u):
        slot = W[st["wi"] % 2]
        st["wi"] += 1
        nel = USZ if U_MERGE <= u < U_OUT else 8192
        k.dma(SP, slot.t[:, 0:nel], WS[u, :, 0:nel], rd=(ws_b[u],), wr=(slot,), sbuf=slot)
        return slot

    class WStream:
        def __init__(self, order, jit=False):
            self.order = list(order)
            self.pos = 0
            self.loaded = []
            self.jit = jit
            self.bg = None
            self.ncall = 0

        def _load(self):
            u = self.order[self.pos]
            self.loaded.append(wconvert(u) if self.jit else wload(u))
            self.pos += 1

        def prefetch(self):
            pass

        def get(self):
            if not self.loaded:
                self._load()
            s = self.loaded.pop(0)
            if self.pos < len(self.order):
                self._load()
            if self.bg is not None:
                self.ncall += 1
                if self.ncall % 2 == 0:
                    self.bg()
            return s

    def evac(E, out_ap, in_ap, rd, wr):
        k.copy(E, out_ap, in_ap, rd=rd, wr=wr)

    def build_xT(x_src, subs, pes, nbuf=2):
        xst = [k.sb("xst%d" % i, [128, D], F32, pes) for i in range(nbuf)]
        for si_, (t0, n) in enumerate(subs):
            xb = xst[si_ % nbuf]
            k.dma(SP, xb.t[0:n, :], x_src[t0:t0 + n, :], wr=(xb,), sbuf=xb)
            for g in range(4):
                bk = getbank()
                for i in range(4):
                    kc = 4 * g + i
                    k.tr(bk.t[:, i * 128:i * 128 + n], xb.t[0:n, kc * 128:(kc + 1) * 128], ident.t[0:n, 0:n],
                         rd=(xb, ident), wr=(bk,), sig=(i == 3))
                E = rr_eng([ACT, DVE])
                evac(E, xT.t[:, 4 * g:4 * g + 4, t0:t0 + n],
                     bk.t[:, :].rearrange("p (a b) -> p a b", b=128)[:, :, 0:n], rd=(bk,), wr=(xT,))

    def proj_tm(slot, sub, dst_bank):
        t0, n = sub
        wv = slot.t[:, 0:8192].rearrange("p (kc c) -> p kc c", c=512)
        for kc in range(16):
            k.mm(dst_bank.t[0:n, :], xT.t[:, kc, t0:t0 + n], wv[:, kc, :], kc == 0, kc == 15,
                 rd=(xT, slot), wr=(dst_bank,), sig=(kc == 15))

    def proj_fm(slot, cb, ntok, dst_bank):
        wv = slot.t[:, 0:8192].rearrange("p (kc c) -> p kc c", c=512)
        for kc in range(16):
            k.mm(dst_bank.t[:, 0:ntok], wv[:, kc, cb * 128:(cb + 1) * 128], xT.t[:, kc, 0:ntok], kc == 0, kc == 15,
                 rd=(xT, slot), wr=(dst_bank,), sig=(kc == 15))

    def rope_apply(tm, n, S):
        v = tm.t[0:n, :].rearrange("p (m d) -> p m d", d=64)
        x1, x2 = v[:, :, 0:8], v[:, :, 8:16]
        cs = rope.t[0:n, S, 0:8].unsqueeze(1).to_broadcast([n, 16, 8])
        sn_ = rope.t[0:n, S, 8:16].unsqueeze(1).to_broadcast([n, 16, 8])
        return x1, x2, cs, sn_

    def phase_a(ws, subs, ntok, rope_slots, k_out, v_out, store_kt, store_v, pes):
        tm = [k.sb("tm%d" % i, [128, 1024], F32, pes) for i in range(len(subs))]
        rt = [k.sb("ropet%d" % i, [128, 16, 8], F32, pes) for i in range(4)]
        ktst = k.sb("ktst", [128, 8, T], BF16, pes)
        vst = k.sb("vst", [128, 4, 8, 130], BF16, pes)
        k.memset(POOL, vst.t[:, :, :, 128:129], 1.0, wr=(vst,))
        k.memset(POOL, vst.t[:, :, :, 129:130], 0.0, wr=(vst,))
        for part in range(3):
            for half in range(2):
                ws.prefetch()
                slot = ws.get()
                ws.prefetch()
                for si_, sub in enumerate(subs):
                    bk = getbank()
                    proj_tm(slot, sub, bk)
                    n = sub[1]
                    E = rr_eng([ACT, DVE])
                    evac(E, tm[si_].t[0:n, half * 512:(half + 1) * 512], bk.t[0:n, :], rd=(bk,), wr=(tm[si_],))
            for si_, (t0, n) in enumerate(subs):
                tmb = tm[si_]
                if part < 2:
                    x1, x2, cs, sn_ = rope_apply(tmb, n, rope_slots[si_])
                    a, b_, c, d_ = [r.t[0:n] for r in rt]
                    rtb = tuple(rt)
                    k.tt(DVE, a, x1, cs, ALU.mult, rd=(tmb, rope), wr=(rt[0],))
                    k.tt(DVE, b_, x2, sn_, ALU.mult, rd=(tmb, rope), wr=(rt[1],))
                    k.tt(DVE, c, x2, cs, ALU.mult, rd=(tmb, rope), wr=(rt[2],))
                    k.tt(DVE, d_, x1, sn_, ALU.mult, rd=(tmb, rope), wr=(rt[3],))
                    k.tt(DVE, x1, a, b_, ALU.subtract, rd=(rt[0], rt[1]), wr=(tmb,))
                    k.tt(DVE, x2, c, d_, ALU.add, rd=(rt[2], rt[3]), wr=(tmb,))
                    if part == 1:
                        k.dma(ACT, k_out[t0:t0 + n, :], tmb.t[0:n, :], rd=(tmb,), sbuf=tmb, is_out=True)
                    for g in range(2):
                        bk = getbank()
                        for i in range(4):
                            h = 4 * g + i
                            k.tr(bk.t[:, i * 128:i * 128 + n], tmb.t[0:n, h * 128:(h + 1) * 128], ident.t[0:n, 0:n],
                                 rd=(tmb, ident), wr=(bk,), sig=(i == 3))
                        bv = bk.t[:, :].rearrange("p (a b) -> p a b", b=128)
                        if part == 0:
                            evac(ACT, qTz[0].t[0:64, 4 * g:4 * g + 4, t0:t0 + n], bv[0:64, :, 0:n], rd=(bk,), wr=(qTz[0],))
                            evac(DVE, qTz[1].t[64:128, 4 * g:4 * g + 4, t0:t0 + n], bv[64:128, :, 0:n], rd=(bk,), wr=(qTz[1],))
                        else:
                            E = rr_eng([ACT, DVE])
                            evac(E, ktst.t[:, 4 * g:4 * g + 4, t0:t0 + n], bv[:, :, 0:n], rd=(bk,), wr=(ktst,))
                else:
                    k.dma(ACT, v_out[t0:t0 + n, :], tmb.t[0:n, :], rd=(tmb,), sbuf=tmb, is_out=True)
                    E = rr_eng([ACT, DVE])
                    evac(E, vst.t[0:n, si_, :, 0:128], tmb.t[0:n, :].rearrange("p (h e) -> p h e", e=128),
                         rd=(tmb,), wr=(vst,))
            if part == 1:
                store_kt(ktst)
            if part == 2:
                store_v(vst)

    AX = mybir.AxisListType.X

    def group_store(dsts_srcs, src_buf, dst_bufs):
        ev = None
        for (dst, src) in dsts_srcs:
            ev = k.dma(ACT, dst, src, rd=(src_buf,), wr=(), sbuf=src_buf)
        for b in dst_bufs:
            for sem, val in list(b.r.items()):
                pass
            b.w = ev
            b.r = {}

    def rstd_from(ss_ap, buf, scale, n):
        k.ts(DVE, ss_ap, ss_ap, scale, EPS, ALU.mult, ALU.add, rd=(buf,), wr=(buf,))
        k.tt(POOL, ss_ap, ss_ap, m05.t[0:n, :], ALU.pow, rd=(buf, m05), wr=(buf,))

    def phase_z(ws, ntok):
        for half in range(2):
            ws.prefetch()
            slot = ws.get()
            ws.prefetch()
            for cb in range(4):
                bk = getbank()
                proj_fm(slot, cb, ntok, bk)
                k.actf(zaTh[0].t[:, half * 4 + cb, 0:ntok], bk.t[:, 0:ntok], AF.Silu, rd=(bk,), wr=(zaTh[0],))

    def attn_post(O, nq, m, h, qcol0, o1b, o2b, onb, stb):
        if m == 0:
            k.op(DVE, lambda: nc.vector.reciprocal(out=stb.t[0:nq, 0:1], in_=O.t[0:nq, 128:129]), rd=(O,), wr=(stb,))
            k.ts(DVE, o1b.t[0:nq, :], O.t[0:nq, 0:128], stb.t[0:nq, 0:1], None, ALU.mult, None, rd=(O, stb), wr=(o1b,))
            return
        k.op(DVE, lambda: nc.vector.reciprocal(out=stb.t[0:nq, 1:2], in_=O.t[0:nq, 128:129]), rd=(O,), wr=(stb,))
        k.tt(DVE, stb.t[0:nq, 1:2], stb.t[0:nq, 1:2], neg_lam[0:nq, :], ALU.mult, rd=(stb, lamv), wr=(stb,))
        k.stt(o2b.t[0:nq, :], O.t[0:nq, 0:128], stb.t[0:nq, 1:2], o1b.t[0:nq, :], ALU.mult, ALU.add,
              rd=(O, stb, o1b), wr=(o2b,))
        k.actf(onb.t[0:nq, :], o2b.t[0:nq, :], AF.Square, rd=(o2b,), wr=(onb, stb), accum=stb.t[0:nq, 2:3])
        rstd_from(stb.t[0:nq, 2:3], stb, 1.0 / 128.0, nq)
        k.stt(onb.t[0:nq, :], o2b.t[0:nq, :], stb.t[0:nq, 2:3], sn.t[0:nq, :], ALU.mult, ALU.mult,
              rd=(o2b, stb, sn), wr=(onb,))
        tb = getbank()
        k.tr(tb.t[:, 0:nq], onb.t[0:nq, :], ident.t[0:nq, 0:nq], rd=(onb, ident), wr=(tb,))
        k.tt(DVE, yaT.t[:, h, qcol0:qcol0 + nq], tb.t[:, 0:nq], zaTh[0].t[:, h, qcol0:qcol0 + nq], ALU.mult,
             rd=(tb, zaTh[0]), wr=(yaT,))

    def attn_prompt(j, pes):
        nk = 4 * (j + 1)
        nfull = 4 * j
        KTh = [k.sb("KTh%d" % i, [128, SEQ], BF16, pes) for i in range(2)]
        Vh = [k.sb("Vh%d" % i, [128, 32, 130], BF16, pes) for i in range(2)]
        PT = [k.sb("PT%d" % i, [128, 2, 512], BF16, pes) for i in range(3)]
        o1b = [k.sb("o1b%d" % i, [128, 128], F32, pes) for i in range(4)]
        o2b = [k.sb("o2b%d" % i, [128, 128], F32, pes) for i in range(2)]
        onb = [k.sb("onb%d" % i, [128, 128], F32, pes) for i in range(2)]
        stb = [k.sb("stb%d" % i, [128, 4], F32, pes) for i in range(4)]
        st["brot"] = [0, 1, 2, 3]
        ob = banks[4:8]

        def load(h):
            k.dma(SP, KTh[h % 2].t[:, 0:nk * 128], KTP_v[h, :, 0:nk * 128], rd=(ktp_b[h],), wr=(KTh[h % 2],), sbuf=KTh[h % 2])
            k.dma(SP, Vh[h % 2].t[:, 0:nk, :], VP_v[h, :, 0:nk, :], rd=(vp_b[h],), wr=(Vh[h % 2],), sbuf=Vh[h % 2])

        items = [(kt, kt + 1) for kt in range(0, nfull, 2)] + [(kt,) for kt in range(nfull, nk)]
        flat = [(h, m, it, idx == len(items) - 1) for h in range(8) for m in range(2) for idx, it in enumerate(items)]
        load(0)
        load(1)
        pend = []
        pi = 0

        def emit_pv(ent):
            h, m, item_, last, d_, pt_ = ent
            V = Vh[h % 2]
            for ii, kt_ in enumerate(item_):
                for qs in range(d_, 4):
                    k.mm(ob[qs].t[:, 0:130], pt_.t[:, ii, (qs - d_) * 128:(qs - d_ + 1) * 128], V.t[:, kt_, :],
                         kt_ == 0, kt_ == nfull + qs, rd=(pt_, V), wr=(ob[qs],),
                         sig=(qs == 3 and ii == len(item_) - 1))
            if last:
                for qs in range(4):
                    attn_post(ob[qs], 128, m, h, qs * 128, o1b[qs], o2b[qs % 2], onb[qs % 2], stb[qs])
                if m == 1 and h + 2 < 8:
                    load(h + 2)

        for (h, m, item, last) in flat:
            KT = KTh[h % 2]
            b0 = 2 * (pi % 2)
            pt = PT[pi % 3]
            pi += 1
            d = max(0, item[0] - nfull)
            q0 = 128 * d
            N = 512 - q0
            bks = [banks[b0 + ii] for ii in range(len(item))]
            for ii, kt in enumerate(item):
                k.mm(bks[ii].t[:, 0:N], KT.t[:, kt * 128:(kt + 1) * 128], qTz[m].t[:, h, q0:512], True, True,
                     rd=(KT, qTz[m]), wr=(bks[ii],), sig=True)
            if len(item) == 2:
                k.actf(pt.t[:, :, :], PS3[:, b0:b0 + 2, :], AF.Exp, rd=tuple(bks), wr=(pt,), scale=0.125)
            else:
                k.actf(pt.t[:, 0, 0:N], bks[0].t[:, 0:N], AF.Exp, rd=tuple(bks), wr=(pt,), scale=0.125)
                k.memset(POOL, pt.t[64:128, 0, 0:64], 0.0, wr=(pt,))
            pend.append((h, m, item, last, d, pt))
            if len(pend) > 2:
                emit_pv(pend.pop(0))
        while pend:
            emit_pv(pend.pop(0))
        st["brot"] = list(range(8))

    def hgrn_prep(h, hh, ntok, csz, qh, ff, tmp, rm, qdT, kdT, qs_writer, ksf, decay):
        nch = ntok // csz
        mid = (csz - 1) // 2
        tg, tk, tb_, t1, e1, e3 = [t_.t[:, 0:ntok] for t_ in tmp[0:6]]
        Tg, Tk, Tb, T1, E1b, E3b = tmp[0:6]
        k.actf(tg, ff.t[:, hh, 0:ntok], AF.Ln, rd=(ff,), wr=(Tg,))
        k.ts(DVE, tk, ff.t[:, hh, 0:ntok], -1.0, 1.0, ALU.mult, ALU.add, rd=(ff,), wr=(Tk,))
        k.op(DVE, lambda: nc.vector.tensor_tensor_scan(out=tb_, data0=rm.t[:, 0:ntok], data1=tg, initial=0.0,
                                                       op0=ALU.mult, op1=ALU.add), rd=(rm, Tg), wr=(Tb,))
        b3 = tb_.rearrange("p (c t) -> p c t", t=csz)
        k.tt(DVE, t1.rearrange("p (c t) -> p c t", t=csz), b3, b3[:, :, mid:mid + 1].to_broadcast([128, nch, csz]),
             ALU.subtract, rd=(Tb,), wr=(T1,))
        k.actf(e1, t1, AF.Exp, rd=(T1,), wr=(E1b,))
        k.tt(POOL, qdT.t[:, hh, 0:ntok], qh.t[:, hh, 0:ntok], e1, ALU.mult, rd=(qh, E1b), wr=(qdT,))
        k.actf(e1, t1, AF.Exp, rd=(T1,), wr=(E1b,), scale=-1.0)
        k.tt(POOL, kdT.t[:, hh, 0:ntok], tk, e1, ALU.mult, rd=(Tk, E1b), wr=(kdT,))
        k.actf(e3, tb_, AF.Exp, rd=(Tb,), wr=(E3b,))
        qs_writer(hh, qh.t[:, hh, 0:ntok], e3, qh, E3b)
        k.copy(POOL, decay.t[:, hh, 0:nch], e3.rearrange("p (c t) -> p c t", t=csz)[:, :, csz - 1], rd=(E3b,), wr=(decay,))
        k.tt(DVE, t1.rearrange("p (c t) -> p c t", t=csz), b3[:, :, csz - 1:csz].to_broadcast([128, nch, csz]), b3,
             ALU.subtract, rd=(Tb,), wr=(T1,))
        k.actf(e1, t1, AF.Exp, rd=(T1,), wr=(E1b,))
        k.tt(DVE, ksf.t[:, 0:ntok], tk, e1, ALU.mult, rd=(Tk, E1b), wr=(ksf,))

    def ob_post1(obank, c0, nq, h, onb, stb, si):
        k.actf(onb.t[0:nq, :], obank.t[0:nq, c0:c0 + 128], AF.Square, rd=(obank,), wr=(onb, stb), accum=stb.t[0:nq, si:si + 1])
        rstd_from(stb.t[0:nq, si:si + 1], stb, 1.0 / 128.0, nq)
        k.stt(onb.t[0:nq, :], obank.t[0:nq, c0:c0 + 128], stb.t[0:nq, si:si + 1], gain.t[0:nq, h * 128:(h + 1) * 128],
              ALU.mult, ALU.mult, rd=(obank, stb, gain), wr=(onb,))

    def ob_post2(nq, h, gz_ap, gz_buf, qcol0, onb):
        tb = getbank()
        k.tr(tb.t[:, 0:nq], onb.t[0:nq, :], ident.t[0:nq, 0:nq], rd=(onb, ident), wr=(tb,))
        k.tt(DVE, ybT.t[:, h, qcol0:qcol0 + nq], tb.t[:, 0:nq], gz_ap, ALU.mult, rd=(tb, gz_buf), wr=(ybT,))

    def ob_post(obank, c0, nq, h, gz_ap, gz_buf, qcol0, onb, stb, si):
        ob_post1(obank, c0, nq, h, onb, stb, si)
        ob_post2(nq, h, gz_ap, gz_buf, qcol0, onb)

    def hgrn_proj(ws, g, ntok, subs, qh, ff, vB, gz, og, tmp, preps):
        preps = list(preps)
        for which in range(5):
            ws.prefetch()
            slot = ws.get()
            ws.prefetch()
            if which == 2:
                for si_, sub in enumerate(subs):
                    bk = getbank()
                    proj_tm(slot, sub, bk)
                    evac(rr_eng([ACT, DVE]), vB.t[0:sub[1], si_, :], bk.t[0:sub[1], :], rd=(bk,), wr=(vB,))
                    if preps:
                        preps.pop(0)()
                while preps:
                    preps.pop(0)()
                continue
            for hh in range(4):
                h = 4 * g + hh
                bk = getbank()
                proj_fm(slot, hh, ntok, bk)
                src = bk.t[:, 0:ntok]
                if which == 0:
                    k.actf(qh.t[:, hh, 0:ntok], src, AF.Silu, rd=(bk,), wr=(qh,))
                elif which == 1:
                    k.actf(ff.t[:, hh, 0:ntok], src, AF.Sigmoid, rd=(bk,), wr=(ff,))
                    k.ts(DVE, ff.t[:, hh, 0:ntok], ff.t[:, hh, 0:ntok], lbv.t[:, 1, h:h + 1], lbv.t[:, 0, h:h + 1],
                         ALU.mult, ALU.add, rd=(ff, lbv), wr=(ff,))
                elif which == 3:
                    k.actf(og.t[:, hh, 0:ntok], src, AF.Sigmoid, rd=(bk,), wr=(og,))
                else:
                    k.actf(tmp[5].t[:, 0:ntok], src, AF.Silu, rd=(bk,), wr=(tmp[5],))
                    k.tt(DVE, og.t[:, hh, 0:ntok], tmp[5].t[:, 0:ntok], og.t[:, hh, 0:ntok], ALU.mult,
                         rd=(tmp[5], og), wr=(og,))

    def phase_b_prompt(ws, j):
        for g in range(2):
            with ExitStack() as pes:
                qh = k.sb("qh", [128, 4, T], F32, pes)
                ff = k.sb("ff", [128, 4, T], F32, pes)
                vB = k.sb("vB", [128, 4, 512], BF16, pes)
                gz = None
                og = k.sb("og", [128, 4, T], BF16, pes)
                qdT = k.sb("qdT", [128, 4, T], BF16, pes)
                kdT = k.sb("kdT", [128, 4, T], BF16, pes)
                qsE = k.sb("qsE", [128, 4, 4, 128], BF16, pes)
                qsO = k.sb("qsO", [128, 4, 4, 128], BF16, pes)
                ks_tm = k.sb("ks_tm", [128, 4, 4, 128], BF16, pes)
                decay = k.sb("decay", [128, 4, 8], F32, pes)
                tmp = [k.sb("htmp%d" % i, [128, T], F32, pes) for i in range(6)]
                scTs = [k.sb("scT%d" % i, [128, 4, 128], BF16, pes) for i in range(2)]
                onb = [k.sb("onbB%d" % i, [128, 128], F32, pes) for i in range(8)]
                stbs = [k.sb("stbB%d" % i, [128, 2], F32, pes) for i in range(8)]
                for s_ in scTs:
                    k.memset(POOL, s_.t[:], 0.0, wr=(s_,))
                k.memset(POOL, qsE.t[:], 0.0, wr=(qsE,))
                k.memset(POOL, qsO.t[:], 0.0, wr=(qsO,))
                ksf = [k.sb("ksf%d" % i, [128, T], F32, pes) for i in range(4)]

                def qs_writer(hh, q_ap, e3_ap, qbuf, ebuf):
                    qv = q_ap.rearrange("p (s two t) -> p s two t", two=2, t=64)
                    ev_ = e3_ap.rearrange("p (s two t) -> p s two t", two=2, t=64)
                    k.tt(DVE, qsE.t[:, hh, :, 0:64], qv[:, :, 0, :], ev_[:, :, 0, :], ALU.mult, rd=(qbuf, ebuf), wr=(qsE,))
                    k.tt(DVE, qsO.t[:, hh, :, 64:128], qv[:, :, 1, :], ev_[:, :, 1, :], ALU.mult, rd=(qbuf, ebuf), wr=(qsO,))

                def ks_writer(hh, Tg):
                    bk = getbank()
                    for s_ in range(4):
                        k.tr(bk.t[:, s_ * 128:(s_ + 1) * 128], Tg.t[:, s_ * 128:(s_ + 1) * 128], ident.t[:, :],
                             rd=(Tg, ident), wr=(bk,), sig=(s_ == 3))
                    evac(rr_eng([ACT, DVE]), ks_tm.t[:, hh, :, :], bk.t[:, :].rearrange("p (s d) -> p s d", d=128), rd=(bk,), wr=(ks_tm,))

                preps = [lambda hh=hh: hgrn_prep(4 * g + hh, hh, T, 64, qh, ff, tmp, rmask, qdT, kdT, qs_writer, ksf[hh], decay)
                         for hh in range(4)]
                hgrn_proj(ws, g, T, subs4, qh, ff, vB, gz, og, tmp, preps)
                for hh in range(4):
                    ks_writer(hh, ksf[hh])

                def post2(cp_):
                    cs2 = slice(cp_ * 128, (cp_ + 1) * 128)
                    for hh in range(4):
                        ob_post2(128, 4 * g + hh, og.t[:, hh, cs2], og, cp_ * 128, onb[(cp_ % 2) * 4 + hh])

                for cp in range(4):
                    cs_ = slice(cp * 128, (cp + 1) * 128)
                    scb = getbank()
                    for hh in range(4):
                        k.mm(scb.t[:, hh * 128:(hh + 1) * 128], kdT.t[:, hh, cs_], qdT.t[:, hh, cs_], True, True,
                             rd=(kdT, qdT), wr=(scb,), sig=(hh == 3))
                    scT = scTs[cp % 2]
                    scv = scb.t[:, :].rearrange("p (h t) -> p h t", t=128)
                    k.tt(DVE, scT.t[0:64, :, 0:64], scv[0:64, :, 0:64], tri.t[0:64, :].unsqueeze(1).to_broadcast([64, 4, 64]),
                         ALU.mult, rd=(scb, tri), wr=(scT,))
                    k.tt(DVE, scT.t[64:128, :, 64:128], scv[64:128, :, 64:128],
                         tri.t[64:128, :].unsqueeze(1).to_broadcast([64, 4, 64]), ALU.mult, rd=(scb, tri), wr=(scT,))
                    for par in range(2):
                        ps_ = slice(par * 64, (par + 1) * 64)
                        dsb = getbank()
                        for hh in range(4):
                            k.mm(dsb.t[:, hh * 128:(hh + 1) * 128], ks_tm.t[ps_, hh, cp, :], vB.t[ps_, cp, hh * 128:(hh + 1) * 128],
                                 True, True, rd=(ks_tm, vB), wr=(dsb,), sig=(hh == 3))
                        if par == 0 and cp > 0:
                            post2(cp - 1)
                        if par == 1:
                            obk = getbank()
                            for hh in range(4):
                                h = 4 * g + hh
                                oc_ = slice(hh * 128, (hh + 1) * 128)
                                k.mm(obk.t[:, oc_], scT.t[:, hh, :], vB.t[:, cp, oc_], True, False, rd=(scT, vB), wr=(obk,), sig=False)
                                k.mm(obk.t[:, oc_], qsE.t[:, hh, cp, :], Sbf[h][1].t[:], False, False, rd=(qsE, Sbf[h][1]), wr=(obk,), sig=False)
                                k.mm(obk.t[:, oc_], qsO.t[:, hh, cp, :], Sbf[h][0].t[:], False, True, rd=(qsO, Sbf[h][0]), wr=(obk,), sig=True)
                        for hh in range(4):
                            h = 4 * g + hh
                            k.stt(Sst[h].t[:], Sst[h].t[:], decay.t[:, hh, 2 * cp + par:2 * cp + par + 1],
                                  dsb.t[:, hh * 128:(hh + 1) * 128], ALU.mult, ALU.add, rd=(Sst[h], decay, dsb), wr=(Sst[h],))
                            k.copy(rr_eng([ACT, POOL]), Sbf[h][par].t[:], Sst[h].t[:], rd=(Sst[h],), wr=(Sbf[h][par],))
                    for hh in range(4):
                        ob_post1(obk, hh * 128, 128, 4 * g + hh, onb[(cp % 2) * 4 + hh], stbs[(cp % 2) * 4 + hh], 0)
                post2(3)
                k.barrier()

    def phase_c(ws, ntok, groups, pes):
        qcT = k.sb("qcT", [128, 8, ntok], BF16, pes)
        zcs = k.sb("zcs", [128, 8, ntok], BF16, pes)
        PTm = [k.sb("PTm%d" % i, [128, 2, ntok], BF16, pes) for i in range(2)]
        rs = [k.sb("rsC%d" % i, [128, ntok], F32, pes) for i in range(2)]
        tc_ = [k.sb("tmpC%d" % i, [128, ntok], F32, pes) for i in range(2)]
        for which in range(2):
            for half in range(2):
                ws.prefetch()
                slot = ws.get()
                ws.prefetch()
                for cb in range(4):
                    bk = getbank()
                    proj_fm(slot, cb, ntok, bk)
                    if which == 0:
                        evac(rr_eng([ACT, DVE]), qcT.t[:, half * 4 + cb, 0:ntok], bk.t[:, 0:ntok], rd=(bk,), wr=(qcT,))
                    else:
                        k.actf(zcs.t[:, half * 4 + cb, 0:ntok], bk.t[:, 0:ntok], AF.Silu, rd=(bk,), wr=(zcs,))
        it = 0
        for (c0, n, mkb, mvb_) in groups:
            cs_ = slice(c0, c0 + n)
            for h in range(4):
                pt = PTm[it % 2]
                r_ = rs[it % 2]
                it += 1
                for mt in range(2):
                    sbk = getbank()
                    for half in range(2):
                        k.mm(sbk.t[:, 0:n], mkb.t[:, 2 * h + half, mt * 128:(mt + 1) * 128], qcT.t[:, 2 * h + half, cs_],
                             half == 0, half == 1, rd=(mkb, qcT), wr=(sbk,), sig=(half == 1))
                    k.actf(pt.t[:, mt, 0:n], sbk.t[:, 0:n], AF.Exp, rd=(sbk,), wr=(pt,), scale=1.0 / 16.0)
                smb = getbank()
                for mt in range(2):
                    k.mm(smb.t[:, 0:n], ones_bf.t[:, :], pt.t[:, mt, 0:n], mt == 0, mt == 1, rd=(ones_bf, pt), wr=(smb,), sig=(mt == 1))
                k.op(DVE, lambda: nc.vector.reciprocal(out=r_.t[:, 0:n], in_=smb.t[:, 0:n]), rd=(smb,), wr=(r_,))
                for eh in range(2):
                    obk = getbank()
                    ch = 2 * h + eh
                    for mt in range(2):
                        k.mm(obk.t[:, 0:n], mvb_.t[:, mt, ch * 128:(ch + 1) * 128], pt.t[:, mt, 0:n], mt == 0, mt == 1,
                             rd=(mvb_, pt), wr=(obk,), sig=(mt == 1))
                    t_ = tc_[eh]
                    k.tt(DVE, t_.t[:, 0:n], obk.t[:, 0:n], r_.t[:, 0:n], ALU.mult, rd=(obk, r_), wr=(t_,))
                    k.tt(POOL, ycT.t[:, ch, cs_], t_.t[:, 0:n], zcs.t[:, ch, cs_], ALU.mult, rd=(t_, zcs), wr=(ycT,))

    def phase_merge_out(ws, ntok, subs, x_src, y_dst, pes, next_x=None):
        mergedT = k.sb("mergedT", [128, 16, T], BF16, pes)
        mes = ExitStack()
        xr = [k.sb("xr%d" % i, [128, D], F32, pes) for i in range(len(subs))]
        stats = [k.sb("lnstat%d" % i, [128, 4, 6], F32, pes) for i in range(len(subs))]
        mvs = [k.sb("lnmv%d" % i, [128, 4], F32, pes) for i in range(len(subs))]
        gam = k.sb("gam", [128, D], F32, pes)
        bet = k.sb("bet", [128, D], F32, pes)
        k.dma(SP, gam.t[:], lng[0:1, :].to_broadcast([128, D]), wr=(gam,), sbuf=gam)
        k.dma(SP, bet.t[:], lnb[0:1, :].to_broadcast([128, D]), wr=(bet,), sbuf=bet)
        sg = [k.sb("sgM%d" % i, [128, T], F32, mes) for i in range(2)]
        acc = [k.sb("accM%d" % i, [128, T], F32, mes) for i in range(2)]
        tmpm = [k.sb("tmpM%d" % i, [128, T], F32, mes) for i in range(2)]
        for si_, (t0, n) in enumerate(subs):
            k.dma(SP, xr[si_].t[0:n, :], x_src[t0:t0 + n, :], wr=(xr[si_],), sbuf=xr[si_])
        yTs = (yaT, ybT, ycT)
        for oc in range(16):
            ws.prefetch()
            slot = ws.get()
            ws.prefetch()
            a_ = acc[oc % 2]
            for br in range(3):
                gb = getbank()
                gv = slot.t[:, br * 2048:(br + 1) * 2048].rearrange("p (kc c) -> p kc c", c=128)
                for kc in range(16):
                    k.mm(gb.t[:, 0:ntok], gv[:, kc, :], xT.t[:, kc, 0:ntok], kc == 0, kc == 15, rd=(slot, xT), wr=(gb,), sig=(kc == 15))
                yb_ = getbank()
                wv = slot.t[:, 6144 + br * 1024:6144 + (br + 1) * 1024].rearrange("p (kc c) -> p kc c", c=128)
                for kc in range(8):
                    k.mm(yb_.t[:, 0:ntok], wv[:, kc, :], yTs[br].t[:, kc, 0:ntok], kc == 0, kc == 7, rd=(slot, yTs[br]), wr=(yb_,), sig=(kc == 7))
                s_ = sg[(3 * oc + br) % 2]
                k.actf(s_.t[:, 0:ntok], gb.t[:, 0:ntok], AF.Sigmoid, rd=(gb,), wr=(s_,))
                if br == 0:
                    k.tt(DVE, a_.t[:, 0:ntok], s_.t[:, 0:ntok], yb_.t[:, 0:ntok], ALU.mult, rd=(s_, yb_), wr=(a_,))
                else:
                    t_ = tmpm[br - 1]
                    k.tt(DVE, t_.t[:, 0:ntok], s_.t[:, 0:ntok], yb_.t[:, 0:ntok], ALU.mult, rd=(s_, yb_), wr=(t_,))
                    if br == 1:
                        k.tt(POOL, a_.t[:, 0:ntok], a_.t[:, 0:ntok], t_.t[:, 0:ntok], ALU.add, rd=(a_, t_), wr=(a_,))
                    else:
                        k.tt(POOL, mergedT.t[:, oc, 0:ntok], a_.t[:, 0:ntok], t_.t[:, 0:ntok], ALU.add, rd=(a_, t_), wr=(mergedT,))
        k._deps(SP, (), tuple(sg + acc + tmpm))
        mes.close()
        for cb in range(4):
            ws.prefetch()
            slot = ws.get()
            ws.prefetch()
            wv = slot.t[:, 0:8192].rearrange("p (kc c) -> p kc c", c=512)
            for si_, (t0, n) in enumerate(subs):
                bk = getbank()
                for kc in range(16):
                    k.mm(bk.t[0:n, :], mergedT.t[:, kc, t0:t0 + n], wv[:, kc, :], kc == 0, kc == 15, rd=(mergedT, slot), wr=(bk,), sig=(kc == 15))
                xc = xr[si_].t[0:n, cb * 512:(cb + 1) * 512]
                k.stt(xc, xc, ALPHA, bk.t[0:n, :], ALU.mult, ALU.add, rd=(xr[si_], bk), wr=(xr[si_],))
        for si_, (t0, n) in enumerate(subs):
            xb = xr[si_]
            stat, mv_ = stats[si_], mvs[si_]
            for c in range(4):
                k.op(DVE, lambda c=c: nc.vector.bn_stats(out=stat.t[0:n, c, :], in_=xb.t[0:n, c * 512:(c + 1) * 512]), rd=(xb,), wr=(stat,))
            k.op(DVE, lambda: nc.vector.bn_aggr(out=mv_.t[0:n, 0:2], in_=stat.t[0:n, :, :].rearrange("p a b -> p (a b)")), rd=(stat,), wr=(mv_,))
            k.ts(DVE, mv_.t[0:n, 2:3], mv_.t[0:n, 1:2], EPS, None, ALU.add, None, rd=(mv_,), wr=(mv_,))
            k.tt(POOL, mv_.t[0:n, 2:3], mv_.t[0:n, 2:3], m05.t[0:n, :], ALU.pow, rd=(mv_, m05), wr=(mv_,))
            k.ts(DVE, mv_.t[0:n, 3:4], mv_.t[0:n, 0:1], mv_.t[0:n, 2:3], -1.0, ALU.mult, ALU.mult, rd=(mv_,), wr=(mv_,))
            k.actf(xb.t[0:n, :], xb.t[0:n, :], AF.Identity, rd=(xb, mv_), wr=(xb,), scale=mv_.t[0:n, 2:3], bias=mv_.t[0:n, 3:4])
            k.tt(DVE, xb.t[0:n, :], xb.t[0:n, :], gam.t[0:n, :], ALU.mult, rd=(xb, gam), wr=(xb,))
            k.tt(POOL, xb.t[0:n, :], xb.t[0:n, :], bet.t[0:n, :], ALU.add, rd=(xb, bet), wr=(xb,))
            k.dma(ACT, y_dst[t0:t0 + n, :], xb.t[0:n, :], rd=(xb,), sbuf=xb, is_out=True)
        if next_x is not None:
            build_xT(next_x, subs4, pes, nbuf=1)

    def mem_transposes(src_bufs, n_sub, dstT):
        for s_ in range(n_sub):
            for g in range(2):
                bk = getbank()
                for i in range(4):
                    ch = 4 * g + i
                    k.tr(bk.t[:, i * 128:(i + 1) * 128], src_bufs[s_].t[:, ch * 128:(ch + 1) * 128], ident.t[:, :],
                         rd=(src_bufs[s_], ident), wr=(bk,), sig=(i == 3))
                evac(rr_eng([ACT, DVE]), dstT.t[:, 4 * g:4 * g + 4, s_ * 128:(s_ + 1) * 128],
                     bk.t[:, :].rearrange("p (a b) -> p a b", b=128), rd=(bk,), wr=(dstT,))

    def phase_mem_prompt():
        with ExitStack() as pes:
            subs2 = [(0, 128), (128, 128)]
            build_xT(mem, subs2, pes)
            ws = WStream([U_MK, U_MK + 1, U_MV, U_MV + 1], jit=True)
            mtm = [k.sb("mtm%d" % i, [128, 1024], F32, pes) for i in range(2)]
            for which in range(2):
                for half in range(2):
                    ws.prefetch()
                    slot = ws.get()
                    ws.prefetch()
                    for si_, sub in enumerate(subs2):
                        bk = getbank()
                        proj_tm(slot, sub, bk)
                        evac(rr_eng([ACT, DVE]), mtm[si_].t[:, half * 512:(half + 1) * 512], bk.t[:, :], rd=(bk,), wr=(mtm[si_],))
                for si_, (t0, n) in enumerate(subs2):
                    dst = mk_p if which == 0 else mv_p
                    k.dma(ACT, dst[t0:t0 + n, :], mtm[si_].t[:, :], rd=(mtm[si_],), sbuf=mtm[si_], is_out=True)
                if which == 0:
                    mem_transposes(mtm, 2, mkT)
                else:
                    for si_ in range(2):
                        evac(rr_eng([ACT, DVE]), mvb.t[:, si_, :], mtm[si_].t[:, :], rd=(mtm[si_],), wr=(mvb,))
            k.barrier()

    KTS_v = KTS
    VS_v = VS.rearrange("i p (s e) -> i p s e", e=130)
    subs1 = [(0, 64)]

    def phase_s_cache():
        with ExitStack() as pes:
            ckst = [[k.sb("ckst%d_%d" % (S, i), [128, 1024], F32, pes) for i in range(2)] for S in range(2)]
            cvst = [[k.sb("cvst%d_%d" % (S, i), [128, 1024], F32, pes) for i in range(2)] for S in range(2)]
            ktst = [k.sb("ktstS%d" % S, [128, 8, 256], BF16, pes) for S in range(2)]
            vst = [k.sb("vstS%d" % S, [128, 2, 8, 130], BF16, pes) for S in range(2)]
            for S in range(2):
                k.memset(POOL, vst[S].t[:, :, :, 128:129], 1.0, wr=(vst[S],))
                k.memset(POOL, vst[S].t[:, :, :, 129:130], 0.0, wr=(vst[S],))
            its = [(b, jj) for b in range(4) for jj in range(16)]

            def loads(it):
                b, jj = its[it]
                S = it % 2
                for s_ in range(2):
                    r0 = jj * 256 + s_ * 128
                    k.dma(SP, ckst[S][s_].t[:, :], ck[b, r0:r0 + 128, :], wr=(ckst[S][s_],), sbuf=ckst[S][s_])
                    k.dma(SP, cvst[S][s_].t[:, :], cv[b, r0:r0 + 128, :], wr=(cvst[S][s_],), sbuf=cvst[S][s_])

            loads(0)
            for it, (b, jj) in enumerate(its):
                if it + 1 < len(its):
                    loads(it + 1)
                S = it % 2
                for s_ in range(2):
                    for g in range(2):
                        bk = getbank()
                        for i in range(4):
                            h = 4 * g + i
                            k.tr(bk.t[:, i * 128:(i + 1) * 128], ckst[S][s_].t[:, h * 128:(h + 1) * 128], ident.t[:, :],
                                 rd=(ckst[S][s_], ident), wr=(bk,), sig=(i == 3))
                        evac(rr_eng([ACT, DVE]), ktst[S].t[:, 4 * g:4 * g + 4, s_ * 128:(s_ + 1) * 128],
                             bk.t[:, :].rearrange("p (a b) -> p a b", b=128), rd=(bk,), wr=(ktst[S],))
                    evac(rr_eng([POOL, DVE]), vst[S].t[:, s_, :, 0:128], cvst[S][s_].t[:, :].rearrange("p (h e) -> p h e", e=128),
                         rd=(cvst[S][s_],), wr=(vst[S],))
                group_store([(KTS_v[b * 8 + h, :, jj * 256:(jj + 1) * 256], ktst[S].t[:, h, :]) for h in range(8)],
                            ktst[S], [kts_b[b * 8 + h] for h in range(8)])
                group_store([(VS_v[b * 8 + h, :, 2 * jj:2 * jj + 2, :], vst[S].t[:, :, h, :]) for h in range(8)],
                            vst[S], [vs_b[b * 8 + h] for h in range(8)])
            k.barrier()

    cache = {"step": 0, "sets": None}
    NSTEP = 4 * 32

    def cache_alloc(es_):
        sets = []
        for S in range(2):
            d = dict(ck=k.sb("cck%d" % S, [128, 1024], F32, es_), cv=k.sb("ccv%d" % S, [128, 1024], F32, es_),
                     kt=k.sb("ckt%d" % S, [128, 8, 128], BF16, es_), v=k.sb("cvv%d" % S, [128, 8, 130], BF16, es_))
            k.memset(POOL, d["v"].t[:, :, 128:129], 1.0, wr=(d["v"],))
            k.memset(POOL, d["v"].t[:, :, 129:130], 0.0, wr=(d["v"],))
            sets.append(d)
        cache["sets"] = sets
        k.local_bufs = [b for b in k.local_bufs if all(b is not x for d in sets for x in d.values())]

    def cache_loads(s_):
        if s_ >= NSTEP:
            return
        b, r = divmod(s_, 32)
        d = cache["sets"][s_ % 2]
        k.dma(SP, d["ck"].t[:, :], ck[b, r * 128:(r + 1) * 128, :], wr=(d["ck"],), sbuf=d["ck"])
        k.dma(SP, d["cv"].t[:, :], cv[b, r * 128:(r + 1) * 128, :], wr=(d["cv"],), sbuf=d["cv"])

    def cache_step():
        s_ = cache["step"]
        if s_ >= NSTEP or cache["sets"] is None:
            return
        if s_ == 0:
            cache_loads(0)
        cache_loads(s_ + 1)
        b, r = divmod(s_, 32)
        d = cache["sets"][s_ % 2]
        for g in range(2):
            bk = getbank()
            for i in range(4):
                h = 4 * g + i
                k.tr(bk.t[:, i * 128:(i + 1) * 128], d["ck"].t[:, h * 128:(h + 1) * 128], ident.t[:, :],
                     rd=(d["ck"], ident), wr=(bk,), sig=(i == 3))
            evac(DVE, d["kt"].t[:, 4 * g:4 * g + 4, :], bk.t[:, :].rearrange("p (a b) -> p a b", b=128), rd=(bk,), wr=(d["kt"],))
        evac(POOL, d["v"].t[:, :, 0:128], d["cv"].t[:, :].rearrange("p (h e) -> p h e", e=128), rd=(d["cv"],), wr=(d["v"],))
        hb = [kts_b[b * 8 + h] for h in range(8)]
        ev = k.dma(SP, KTS_v[b * 8:(b + 1) * 8, :, r * 128:(r + 1) * 128].rearrange("h p t -> p h t"), d["kt"].t[:, :, :],
                   rd=(d["kt"],), wr=(), sbuf=d["kt"])
        for x in hb:
            x.w = ev
            x.r = {}
        vb_ = [vs_b[b * 8 + h] for h in range(8)]
        ev = k.dma(SP, VS_v[b * 8:(b + 1) * 8, :, r, :].rearrange("h p e -> p h e"), d["v"].t[:, :, :],
                   rd=(d["v"],), wr=(), sbuf=d["v"])
        for x in vb_:
            x.w = ev
            x.r = {}
        cache["step"] = s_ + 1

    def attn_sample(pes):
        KTh = [k.sb("KThS%d" % i, [128, SEQ + 16], BF16, pes) for i in range(2)]
        Vh = [k.sb("VhS%d" % i, [128, 33, 130], BF16, pes) for i in range(2)]
        PT = [k.sb("PTS%d" % i, [128, 512], BF16, pes) for i in range(2)]
        PTt = [k.sb("PTtS%d" % i, [16, 16], BF16, pes) for i in range(2)]
        o1b = [k.sb("o1bS%d" % i, [128, 128], F32, pes) for i in range(2)]
        o2b = [k.sb("o2bS%d" % i, [128, 128], F32, pes) for i in range(2)]
        onb = [k.sb("onbS%d" % i, [128, 128], F32, pes) for i in range(2)]
        stb = [k.sb("stbS%d" % i, [128, 4], F32, pes) for i in range(2)]

        def load(i):
            k.dma(SP, KTh[i % 2].t[:, :], KTS_v[i, :, :], rd=(kts_b[i],), wr=(KTh[i % 2],), sbuf=KTh[i % 2])
            k.dma(SP, Vh[i % 2].t[:, :, :], VS_v[i, :, :, :], rd=(vs_b[i],), wr=(Vh[i % 2],), sbuf=Vh[i % 2])

        load(0)
        pi = 0
        for i in range(32):
            b, h = i // 8, i % 8
            if i + 1 < 32:
                load(i + 1)
            KT, V = KTh[i % 2], Vh[i % 2]
            qc = slice(b * 16, (b + 1) * 16)
            for m in range(2):
                ms = slice(m * 64, (m + 1) * 64)
                pt, ptt = PT[pi % 2], PTt[pi % 2]
                pi += 1
                sbk = getbank()
                for kt in range(32):
                    k.mm(sbk.t[:, kt * 16:(kt + 1) * 16], KT.t[:, kt * 128:(kt + 1) * 128], qTz[m].t[:, h, qc], True, True,
                         rd=(KT, qTz[m]), wr=(sbk,), sig=(kt == 31))
                k.actf(pt.t[:, :], sbk.t[:, :], AF.Exp, rd=(sbk,), wr=(pt,), scale=0.125)
                sbt = getbank()
                k.mm(sbt.t[0:16, 0:16], KT.t[:, SEQ:SEQ + 16], qTz[m].t[:, h, qc], True, True, rd=(KT, qTz[m]), wr=(sbt,), sig=True)
                k.actf(ptt.t[:, :], sbt.t[0:16, 0:16], AF.Exp, rd=(sbt,), wr=(ptt,), scale=0.125)
                O = getbank()
                for kt in range(32):
                    k.mm(O.t[0:16, 0:130], pt.t[:, kt * 16:(kt + 1) * 16], V.t[:, kt, :], kt == 0, False, rd=(pt, V), wr=(O,), sig=False)
                k.mm(O.t[0:16, 0:130], ptt.t[:, :], V.t[0:16, 32, :], False, True, rd=(ptt, V), wr=(O,), sig=True)
                attn_post(O, 16, m, h, b * 16, o1b[i % 2], o2b[i % 2], onb[i % 2], stb[i % 2])

    def phase_b_sample(ws):
        for g in range(2):
            with ExitStack() as pes:
                qh = k.sb("qhS", [128, 4, 64], F32, pes)
                ff = k.sb("ffS", [128, 4, 64], F32, pes)
                vB = k.sb("vBS", [128, 1, 512], BF16, pes)
                vBm = k.sb("vBmS", [64, 4, 512], BF16, pes)
                gz = None
                og = k.sb("ogS", [128, 4, 64], BF16, pes)
                qdT = k.sb("qdTS", [128, 4, 64], BF16, pes)
                kdT = k.sb("kdTS", [128, 4, 64], BF16, pes)
                qsZ = k.sb("qsZS", [128, 4, 4, 64], BF16, pes)
                ks_tm = k.sb("ks_tmS", [64, 4, 128], BF16, pes)
                decay = k.sb("decayS", [128, 4, 4], F32, pes)
                tmp = [k.sb("htmpS%d" % i, [128, 64], F32, pes) for i in range(6)]
                scT = [k.sb("scTS%d" % i, [64, 64], BF16, pes) for i in range(2)]
                onb = [k.sb("onbBS%d" % i, [128, 128], F32, pes) for i in range(2)]
                stb = k.sb("stbBS", [128, 8], F32, pes)
                S0f = [k.sb("S0f%d" % i, [128, 128], F32, pes) for i in range(8)]
                S0b = [k.sb("S0b%d" % i, [128, 128], BF16, pes) for i in range(8)]
                k.memset(POOL, qsZ.t[:], 0.0, wr=(qsZ,))
                ksf = [k.sb("ksfS%d" % i, [128, 64], F32, pes) for i in range(4)]
                for b in []:
                    k.ts(DVE, vBm.t[0:64, b, :], vB.t[0:64, 0, :], rowmask.t[0:64, b:b + 1], None, ALU.mult, None,
                         rd=(vB, rowmask), wr=(vBm,))

                def qs_writer(hh, q_ap, e3_ap, qbuf, ebuf):
                    for b in range(4):
                        c_ = slice(b * 16, (b + 1) * 16)
                        k.tt(DVE, qsZ.t[:, hh, b, c_], q_ap[:, c_], e3_ap[:, c_], ALU.mult, rd=(qbuf, ebuf), wr=(qsZ,))

                def ks_writer(hh, Tg):
                    bk = getbank()
                    k.tr(bk.t[0:64, 0:128], Tg.t[:, 0:64], ident.t[:, :], rd=(Tg, ident), wr=(bk,))
                    evac(rr_eng([ACT, DVE]), ks_tm.t[0:64, hh, :], bk.t[0:64, 0:128], rd=(bk,), wr=(ks_tm,))

                preps = [lambda hh=hh: hgrn_prep(4 * g + hh, hh, 64, 16, qh, ff, tmp, rmask16, qdT, kdT, qs_writer, ksf[hh], decay)
                         for hh in range(4)]
                hgrn_proj(ws, g, 64, subs1, qh, ff, vB, gz, og, tmp, preps)
                for b in range(4):
                    k.ts(DVE, vBm.t[0:64, b, :], vB.t[0:64, 0, :], rowmask.t[0:64, b:b + 1], None, ALU.mult, None,
                         rd=(vB, rowmask), wr=(vBm,))
                for hh in range(4):
                    ks_writer(hh, ksf[hh])
                for hh in range(4):
                    h = 4 * g + hh
                    hc = slice(hh * 128, (hh + 1) * 128)
                    for b in range(4):
                        sf, sbb = S0f[(hh % 2) * 4 + b], S0b[(hh % 2) * 4 + b]
                        k.dma(SP, sf.t[:, :], s0[b * 8 + h], wr=(sf,), sbuf=sf)
                        evac(rr_eng([ACT, POOL]), sbb.t[:, :], sf.t[:, :], rd=(sf,), wr=(sbb,))
                    scb = getbank()
                    k.mm(scb.t[0:64, 0:64], kdT.t[:, hh, 0:64], qdT.t[:, hh, 0:64], True, True, rd=(kdT, qdT), wr=(scb,), sig=True)
                    sc_ = scT[hh % 2]
                    k.tt(DVE, sc_.t[:, :], scb.t[0:64, 0:64], tri16.t[:, :], ALU.mult, rd=(scb, tri16), wr=(sc_,))
                    obk = getbank()
                    k.mm(obk.t[0:64, 0:128], sc_.t[:, :], vB.t[0:64, 0, hc], True, False, rd=(sc_, vB), wr=(obk,), sig=False)
                    for b in range(4):
                        sbb = S0b[(hh % 2) * 4 + b]
                        k.mm(obk.t[0:64, 0:128], qsZ.t[:, hh, b, :], sbb.t[:, :], False, b == 3, rd=(qsZ, sbb), wr=(obk,), sig=(b == 3))
                    dsb = getbank()
                    for b in range(4):
                        k.mm(dsb.t[:, b * 128:(b + 1) * 128], ks_tm.t[0:64, hh, :], vBm.t[0:64, b, hc], True, True,
                             rd=(ks_tm, vBm), wr=(dsb,), sig=(b == 3))
                    for b in range(4):
                        sf = S0f[(hh % 2) * 4 + b]
                        k.stt(sf.t[:, :], sf.t[:, :], decay.t[:, hh, b:b + 1], dsb.t[:, b * 128:(b + 1) * 128], ALU.mult, ALU.add,
                              rd=(sf, decay, dsb), wr=(sf,))
                        k.dma(ACT, hg_s[b * 8 + h], sf.t[:, :], rd=(sf,), sbuf=sf, is_out=True)
                    ob_post(obk, 0, 64, h, og.t[:, hh, 0:64], og, 0, onb[hh % 2], stb, hh)
                k.barrier()

    def run_sample():
        ws = WStream(TILE_UNITS)
        outer = ExitStack()
        alloc_q_za(outer)
        with ExitStack() as pes:
            build_xT(xs, subs1, pes)

            def store_kt(ktst):
                group_store([(KTS_v[b * 8 + h, :, SEQ:SEQ + 16], ktst.t[:, h, b * 16:(b + 1) * 16])
                             for b in range(4) for h in range(8)], ktst, kts_b)

            def store_v(vst):
                group_store([(VS_v[b * 8 + h, 0:16, 32, :], vst.t[b * 16:(b + 1) * 16, 0, h, :])
                             for b in range(4) for h in range(8)], vst, vs_b)

            with nc.allow_non_contiguous_dma(reason="small per-sequence K^T column writes"):
                phase_a(ws, subs1, 64, [32], k_s, v_s, store_kt, store_v, pes)
            phase_z(ws, 64)
            k.barrier()
        with ExitStack() as pes:
            attn_sample(pes)
            k.barrier()
        outer.close()
        phase_b_sample(ws)
        with ExitStack() as pes:
            mkTs = [k.sb("mkTs%d" % b, [128, 8, 256], BF16, pes) for b in range(4)]
            mvbs = [k.sb("mvbs%d" % b, [128, 2, 1024], BF16, pes) for b in range(4)]
            mst = [k.sb("mst%d" % i, [128, 1024], F32, pes) for i in range(4)]
            for b in range(4):
                for s_ in range(2):
                    k.dma(SP, mst[s_].t[:, :], cmk[b, s_ * 128:(s_ + 1) * 128, :], wr=(mst[s_],), sbuf=mst[s_])
                    k.dma(SP, mst[2 + s_].t[:, :], cmv[b, s_ * 128:(s_ + 1) * 128, :], wr=(mst[2 + s_],), sbuf=mst[2 + s_])
                mem_transposes(mst[0:2], 2, mkTs[b])
                for s_ in range(2):
                    evac(rr_eng([ACT, DVE]), mvbs[b].t[:, s_, :], mst[2 + s_].t[:, :], rd=(mst[2 + s_],), wr=(mvbs[b],))
            phase_c(ws, 64, [(b * 16, 16, mkTs[b], mvbs[b]) for b in range(4)], pes)
            k.barrier()
        with ExitStack() as pes:
            phase_merge_out(ws, 64, subs1, xs, y_s, pes)
            k.barrier()

    def alloc_q_za(pes):
        for m in range(2):
            qTz[m] = k.sb("qTz%d" % m, [128, 8, T], BF16, pes)
        zaTh[0] = k.sb("zaT", [128, 8, T], BF16, pes)
        k.memset(POOL, qTz[0].t[64:128, :, :], 0.0, wr=(qTz[0],))
        k.memset(POOL, qTz[1].t[0:64, :, :], 0.0, wr=(qTz[1],))

    KTP_v = KTP
    VP_v = VP.rearrange("h p (s e) -> h p s e", e=130)
    subs4 = [(i * 128, 128) for i in range(4)]
    TILE_UNITS = list(range(0, 42))

    for h in range(8):
        k.memset(POOL, Sst[h].t[:], 0.0, wr=(Sst[h],))
        for p_ in range(2):
            k.memset(POOL, Sbf[h][p_].t[:], 0.0, wr=(Sbf[h][p_],))

    es_stg = ExitStack()
    for i in range(4):
        stg.append(k.sb("stg%d" % i, [128, 1024], F32, es_stg))
    k.local_bufs = [b for b in k.local_bufs if all(b is not x for x in stg)]
    es_cache = ExitStack()
    phase_mem_prompt()
    if not JIT_TILE0:
        for u in range(0, 42):
            wconvert(u)
        k.barrier()

    NT_RUN = NT
    for j in range(NT_RUN):
        if j == 1:
            es_stg.close()
            with nc.allow_non_contiguous_dma(reason="per-head scatter of converted cache tiles"):
                cache_alloc(es_cache)
        ws = WStream(TILE_UNITS, jit=(j == 0 and JIT_TILE0))
        if j >= 1:
            ws.bg = cache_step
        outer = ExitStack()
        alloc_q_za(outer)
        with ExitStack() as pes:
            if j == 0:
                build_xT(xp[j * T:(j + 1) * T, :], subs4, pes)

            def store_kt(ktst, j=j):
                group_store([(KTP_v[h, :, j * T:(j + 1) * T], ktst.t[:, h, 0:T]) for h in range(8)], ktst, ktp_b)

            def store_v(vst, j=j):
                group_store([(VP_v[h, :, 4 * j:4 * j + 4, :], vst.t[:, :, h, :]) for h in range(8)], vst, vp_b)

            phase_a(ws, subs4, T, [4 * j + i for i in range(4)],
                    k_p[j * T:(j + 1) * T, :], v_p[j * T:(j + 1) * T, :], store_kt, store_v, pes)
            phase_z(ws, T)
            k.barrier()
        with ExitStack() as pes:
            attn_prompt(j, pes)
            k.barrier()
        outer.close()
        phase_b_prompt(ws, j)
        with ExitStack() as pes:
            phase_c(ws, T, [(0, T, mkT, mvb)], pes)
            k.barrier()
        with ExitStack() as pes:
            phase_merge_out(ws, T, subs4, xp[j * T:(j + 1) * T, :], y_p[j * T:(j + 1) * T, :], pes,
                            next_x=(xp[(j + 1) * T:(j + 2) * T, :] if j + 1 < NT_RUN else None))
            k.barrier()
    for h in range(8):
        k.dma(ACT, hg_p[h], Sst[h].t[:], rd=(Sst[h],), sbuf=Sst[h], is_out=True)
    while cache["step"] < NSTEP:
        cache_step()
    k.barrier()
    es_cache.close()
    if RUN_SAMPLE:
        run_sample()

    k.finish()
    es.close()
    return nc


_CACHE = {}


def _consts():
    p = np.arange(128)
    inv = np.power(500000.0, -np.arange(0, 16, 2, dtype=np.float32) / 16.0).astype(np.float32)
    rope = np.zeros((128, 33, 16), np.float32)
    for S in range(33):
        pos = (128 * S + p) if S < 32 else (4096 + (p % 16))
        ang = pos.astype(np.float32)[:, None] * inv[None, :]
        rope[:, S, 0:8] = np.cos(ang)
        rope[:, S, 8:16] = np.sin(ang)
    t = np.arange(64)
    tri = ((p[:, None] % 64) <= t[None, :]).astype(np.float32)
    rmask = (np.arange(512) % 64 != 0).astype(np.float32)[None, :].repeat(128, 0)
    rmask16 = (np.arange(64) % 16 != 0).astype(np.float32)[None, :].repeat(128, 0)
    s = np.arange(64)
    tri16 = ((s[:, None] // 16 == s[None, :] // 16) & (s[:, None] <= s[None, :])).astype(np.float32)
    rowmask = (s[:, None] // 16 == np.arange(4)[None, :]).astype(np.float32)
    return {
        "c_ident": np.eye(128, dtype=np.float32),
        "c_rope": np.ascontiguousarray(rope.reshape(128, 33 * 16)),
        "c_tri": np.ascontiguousarray(tri),
        "c_rmask": np.ascontiguousarray(rmask),
        "c_rmask16": np.ascontiguousarray(rmask16),
        "c_tri16": np.ascontiguousarray(tri16),
        "c_rowmask": np.ascontiguousarray(rowmask),
    }


def kernel(x_prompt, x_sample, cache_attn_k, cache_attn_v, state_hgrn, cache_mem_k, cache_mem_v,
           mem_prompt, w_in, lambda_q1, lambda_k1, lambda_q2, lambda_k2, attn_sub_norm,
           hgrn_lb_logits, hgrn_norm, w_mem_k, w_mem_v, w_branch_a, w_branch_b, w_branch_c,
           w_out, ln_gamma, ln_beta):
    f = lambda a: np.ascontiguousarray(np.asarray(a, dtype=np.float32))
    if "nc" not in _CACHE:
        _CACHE["nc"] = build_program()
    nc = _CACHE["nc"]
    consts = _consts()
    shared = {
        "w_in": f(w_in)[0], "w_mk": f(w_mem_k)[0], "w_mv": f(w_mem_v)[0],
        "w_a": f(w_branch_a)[0], "w_b": f(w_branch_b)[0], "w_c": f(w_branch_c)[0], "w_o": f(w_out)[0],
        "lam4": np.ascontiguousarray(np.concatenate([f(lambda_q1), f(lambda_k1), f(lambda_q2), f(lambda_k2)], 0)),
        "subn": f(attn_sub_norm), "lbl": f(hgrn_lb_logits), "hgn": f(hgrn_norm),
        "lng": f(ln_gamma), "lnb": f(ln_beta),
    }
    shared.update(consts)
    xp_, xs_ = f(x_prompt), f(x_sample)
    ck_, cv_ = f(cache_attn_k)[0], f(cache_attn_v)[0]
    s0_, cmk_, cmv_, mem_ = f(state_hgrn)[0], f(cache_mem_k)[0], f(cache_mem_v)[0], f(mem_prompt)
    in_maps = []
    for c in range(8):
        m = dict(shared)
        m["xp"] = xp_[c]
        m["xs"] = xs_[4 * c:4 * c + 4].reshape(64, D)
        m["ck"] = ck_[4 * c:4 * c + 4].reshape(4, SEQ, 1024)
        m["cv"] = cv_[4 * c:4 * c + 4].reshape(4, SEQ, 1024)
        m["s0"] = s0_[4 * c:4 * c + 4].reshape(32, 128, 128)
        m["cmk"] = cmk_[4 * c:4 * c + 4].reshape(4, 256, 1024)
        m["cmv"] = cmv_[4 * c:4 * c + 4].reshape(4, 256, 1024)
        m["mem"] = mem_[c]
        in_maps.append(m)
    res = run_bass_kernel_spmd(nc, in_maps, core_ids=list(range(8)))
    R = res.results
    cat = lambda name: np.stack([np.asarray(R[c][name]) for c in range(8)], 0)
    y_prompt = cat("y_p")
    y_sample = cat("y_s").reshape(32, 16, D)
    k_prompt = cat("k_p").reshape(1, 8, SEQ, 8, 2, 64)
    v_prompt = cat("v_p").reshape(1, 8, SEQ, 8, 128)
    hgrn_prompt = cat("hg_p").reshape(1, 8, 8, 128, 128)
    mem_k_prompt = cat("mk_p").reshape(1, 8, 256, 4, 256)
    mem_v_prompt = cat("mv_p").reshape(1, 8, 256, 4, 256)
    k_sample = cat("k_s").reshape(1, 32, 16, 8, 2, 64)
    v_sample = cat("v_s").reshape(1, 32, 16, 8, 128)
    hgrn_sample = cat("hg_s").reshape(1, 32, 8, 128, 128)
    return (y_prompt, y_sample, k_prompt, v_prompt, hgrn_prompt, mem_k_prompt, mem_v_prompt,
            k_sample, v_sample, hgrn_sample)
```
